# Optimizing a Trainium2 kernel written in Bass

```python
import jax, jax.numpy as jnp
from jax import lax
import numpy as np

D_MODEL = 2048
BATCH = 4
SEQ = 2048
DEPTH = 1

CHUNK = 64
HGRN_WIDTH = D_MODEL // 2
HGRN_HEAD_DIM = 128
HGRN_HEADS = HGRN_WIDTH // HGRN_HEAD_DIM
GMLP_WIDTH = D_MODEL // 2
GMLP_CHUNK = 128
GMLP_HEADS = 8
GMLP_HEAD_DIM = GMLP_WIDTH // GMLP_HEADS
D_FF = 4 * D_MODEL
PLE_DIM = 256
EPS = 1e-6

kernel_name = "hybrid_hgrn2_gmlp_gated_block"


def rmsnorm(x, g):
    xf = x.astype(jnp.float32)
    y = xf * lax.rsqrt(jnp.mean(xf * xf, axis=-1, keepdims=True) + EPS)
    return (y * g.astype(jnp.float32)).astype(x.dtype)


def layernorm(x, g, b):
    xf = x.astype(jnp.float32)
    mu = jnp.mean(xf, axis=-1, keepdims=True)
    var = jnp.mean(jnp.square(xf - mu), axis=-1, keepdims=True)
    y = (xf - mu) * lax.rsqrt(var + EPS)
    return (y * g.astype(jnp.float32) + b.astype(jnp.float32)).astype(x.dtype)


def hgrn2_scan(q, k, v, logf):
    B, S, H, d = q.shape
    nC = S // CHUNK

    def to_chunks(a):
        return a.reshape(B, nC, CHUNK, H, d).transpose(1, 0, 3, 2, 4)

    xs = (to_chunks(q), to_chunks(k), to_chunks(v), to_chunks(logf))
    causal = jnp.tril(jnp.ones((CHUNK, CHUNK), dtype=bool))[None, None, :, :, None]

    def step(state, inp):
        qc, kc, vc, gc = inp
        b = jnp.cumsum(gc, axis=2)
        inter = jnp.einsum('bhtd,bhde->bhte', qc * jnp.exp(b), state)
        diff = b[:, :, :, None, :] - b[:, :, None, :, :]
        decay = jnp.exp(jnp.where(causal, diff, -jnp.inf))
        attn = jnp.einsum('bhtd,bhsd,bhtsd->bhts', qc, kc, decay)
        intra = jnp.einsum('bhts,bhse->bhte', attn, vc)
        b_last = b[:, :, -1:, :]
        new_state = (jnp.exp(b_last[:, :, 0, :])[..., None] * state
                     + jnp.einsum('bhsd,bhse->bhde', kc * jnp.exp(b_last - b), vc))
        return new_state, inter + intra

    state0 = jnp.zeros((B, H, d, d), jnp.float32)
    _, out = lax.scan(step, state0, xs)
    return out.transpose(1, 0, 3, 2, 4).reshape(B, S, H, d)


def gmlp_spatial(v, w_s, b_s):
    B, S, _ = v.shape
    nG = S // GMLP_CHUNK
    pos = jnp.arange(GMLP_CHUNK)
    mask = (pos[None, :] // CHUNK) <= (pos[:, None] // CHUNK)
    w_eff = jnp.where(mask[None], w_s, jnp.zeros_like(w_s))
    vg = v.reshape(B, nG, GMLP_CHUNK, GMLP_HEADS, GMLP_HEAD_DIM)
    sv = jnp.einsum('hts,bgshc->bgthc', w_eff, vg) + b_s.T[:, :, None]
    return sv.reshape(B, S, GMLP_WIDTH)


def setup_inputs(seed: int = 0) -> dict:
    key = jax.random.key(seed)
    ks = jax.random.split(key, 24)
    f32 = jnp.float32
    n_in = 4 * HGRN_WIDTH + 2 * GMLP_WIDTH + 2 * D_MODEL

    def nrm(k, shape, fan_in):
        return jax.random.normal(k, shape, f32) * (fan_in ** -0.5)

    def gain(k, shape):
        return 1.0 + 0.02 * jax.random.normal(k, shape, f32)

    return {
        "x": jax.random.normal(ks[0], (BATCH, SEQ, D_MODEL), f32),
        "p": jax.random.normal(ks[1], (DEPTH, BATCH, SEQ, PLE_DIM), f32),
        "norm_mix": gain(ks[2], (DEPTH, D_MODEL)),
        "w_in": nrm(ks[3], (DEPTH, D_MODEL, n_in), D_MODEL),
        "lb_logits": 0.1 * jax.random.normal(ks[4], (DEPTH + 1, HGRN_WIDTH), f32),
        "hgrn_norm": gain(ks[5], (DEPTH, HGRN_HEAD_DIM)),
        "w_a_out": nrm(ks[6], (DEPTH, HGRN_WIDTH, D_MODEL), HGRN_WIDTH),
        "gmlp_ln_g": gain(ks[7], (DEPTH, GMLP_WIDTH)),
        "gmlp_ln_b": 0.02 * jax.random.normal(ks[8], (DEPTH, GMLP_WIDTH), f32),
        "w_spatial": nrm(ks[9], (DEPTH, GMLP_HEADS, GMLP_CHUNK, GMLP_CHUNK), GMLP_CHUNK),
        "b_spatial": gain(ks[10], (DEPTH, GMLP_HEADS, GMLP_CHUNK)),
        "w_b_out": nrm(ks[11], (DEPTH, GMLP_WIDTH, D_MODEL), GMLP_WIDTH),
        "w_o": nrm(ks[12], (DEPTH, D_MODEL, D_MODEL), D_MODEL),
        "norm_ffn": gain(ks[13], (DEPTH, D_MODEL)),
        "w_ff1": nrm(ks[14], (DEPTH, D_MODEL, D_FF), D_MODEL),
        "w_ff2": nrm(ks[15], (DEPTH, D_FF, D_MODEL), D_FF),
        "norm_ple": gain(ks[16], (DEPTH, D_MODEL)),
        "w_ple_gate": nrm(ks[17], (DEPTH, D_MODEL, D_MODEL), D_MODEL),
        "w_ple_proj": nrm(ks[18], (DEPTH, PLE_DIM, D_MODEL), PLE_DIM),
        "norm_final": gain(ks[19], (D_MODEL,)),
    }


def reference(x, p, norm_mix, w_in, lb_logits, hgrn_norm, w_a_out, gmlp_ln_g, gmlp_ln_b,
              w_spatial, b_spatial, w_b_out, w_o, norm_ffn, w_ff1, w_ff2, norm_ple,
              w_ple_gate, w_ple_proj, norm_final):
    B, S, _ = x.shape
    f32 = jnp.float32
    split_sizes = [HGRN_WIDTH] * 4 + [GMLP_WIDTH] * 2 + [D_MODEL] * 2
    split_idx = [int(s) for s in np.cumsum(split_sizes)[:-1]]
    lower_bounds = jnp.cumsum(jax.nn.softmax(lb_logits.astype(f32), axis=0), axis=0)

    for i in range(DEPTH):
        h = rmsnorm(x, norm_mix[i])
        proj = h @ w_in[i]
        q, f_pre, inp, g, u, v, gate_a, gate_b = jnp.split(proj, split_idx, axis=-1)

        lb = lower_bounds[i]
        f = lb + (1.0 - lb) * jax.nn.sigmoid(f_pre.astype(f32))
        k = 1.0 - f
        heads = lambda a: a.reshape(B, S, HGRN_HEADS, HGRN_HEAD_DIM)
        o = hgrn2_scan(heads(jax.nn.silu(q.astype(f32))), heads(k),
                       heads(inp.astype(f32)), heads(jnp.log(f)))
        o = rmsnorm(o, hgrn_norm[i]).astype(x.dtype) * jax.nn.silu(heads(g))
        y_a = o.reshape(B, S, HGRN_WIDTH) @ w_a_out[i]

        u = jax.nn.gelu(u, approximate=False)
        v = layernorm(jax.nn.gelu(v, approximate=False), gmlp_ln_g[i], gmlp_ln_b[i])
        y_b = (u * gmlp_spatial(v, w_spatial[i], b_spatial[i])) @ w_b_out[i]

        merged = jax.nn.sigmoid(gate_a) * y_a + jax.nn.sigmoid(gate_b) * y_b
        x = x + merged @ w_o[i]

        hf = rmsnorm(x, norm_ffn[i])
        x = x + jnp.square(jax.nn.relu(hf @ w_ff1[i])) @ w_ff2[i]

        gate_p = jax.nn.sigmoid(rmsnorm(x, norm_ple[i]) @ w_ple_gate[i])
        x = x + gate_p * (p[i] @ w_ple_proj[i])

    return rmsnorm(x, norm_final)
```

```python
from contextlib import ExitStack

import numpy as np
import concourse.bass as bass
import concourse.mybir as mybir
from concourse.bass_utils import run_bass_kernel_spmd

F32 = mybir.dt.float32
BF16 = mybir.dt.bfloat16
AF = mybir.ActivationFunctionType
ALU = mybir.AluOpType

D = 2048
T = 1024
NH = 8
EPS = 1e-6
NSLOT = 4
BLK = 4096
ENGS = ["pe", "act", "dve", "pool", "sp"]

C_GMIX, C_GFFN, C_GPLE, C_GFIN, C_L0, C_L1, C_HN, C_ID, C_MASK, NCV = 0, 16, 32, 48, 64, 72, 80, 96, 224, 352


class Chan:
    def __init__(self, sem, step, name):
        self.sem, self.step, self.val, self.name = sem, step, 0, name


class Planner:
    def __init__(self, nc):
        self.nc = nc
        self.ops = {e: [] for e in ENGS}
        self.echan = {}
        self.waited = {e: {} for e in ENGS}
        self.res = {}
        for e in ["pe", "act", "dve", "pool"]:
            self.echan[e] = self.new_chan(1, "c_" + e)

    def new_chan(self, step, name):
        return Chan(self.nc.alloc_semaphore(name=name), step, name)

    def _deps(self, eng, reads, writes):
        deps = {}

        def add(tok, raw):
            if tok is None:
                return
            ch, v = tok
            if ch is self.echan.get(eng) and eng == "pe":
                return
            if deps.get(ch, 0) < v:
                deps[ch] = v

        for k in reads:
            r = self.res.get(k)
            if r is not None:
                add(r[0], True)
        for k in writes:
            r = self.res.get(k)
            if r is not None:
                add(r[0], False)
                for ch, v in r[1].items():
                    add((ch, v), False)
        waits = []
        for ch, v in deps.items():
            if self.waited[eng].get(ch, 0) < v:
                self.waited[eng][ch] = v
                waits.append((ch, v))
        return waits

    def _record(self, tok, reads, writes):
        for k in writes:
            self.res[k] = [tok, {}]
        ch, v = tok
        for k in reads:
            r = self.res.setdefault(k, [None, {}])
            if r[1].get(ch, 0) < v:
                r[1][ch] = v

    def op(self, eng, fn, reads=(), writes=(), signal=True):
        waits = self._deps(eng, reads, writes)
        ch = self.echan[eng]
        if signal:
            ch.val += 1
            tok = (ch, ch.val)
        else:
            tok = (ch, ch.val + 1)
        self.ops[eng].append((waits, fn, (ch, 1) if signal else None))
        self._record(tok, reads, writes)
        return tok

    def dma(self, eng, chan, fn, reads=(), writes=(), after=()):
        waits = self._deps(eng, reads, writes)
        for ch, v in after:
            if self.waited[eng].get(ch, 0) < v:
                self.waited[eng][ch] = v
                waits.append((ch, v))
        chan.val += chan.step
        tok = (chan, chan.val)
        self.ops[eng].append((waits, fn, (chan, chan.step)))
        self._record(tok, reads, writes)
        return tok

    def wait(self, eng, tok):
        ch, v = tok
        if self.waited[eng].get(ch, 0) < v:
            self.waited[eng][ch] = v
            self.ops[eng].append(([(ch, v)], None, None))

    def simulate(self):
        val = {}
        pc = {e: 0 for e in ENGS}
        progress = True
        while progress:
            progress = False
            for e in ENGS:
                ops = self.ops[e]
                while pc[e] < len(ops):
                    waits, fn, inc = ops[pc[e]]
                    if any(val.get(ch.name, 0) < v for ch, v in waits):
                        break
                    if inc is not None:
                        val[inc[0].name] = val.get(inc[0].name, 0) + inc[1]
                    pc[e] += 1
                    progress = True
        stuck = {e: (pc[e], len(self.ops[e])) for e in ENGS if pc[e] < len(self.ops[e])}
        for e, (i, n) in stuck.items():
            waits, fn, inc = self.ops[e][i]
            print("STUCK", e, i, "/", n, [(ch.name, v, val.get(ch.name, 0)) for ch, v in waits])
        return not stuck

    def emit(self, block):
        handles = {"pe": block.tensor, "act": block.scalar, "dve": block.vector,
                   "pool": block.gpsimd, "sp": block.sync}
        for e in ENGS:
            ops = self.ops[e]
            if not ops:
                continue

            def body(engine, ops=ops):
                for waits, fn, inc in ops:
                    for ch, v in waits:
                        engine.wait_ge(ch.sem, v)
                    if fn is None:
                        continue
                    ins = fn(engine)
                    if inc is not None:
                        ins.then_inc(inc[0].sem, inc[1])

            handles[e](body)


def weight_blocks():
    blks = []
    for h in range(NH):
        blks.append(("w_in", 0, D, [(1024 + h * 128, 128), (2048 + h * 128, 128)]))
        blks.append(("w_in", 0, D, [(h * 128, 128), (3072 + h * 128, 128)]))
    for j in range(4):
        blks.append(("w_in", 0, D, [(4096 + j * 256, 256)]))
    for j in range(4):
        blks.append(("w_in", 0, D, [(5120 + j * 256, 256)]))
    for n in range(16):
        blks.append(("w_in", 0, D, [(6144 + n * 128, 128), (8192 + n * 128, 128)]))
        blks.append(("w_ab", 0, 1024, [(n * 128, 128)]))
    for j in range(8):
        blks.append(("w_o", 0, D, [(j * 256, 256)]))
    for q in range(4):
        for j in range(8):
            blks.append(("w_ff1", 0, D, [(q * 2048 + j * 256, 256)]))
        for j in range(8):
            blks.append(("w_ff2", q * 2048, D, [(j * 256, 256)]))
    blks.append(("w_ple_proj", 0, 256, [(0, 2048)]))
    for j in range(8):
        blks.append(("w_ple_gate", 0, D, [(j * 256, 256)]))
    return blks


def pack_weights(ws):
    blks = weight_blocks()
    out = np.zeros((len(blks), 128, BLK), np.float32)
    for i, (name, r0, K, cols) in enumerate(blks):
        kc = K // 128
        if name == "w_ab":
            c0, ncl = cols[0]
            a = ws["w_a_out"][:, c0:c0 + ncl].reshape(kc, 128, ncl).transpose(1, 0, 2).reshape(128, kc * ncl)
            b = ws["w_b_out"][:, c0:c0 + ncl].reshape(kc, 128, ncl).transpose(1, 0, 2).reshape(128, kc * ncl)
            out[i, :, :kc * ncl] = a
            out[i, :, kc * ncl:2 * kc * ncl] = b
            continue
        W = ws[name]
        sub = np.concatenate([W[r0:r0 + K, c0:c0 + ncl] for c0, ncl in cols], axis=1)
        ncl = sub.shape[1]
        out[i, :, :kc * ncl] = sub.reshape(kc, 128, ncl).transpose(1, 0, 2).reshape(128, kc * ncl)
    return out


NBLK = len(weight_blocks())


def build_program(dbg=()):
    nc = bass.Bass("TRN2", target_bir_lowering=False)
    xT_d = nc.dram_tensor("xT", [D, T], F32, kind="ExternalInput").ap()
    xpT_d = nc.dram_tensor("xpT", [D, T], F32, kind="ExternalInput").ap()
    pT_d = nc.dram_tensor("pT", [256, T], F32, kind="ExternalInput").ap()
    wb_d = nc.dram_tensor("wb", [NBLK, 128, BLK], F32, kind="ExternalInput").ap()
    cv_d = nc.dram_tensor("cvec", [128, NCV], F32, kind="ExternalInput").ap()
    ln_d = nc.dram_tensor("lnrow", [3, 1024], F32, kind="ExternalInput").ap()
    wsp_d = nc.dram_tensor("wspT", [128, 1024], F32, kind="ExternalInput").ap()
    out_d = nc.dram_tensor("outT", [D, T], F32, kind="ExternalOutput").ap()
    dbg_d = {}
    for name, shape in dbg:
        dbg_d[name] = nc.dram_tensor("dbg_" + name, list(shape), F32, kind="ExternalOutput").ap()

    es = ExitStack()
    RX = es.enter_context(nc.sbuf_tensor("RX", [128, 16384], F32))
    RH = es.enter_context(nc.sbuf_tensor("RH", [128, 16384], BF16))
    RP = es.enter_context(nc.sbuf_tensor("RP", [128, 16384], BF16))
    RQ = es.enter_context(nc.sbuf_tensor("RQ", [128, 16384], BF16))
    RW = es.enter_context(nc.sbuf_tensor("RW", [128, NSLOT * BLK], BF16))
    RC = es.enter_context(nc.sbuf_tensor("RC", [128, 3200], F32))
    PS = [es.enter_context(nc.psum_tensor(f"ps{i}", [128, 512], F32)) for i in range(8)]
    P = Planner(nc)
    block = es.enter_context(nc.Block())

    def c3(ap, t=1024):
        return ap.rearrange("p (c t) -> p c t", t=t)

    XT = c3(RX[:, :])
    HT = c3(RH[:, :])
    RPv = c3(RP[:, :])
    RQv = c3(RQ[:, :])
    WS = [RW[:, s * BLK:(s + 1) * BLK] for s in range(NSLOT)]
    PT = PS[7][:, :].bitcast(BF16)

    CV = RC[:, 0:NCV]
    RSTD = RC[:, 352:1376]
    SQ = [RC[:, 1376 + i * 512:1376 + (i + 1) * 512].bitcast(BF16) for i in range(2)]
    RMASK = RC[:, 2400:2912].bitcast(BF16)
    ONESB = RC[:, 2912:2976].bitcast(BF16)
    IDB = RC[:, 2976:3040].bitcast(BF16)
    LB = RC[:, 3040:3048]
    OML = RC[:, 3048:3056]
    LBM1 = RC[:, 3056:3064]
    ONESF = RC[:, 3064:3192]
    MISC = RC[:, 3192:3200]
    MASKF = CV[:, C_MASK:C_MASK + 128]

    def rxf(i):
        return RX[:, i * 1024:(i + 1) * 1024]

    SG, KK, BB, DD, EP, QQ = [rxf(i) for i in range(6)]

    def rxb(i):
        return RX[:, 6144 + i * 512:6144 + (i + 1) * 512].bitcast(BF16)

    KT = [rxb(0), rxb(1)]
    QT = [rxb(2), rxb(3)]
    ZG = [rxb(4), rxb(5)]
    KHT = [rxb(6), rxb(7)]
    KH0 = [rxb(8), rxb(9)]
    KH1 = [rxb(10), rxb(11)]
    VTOK = [rxb(12), rxb(13), rxb(14)]
    INPT = rxb(15)
    o0 = 6144 + 16 * 512
    ATS = [RX[:, o0 + i * 64:o0 + (i + 1) * 64].bitcast(BF16) for i in range(2)]
    SF = [RX[:, o0 + 128 + i * 128:o0 + 128 + (i + 1) * 128] for i in range(4)]
    SBFR = [RX[:, o0 + 640 + i * 64:o0 + 640 + (i + 1) * 64].bitcast(BF16) for i in range(16)]
    GAM = [RX[:, o0 + 1664 + i * 16:o0 + 1664 + (i + 1) * 16] for i in range(3)]
    assert o0 + 1664 + 48 <= 16384
    OZ = c3(RQ[:, 0:8192])
    UU = c3(RQ[:, 8192:16384])
    GVT = [rxf(0), rxf(1)]
    GLN, BLN, BSP = rxf(2), rxf(3), rxf(4)
    WSP = RX[:, 5 * 1024:5 * 1024 + 512].bitcast(BF16)
    VTG = c3(RX[:, 6 * 1024:10 * 1024].bitcast(BF16))
    STATS = RX[:, 10 * 1024:10 * 1024 + 64]

    rx_keys = [("RX", c, h) for c in range(16) for h in range(2)]
    A1_KEYS = [("SG", 0), ("SG", 1), "KK", "BB", "DD", "EP", ("QQ", 0), ("QQ", 1), ("INPT", 0), ("INPT", 1)] + \
        [(n, i) for n in ["KT", "QT", "KHT", "KH0", "KH1", "ATS"] for i in range(2)] + \
        [("VTOK", i) for i in range(3)] + [("GAM", i) for i in range(3)] + [("SF", i) for i in range(4)] + \
        [("SBFR", i) for i in range(16)] + [("ZG", i, hf) for i in range(2) for hf in range(2)]
    A2_KEYS = [("GVT", 0), ("GVT", 1), "GLN", "BLN", "BSP", "WSP", "STATS"] + [("VTG", i) for i in range(8)]

    def ACT(out, in_, func, reads, writes, **kw):
        return P.op("act", lambda e: e.activation(out=out, in_=in_, func=func, **kw), reads, writes)

    def DVE(fn, reads, writes):
        return P.op("dve", fn, reads, writes)

    def MM(out, lhsT, rhs, start, stop, reads, writes, signal=None):
        return P.op("pe", lambda e: e.matmul(out, lhsT, rhs, start=start, stop=stop), reads, writes,
                    signal=stop if signal is None else signal)

    def fence(old, new):
        P.op("dve", lambda e: e.memset(MISC[:, 0:1], 0.0), reads=[], writes=list(old) + list(new) + ["MISC0"])

    def spdma(out, in_, reads, writes, name):
        ch = P.new_chan(16, name)
        return P.dma("sp", ch, lambda e: e.dma_start(out=out, in_=in_), reads, writes)

    def pooldma(out, in_, reads, writes, name):
        ch = P.new_chan(16, name)
        return P.dma("pool", ch, lambda e: e.dma_start(out=out, in_=in_), reads, writes)

    wch = [P.new_chan(16, f"w{s}") for s in range(NSLOT)]
    wstate = {"issued": 0, "done": 0, "cur": 0}

    def w_pump(limit=NBLK):
        while wstate["issued"] < min(NBLK, limit) and wstate["issued"] - NSLOT < wstate["done"]:
            j = wstate["issued"]
            s = j % NSLOT
            P.dma("pool", wch[s],
                  lambda e, j=j, s=s: e.dma_start(out=c3(WS[s], 2048), in_=c3(wb_d[j], 2048)),
                  reads=[], writes=[("ws", s)], after=wstate.get("gate", []) if 0 < j < NSLOT else [])
            wstate["issued"] += 1

    def w_next():
        i = wstate["cur"]
        wstate["cur"] += 1
        w_pump()
        assert wstate["issued"] > i, "weight block not issued (too many blocks held)"
        return WS[i % NSLOT], ("ws", i % NSLOT)

    def w_done(n=1):
        wstate["done"] += n
        w_pump()

    def dump(name, ap, reads):
        if name in dbg_d:
            if ap.dtype != F32:
                ap = ap.bitcast(F32)
            t = spdma(dbg_d[name], ap, reads, [], "dbg_" + name)
            final_toks.append(t)

    final_toks = []

    t_cv = spdma(CV, cv_d, [], ["CV"], "cv")
    EPSC = MISC[:, 1:2]
    P.op("dve", lambda e: e.memset(EPSC, EPS), [], ["EPSC"])
    P.op("dve", lambda e: e.memset(ONESB, 1.0), [], ["ONESB"])
    P.op("dve", lambda e: e.memset(ONESF, 1.0), [], ["ONESF"])
    P.op("dve", lambda e: e.memset(RMASK, 1.0), [], ["RMASK"])
    P.op("dve", lambda e: e.memset(RMASK[:, 0::64], 0.0), [], ["RMASK"])
    P.op("dve", lambda e: e.tensor_copy(out=IDB, in_=CV[:, C_ID:C_ID + 128]), ["CV"], ["IDB"])
    P.op("dve", lambda e: e.tensor_tensor(out=LBM1, in0=CV[:, C_L0:C_L0 + 8], in1=CV[:, C_L1:C_L1 + 8], op=ALU.subtract),
         ["CV"], ["LBM1"])
    ACT(LB, LBM1, AF.Sigmoid, ["LBM1"], ["LB"])
    P.op("dve", lambda e: e.tensor_scalar(out=OML, in0=LB, scalar1=-1.0, scalar2=1.0, op0=ALU.mult, op1=ALU.add),
         ["LB"], ["OML"])
    P.op("dve", lambda e: e.tensor_scalar(out=LBM1, in0=LB, scalar1=-1.0, scalar2=None, op0=ALU.add),
         ["LB"], ["LBM1"])

    sqi = {"i": 0}

    def rms_stats(src, src_key, bank0):
        for c in range(16):
            sq = SQ[sqi["i"] % 2]
            sk = ("SQ", sqi["i"] % 2)
            sqi["i"] += 1
            for h in range(2):
                ACT(sq[:, h * 512:(h + 1) * 512], src[:, c, h * 512:(h + 1) * 512], AF.Square,
                    [src_key(c, h)], [sk + (h,)])
            for h in range(2):
                MM(PS[bank0 + h][:, :], ONESB, sq[:, h * 512:(h + 1) * 512], c == 0, c == 15,
                   [sk + (h,), "ONESB"], [("ps", bank0 + h)], signal=True)
        for h in range(2):
            r = RSTD[:, h * 512:(h + 1) * 512]
            ACT(r, PS[bank0 + h][:, :], AF.Ln, [("ps", bank0 + h), "EPSC"], [("RSTD", h)], scale=1.0 / D, bias=EPSC)
            ACT(r, r, AF.Exp, [("RSTD", h)], [("RSTD", h)], scale=-0.5)

    def rms_apply(dst, dst_key, src, src_key, gcol):
        for h in range(2):
            for c in range(16):
                sl = slice(h * 512, (h + 1) * 512)
                DVE(lambda e, c=c, sl=sl: e.scalar_tensor_tensor(
                    out=dst[:, c, sl], in0=src[:, c, sl], scalar=CV[:, gcol + c:gcol + c + 1], in1=RSTD[:, sl],
                    op0=ALU.mult, op1=ALU.mult),
                    [src_key(c, h), ("RSTD", h), "CV"], [dst_key(c, h)])

    kRX = lambda c, h: ("RX", c, h)
    kRH = lambda c, h: ("RH", c, h)
    kRP = lambda c, h: ("RP", c, h)
    kRQ = lambda c, h: ("RQ", c, h)

    def load_x(src_d, tag):
        v = src_d.rearrange("(c p) t -> p c t", p=128)
        for g in range(4):
            spdma(XT[:, g * 4:(g + 1) * 4, :], v[:, g * 4:(g + 1) * 4, :], [],
                  [kRX(c, h) for c in range(g * 4, g * 4 + 4) for h in range(2)], f"x{tag}{g}")

    NB = [c3(RX[:, 0:8192], 512), c3(RX[:, 8192:16384], 512), c3(RQ[:, :].bitcast(F32), 512)]
    nb_keys = [("NB", i, c) for i in range(3) for c in range(16)]
    jobs = [(xpT_d, 0, RPv, kRP), (xpT_d, 1, RPv, kRP), (xT_d, 0, HT, kRH), (xT_d, 1, HT, kRH)]
    def job_load(ji):
        src_d, hf, dst, dkey = jobs[ji]
        buf, bi = NB[ji % 3], ji % 3
        v = src_d.rearrange("(c p) t -> p c t", p=128)
        toks = []
        for g in range(2):
            toks.append(spdma(buf[:, g * 8:(g + 1) * 8, :], v[:, g * 8:(g + 1) * 8, hf * 512:(hf + 1) * 512], [],
                              [("NB", bi, c) for c in range(g * 8, g * 8 + 8)], f"nx{ji}{g}"))
        return toks

    w_pump(1)
    gate = []
    for ji in range(3):
        gate += job_load(ji)
    for ji, (src_d, hf, dst, dkey) in enumerate(jobs):
        buf = NB[ji % 3]
        bi = ji % 3
        if ji == 3:
            gate += job_load(ji)
            wstate["gate"] = gate
            w_pump()
        bank = ji % 4
        for c in range(16):
            qi = sqi["i"] % 4
            sqi["i"] += 1
            sq = SQ[qi // 2][:, (qi % 2) * 512:(qi % 2 + 1) * 512]
            sk = ("SQ", qi // 2, qi % 2)
            ACT(sq, buf[:, c, :], AF.Square, [("NB", bi, c)], [sk])
            MM(PS[bank][:, :], ONESB, sq, c == 0, c == 15, [sk, "ONESB"], [("ps", bank)], signal=True)
        r = RSTD[:, (ji % 2) * 512:(ji % 2 + 1) * 512]
        rk = ("RSTD", ji % 2)
        ACT(r, PS[bank][:, :], AF.Ln, [("ps", bank), "EPSC"], [rk], scale=1.0 / D, bias=EPSC)
        ACT(r, r, AF.Exp, [rk], [rk], scale=-0.5)
        for c in range(16):
            DVE(lambda e, c=c, buf=buf, dst=dst, hf=hf, r=r: e.scalar_tensor_tensor(
                out=dst[:, c, hf * 512:(hf + 1) * 512], in0=buf[:, c, :], scalar=CV[:, C_GMIX + c:C_GMIX + c + 1], in1=r,
                op0=ALU.mult, op1=ALU.mult), [("NB", bi, c), rk, "CV"], [dkey(c, hf)])
    sqi["i"] = 0
    dump("ht", RH[:, :], [kRH(c, h) for c in range(16) for h in range(2)])
    fence(nb_keys, A1_KEYS + [kRQ(c, h) for c in range(16) for h in range(2)])

    units = [(h, s) for h in range(NH) for s in range(2)]
    uw = {}
    R4 = [0, 1, 2, 7]
    rot_i = {"i": 0}

    def rot():
        b = R4[rot_i["i"] % 4]
        rot_i["i"] += 1
        return b

    KVB = [3, 6]
    kvb_i = {"i": 0}
    for i in range(2):
        P.op("dve", lambda e, i=i: e.memset(KH0[i], 0.0), [], [("KH0", i)])
        P.op("dve", lambda e, i=i: e.memset(KH1[i], 0.0), [], [("KH1", i)])

    def proj_gen(ui):
        h, s = units[ui]
        hp, par2, set3 = h % 2, ui % 2, ui % 3
        src, skey = (RPv, kRP) if s == 0 else (HT, kRH)
        if s == 0:
            uw[h] = w_next()
        wa, ka = uw[h]
        wa3 = c3(wa, 256)

        def fm(w3, cs, hf, k, evac):
            b = rot()
            for c in range(16):
                MM(PS[b][:, :], w3[:, c, cs], src[:, c, hf * 512:(hf + 1) * 512], c == 0, c == 15,
                   [k, skey(c, hf)], [("ps", b)])
            evac(b)

        sgk = [("SG", 0), ("SG", 1)]
        bl = BB[:, 63:64]
        bl_bc = bass.AP(bl.tensor, bl.offset, [[bl.ap[0][0], 128], [64, 16], [0, 64]])
        b3 = BB.rearrange("p (c s) -> p c s", s=64)
        d3 = DD.rearrange("p (c s) -> p c s", s=64)

        def S1():
            DVE(lambda e: e.tensor_scalar(out=KK, in0=SG, scalar1=-1.0, scalar2=LBM1[:, h:h + 1], op0=ALU.add, op1=ALU.mult),
                sgk + ["LBM1"], ["KK"])
            DVE(lambda e: e.tensor_scalar(out=SG, in0=SG, scalar1=OML[:, h:h + 1], scalar2=LB[:, h:h + 1], op0=ALU.mult, op1=ALU.add),
                sgk + ["OML", "LB"], sgk)

        def S2():
            ACT(SG, SG, AF.Ln, sgk, sgk)

        def S3():
            DVE(lambda e: e.tensor_tensor_scan(out=BB, data0=RMASK, data1=SG, initial=0.0, op0=ALU.mult, op1=ALU.add),
                sgk + ["RMASK"], ["BB"])
            DVE(lambda e: e.tensor_tensor(out=d3, in0=bl_bc, in1=b3, op=ALU.subtract), ["BB"], ["DD"])

        def S4():
            ACT(DD, DD, AF.Exp, ["DD"], ["DD"])
            ACT(GAM[set3], BB[:, 63::64], AF.Exp, ["BB"], [("GAM", set3)])
            if s == 1:
                ACT(EP, BB, AF.Exp, ["BB"], ["EP"])

        def S5():
            DVE(lambda e: e.tensor_tensor(out=KHT[par2], in0=KK, in1=DD, op=ALU.mult), ["KK", "DD"], [("KHT", par2)])

        def S6():
            ACT(SG, BB, AF.Exp, ["BB"], sgk, scale=-1.0)

        def S7():
            DVE(lambda e: e.tensor_tensor(out=KT[hp], in0=KK, in1=SG, op=ALU.mult), ["KK"] + sgk, [("KT", hp)])

        def f_step(hf):
            fm(wa3, slice(0, 128), hf, ka,
               lambda b: ACT(SG[:, hf * 512:(hf + 1) * 512], PS[b][:, :], AF.Sigmoid, [("ps", b)], [("SG", hf)]))

        def inp_step(hf):
            fm(wa3, slice(128, 256), hf, ka,
               lambda b: ACT(INPT[:, hf * 512:(hf + 1) * 512], PS[b][:, :], AF.Copy, [("ps", b)], [("INPT", hf)]))

        def tv_step():
            b = rot()
            ptb = PS[b][:, :].bitcast(BF16)
            for i in range(8):
                P.op("pe", lambda e, i=i, ptb=ptb: e.transpose(ptb[:, i * 128:(i + 1) * 128], INPT[:, i * 128:(i + 1) * 128], IDB),
                     [("INPT", i // 4), "IDB"], [("ps", b)], signal=(i == 7))
            ACT(VTOK[set3], ptb, AF.Copy, [("ps", b)], [("VTOK", set3)])

        f_step(0)
        yield
        f_step(1)
        S1()
        yield
        inp_step(0)
        S2()
        yield
        inp_step(1)
        S3()
        yield
        if s == 0:
            tv_step()
            S4()
            S5()
            yield
            return
        w_done()
        wb, kb = w_next()
        wb3 = c3(wb, 256)

        def q_step(hf):
            fm(wb3, slice(0, 128), hf, kb,
               lambda b: ACT(QQ[:, hf * 512:(hf + 1) * 512], PS[b][:, :], AF.Silu, [("ps", b)], [("QQ", hf)]))

        def g_step(hf):
            fm(wb3, slice(128, 256), hf, kb,
               lambda b: ACT(ZG[hp][:, hf * 512:(hf + 1) * 512], PS[b][:, :], AF.Silu, [("ps", b)], [("ZG", hp, hf)]))

        tv_step()
        S4()
        S6()
        yield
        q_step(0)
        S5()
        S7()
        yield
        q_step(1)
        yield
        g_step(0)
        yield
        g_step(1)
        DVE(lambda e: e.tensor_tensor(out=QT[hp], in0=QQ, in1=EP, op=ALU.mult),
            [("QQ", 0), ("QQ", 1), "EP"], [("QT", hp)])
        w_done()
        yield

    def tk(ui):
        par2 = ui % 2
        b = rot()
        ptb = PS[b][:, :].bitcast(BF16)
        for i in range(8):
            P.op("pe", lambda e, i=i, ptb=ptb: e.transpose(ptb[:, i * 128:(i + 1) * 128], KHT[par2][:, i * 128:(i + 1) * 128], IDB),
                 [("KHT", par2), "IDB"], [("ps", b)], signal=(i == 7))
        ACT(KH0[par2][0:64, :], ptb[0:64, :], AF.Copy, [("ps", b)], [("KH0", par2)])
        ACT(KH1[par2][64:128, :], ptb[64:128, :], AF.Copy, [("ps", b)], [("KH1", par2)])

    def rec_gen(ui, pending):
        h, s = units[ui]
        hp, par2, set3 = h % 2, ui % 2, ui % 3
        kh = [KH0[par2].rearrange("p (i d) -> p i d", d=128), KH1[par2].rearrange("p (i d) -> p i d", d=128)]
        v3 = VTOK[set3].rearrange("p (i d) -> p i d", d=128)
        if s == 0:
            DVE(lambda e: e.memset(SF[0], 0.0), [], [("SF", 0)])

        def stageA(g):
            bk = KVB[kvb_i["i"] % 2]
            kvb_i["i"] += 1
            todo = [cc for cc in range(4) if not (s == 1 and 4 * g + cc == 15)]
            for cc in todo:
                c = 4 * g + cc
                MM(PS[bk][:, cc * 128:(cc + 1) * 128], kh[c % 2][:, c // 2, :], v3[:, c // 2, :], True, True,
                   [("KH0", par2), ("KH1", par2), ("VTOK", set3)], [("ps", bk)])
            for cc in todo:
                c = 4 * g + cc
                k = 16 * s + c
                kv = PS[bk][:, cc * 128:(cc + 1) * 128]
                g_ = GAM[set3][:, c:c + 1]
                DVE(lambda e, k=k, g_=g_, kv=kv: e.scalar_tensor_tensor(
                    out=SF[(k + 1) % 4], in0=SF[k % 4], scalar=g_, in1=kv, op0=ALU.mult, op1=ALU.add),
                    [("SF", k % 4), ("GAM", set3), ("ps", bk)], [("SF", (k + 1) % 4)])
                if 16 <= k + 1 <= 31:
                    P.op("pool", lambda e, k=k: e.tensor_copy(out=SBFR[(k + 1) % 16], in_=SF[(k + 1) % 4]),
                         [("SF", (k + 1) % 4)], [("SBFR", (k + 1) % 16)])

        def attn(i):
            tk_ = slice(i * 128, (i + 1) * 128)
            b = rot()
            MM(PS[b][:, 0:128], KT[hp][:, tk_], QT[hp][:, tk_], True, True, [("KT", hp), ("QT", hp)], [("ps", b)])
            DVE(lambda e, i=i, b=b: e.tensor_tensor(out=ATS[i % 2], in0=PS[b][:, 0:128], in1=MASKF, op=ALU.mult),
                [("ps", b), "CV"], [("ATS", i % 2)])

        def intra(i):
            ob = 4 + i // 4
            ocol = (i % 4) * 128
            MM(PS[ob][:, ocol:ocol + 128], v3[:, i, :], ATS[i % 2], i % 4 == 0, False,
               [("VTOK", set3), ("ATS", i % 2)], [("ps", ob)], signal=True)

        def stageC(g):
            for cc in range(4):
                c = 4 * g + cc
                k = 16 + c
                ob = 4 + c // 8
                ocol = (c % 8) * 64
                MM(PS[ob][:, ocol:ocol + 64], SBFR[k % 16], QT[hp][:, c * 64:(c + 1) * 64], False, c % 8 == 7,
                   [("SBFR", k % 16), ("QT", hp)], [("ps", ob)], signal=True)

        if s == 0:
            for g in range(4):
                stageA(g)
                yield
        else:
            seq = [("A", 0), ("B", 0), ("A", 1), ("B", 1), ("C", 0), ("A", 2), ("B", 2), ("C", 1), ("A", 3), ("B", 3),
                   ("C", 2), ("C", 3)]
            for kind, g in seq:
                if kind == "A":
                    stageA(g)
                    yield
                elif kind == "B":
                    attn(2 * g)
                    attn(2 * g + 1)
                    yield
                    intra(2 * g)
                    intra(2 * g + 1)
                else:
                    stageC(g)
                    yield
            for hf in range(2):
                ACT(SQ[0][:, hf * 512:(hf + 1) * 512], PS[4 + hf][:, :], AF.Square, [("ps", 4 + hf)], [("SQ", 0, hf)])
        if ui + 1 < len(units):
            tk(ui + 1)
        yield
        if s == 1:
            for hf in range(2):
                sl = slice(hf * 512, (hf + 1) * 512)
                b = rot()
                MM(PS[b][:, :], ONESB, SQ[0][:, sl], True, True, [("SQ", 0, hf), "ONESB"], [("ps", b)])
                r = RSTD[:, sl]
                ACT(r, PS[b][:, :], AF.Ln, [("ps", b), "EPSC"], [("RSTD", hf)], scale=1.0 / 128, bias=EPSC)
                ACT(r, r, AF.Exp, [("RSTD", hf)], [("RSTD", hf)], scale=-0.5)
                DVE(lambda e, r=r, hf=hf: e.scalar_tensor_tensor(
                    out=r, in0=PS[4 + hf][:, :], scalar=CV[:, C_HN:C_HN + 1], in1=r, op0=ALU.mult, op1=ALU.mult),
                    [("ps", 4 + hf), ("RSTD", hf), "CV"], [("RSTD", hf)])
                DVE(lambda e, r=r, sl=sl: e.tensor_tensor(out=OZ[:, h, sl], in0=r, in1=ZG[hp][:, sl], op=ALU.mult),
                    [("RSTD", hf), ("ZG", hp, hf)], [kRQ(h, hf)])
            yield

    def drain(g):
        for _ in g:
            pass

    def u_gen():
        for j in range(4):
            w, k = w_next()
            w3 = c3(w, 256)
            for jj in range(2):
                n = 2 * j + jj
                for hf in range(2):
                    b = rot()
                    for c in range(16):
                        MM(PS[b][:, :], w3[:, c, jj * 128:(jj + 1) * 128], HT[:, c, hf * 512:(hf + 1) * 512],
                           c == 0, c == 15, [k, kRH(c, hf)], [("ps", b)])
                    ACT(UU[:, n, hf * 512:(hf + 1) * 512], PS[b][:, :], AF.Gelu, [("ps", b)], [kRQ(8 + n, hf)])
                    if jj == 1 and hf == 1:
                        w_done()
                    yield

    ug = u_gen()
    NU = len(units)
    drain(proj_gen(0))
    drain(proj_gen(1))
    tk(0)
    for ui in range(NU):
        rg = rec_gen(ui, None)
        n_y = 5 if units[ui][1] == 0 else 14
        if ui + 2 < NU:
            pg = proj_gen(ui + 2)
            n_p = 5 if units[ui + 2][1] == 0 else 9
        else:
            pg, n_p = ug, (8 if units[ui][1] == 0 else 8)
        yd, pd = 0, 0
        for _ in rg:
            yd += 1
            while pd < n_p and pd * n_y < yd * n_p:
                next(pg, None)
                pd += 1
        if pg is not ug:
            drain(pg)
    dump("oz", RQ[:, 0:8192], [kRQ(h, hf) for h in range(8) for hf in range(2)])

    TMPA = [RX[:, 14336 + i * 512:14336 + (i + 1) * 512] for i in range(4)]
    TMPS = [RX[:, 10368 + i * 512:10368 + (i + 1) * 512] for i in range(2)]
    tmpa_keys = [("TMPA", i) for i in range(4)]
    A2K = A2_KEYS + [("TMPS", 0), ("TMPS", 1)]
    fence(A1_KEYS, A2K + tmpa_keys)
    bcast = lambda r: bass.AP(ln_d.tensor, r * 1024, [[0, 128], [1, 1024]])
    spdma(GLN, bcast(0), [], ["GLN"], "gln")
    spdma(BLN, bcast(1), [], ["BLN"], "bln")
    spdma(BSP, bcast(2), [], ["BSP"], "bsp")
    pooldma(WSP, wsp_d, [], ["WSP"], "wsp")
    wsp3 = WSP.rearrange("p (h t) -> p h t", t=128)
    P.op("dve", lambda e: e.memset(wsp3[64:128, :, 0:64], 0.0), [], ["WSP"])
    rr = {"i": 0}

    def pair4():
        b = (rr["i"] % 3) * 2
        rr["i"] += 1
        return b

    spi = {"i": 0}

    def sp_group(h, g4):
        b = pair4() + (spi["i"] % 2)
        ti = spi["i"] % 2
        spi["i"] += 1
        for ii in range(4):
            i = g4 * 4 + ii
            o = PS[b][:, ii * 128:(ii + 1) * 128]
            MM(o, VTG[:, i, h * 128:(h + 1) * 128], wsp3[:, h, :], True, True, [("VTG", i), "WSP"], [("ps", b)])
        sl = slice(g4 * 512, (g4 + 1) * 512)
        tmp = TMPS[ti]
        bs = BSP[:, h * 128:h * 128 + 1]
        bs_bc = bass.AP(bs.tensor, bs.offset, [[bs.ap[0][0], 128], [0, 4], [1, 128]])
        DVE(lambda e: e.tensor_tensor(
            out=tmp.rearrange("p (i t) -> p i t", t=128), in0=PS[b][:, :].rearrange("p (i t) -> p i t", t=128),
            in1=bs_bc, op=ALU.add), [("ps", b), "BSP"], [("TMPS", ti)])
        DVE(lambda e: e.tensor_tensor(out=UU[:, h, sl], in0=UU[:, h, sl], in1=tmp, op=ALU.mult),
            [kRQ(8 + h, g4), ("TMPS", ti)], [kRQ(8 + h, g4)])

    drain(ug)
    wv = [w_next() for _ in range(4)]
    for i in range(8):
        b = pair4()
        gv = GVT[i % 2]
        gk = ("GVT", i % 2)
        for j in range(4):
            w3 = c3(wv[j][0], 256)
            o = PS[b + j // 2][:, (j % 2) * 256:(j % 2 + 1) * 256]
            for c in range(16):
                MM(o, HT[:, c, i * 128:(i + 1) * 128], w3[:, c, :], c == 0, c == 15,
                   [wv[j][1], kRH(c, i // 4)], [("ps", b + j // 2)])
        for hf in range(2):
            ACT(gv[:, hf * 512:(hf + 1) * 512], PS[b + hf][:, :], AF.Gelu, [("ps", b + hf)], [gk])
        for hf in range(2):
            DVE(lambda e, hf=hf, gv=gv: e.bn_stats(out=STATS[:, hf * 6:(hf + 1) * 6], in_=gv[:, hf * 512:(hf + 1) * 512]),
                [gk], ["STATS"])
        DVE(lambda e: e.bn_aggr(out=STATS[:, 16:18], in_=STATS[:, 0:12]), ["STATS"], ["STATS"])
        ACT(STATS[:, 18:19], STATS[:, 17:18], AF.Ln, ["STATS", "EPSC"], ["STATS"], bias=EPSC)
        ACT(STATS[:, 18:19], STATS[:, 18:19], AF.Exp, ["STATS"], ["STATS"], scale=-0.5)
        DVE(lambda e, gv=gv: e.tensor_scalar(out=gv, in0=gv, scalar1=STATS[:, 16:17], scalar2=STATS[:, 18:19],
                                             op0=ALU.subtract, op1=ALU.mult), [gk, "STATS"], [gk])
        DVE(lambda e, gv=gv: e.tensor_tensor(out=gv, in0=gv, in1=GLN, op=ALU.mult), [gk, "GLN"], [gk])
        DVE(lambda e, gv=gv, i=i: e.tensor_tensor(out=VTG[:, i, :], in0=gv, in1=BLN, op=ALU.add), [gk, "BLN"], [("VTG", i)])
        if i >= 4:
            sp_group(2 * (i - 4), 0)
            sp_group(2 * (i - 4) + 1, 0)
    w_done(4)

    rx_lo = [kRX(c, h) for c in range(14) for h in range(2)]
    rx_hi = [kRX(c, h) for c in range(14, 16) for h in range(2)]
    xv_ = xT_d.rearrange("(c p) t -> p c t", p=128)
    bset = {"i": 0}
    mw = {}

    def merge_group(n, hf):
        if hf == 0:
            mw["g"] = w_next()
            mw["ab"] = w_next()
        wg, kg = mw["g"]
        wab, kab = mw["ab"]
        g3 = c3(wg, 256)
        wa3 = c3(wab[:, 0:1024], 128)
        wb3 = c3(wab[:, 1024:2048], 128)
        b0 = (bset["i"] % 2) * 4
        bset["i"] += 1
        ts = slice(hf * 512, (hf + 1) * 512)
        for c in range(16):
            MM(PS[b0][:, :], g3[:, c, 0:128], HT[:, c, ts], c == 0, c == 15, [kg, kRH(c, hf)], [("ps", b0)])
        for c in range(8):
            MM(PS[b0 + 1][:, :], wa3[:, c, :], OZ[:, c, ts], c == 0, c == 7, [kab, kRQ(c, hf)], [("ps", b0 + 1)])
        for c in range(16):
            MM(PS[b0 + 2][:, :], g3[:, c, 128:256], HT[:, c, ts], c == 0, c == 15, [kg, kRH(c, hf)], [("ps", b0 + 2)])
        for c in range(8):
            MM(PS[b0 + 3][:, :], wb3[:, c, :], UU[:, c, ts], c == 0, c == 7, [kab, kRQ(8 + c, hf)], [("ps", b0 + 3)])
        ta, tb = TMPA[(b0 // 4) * 2], TMPA[(b0 // 4) * 2 + 1]
        ka_, kb_ = ("TMPA", (b0 // 4) * 2), ("TMPA", (b0 // 4) * 2 + 1)
        ACT(ta, PS[b0][:, :], AF.Sigmoid, [("ps", b0)], [ka_])
        DVE(lambda e: e.tensor_tensor(out=ta, in0=ta, in1=PS[b0 + 1][:, :], op=ALU.mult), [ka_, ("ps", b0 + 1)], [ka_])
        ACT(tb, PS[b0 + 2][:, :], AF.Sigmoid, [("ps", b0 + 2)], [kb_])
        DVE(lambda e: e.tensor_tensor(out=tb, in0=tb, in1=PS[b0 + 3][:, :], op=ALU.mult), [kb_, ("ps", b0 + 3)], [kb_])
        DVE(lambda e: e.tensor_tensor(out=RPv[:, n, ts], in0=ta, in1=tb, op=ALU.add), [ka_, kb_], [kRP(n, hf)])
        if hf == 1:
            w_done(2)

    merge_group(0, 0)
    for h in range(8):
        sp_group(h, 1)
    dump("prod", RQ[:, 8192:16384], [kRQ(8 + h, hf) for h in range(8) for hf in range(2)])
    fence(A2K, rx_lo)
    for g, (c0, c1) in enumerate([(0, 4), (4, 8), (8, 11), (11, 14)]):
        spdma(XT[:, c0:c1, :], xv_[:, c0:c1, :], [], [kRX(c, h) for c in range(c0, c1) for h in range(2)], f"xr{g}")
    merge_group(0, 1)
    for n in range(1, 16):
        merge_group(n, 0)
        merge_group(n, 1)
    dump("merged", RP[:, :], [kRP(c, h) for c in range(16) for h in range(2)])

    fence([("TMPA", i) for i in range(4)], rx_hi)
    spdma(XT[:, 14:16, :], xv_[:, 14:16, :], [], rx_hi, "xr4")

    def sq_inline(n, hf):
        qi = sqi["i"] % 4
        sqi["i"] += 1
        sq = SQ[qi // 2][:, (qi % 2) * 512:(qi % 2 + 1) * 512]
        sk = ("SQ", qi // 2, qi % 2)
        ACT(sq, XT[:, n, hf * 512:(hf + 1) * 512], AF.Square, [kRX(n, hf)], [sk])
        deferred.append(lambda: MM(PS[6 + hf][:, :], ONESB, sq, n == 0, n == 15, [sk, "ONESB"], [("ps", 6 + hf)],
                                   signal=True))

    deferred = []

    def run_deferred(keep):
        while len(deferred) > keep:
            deferred.pop(0)()

    def rstd_from(bank0):
        for h in range(2):
            r = RSTD[:, h * 512:(h + 1) * 512]
            ACT(r, PS[bank0 + h][:, :], AF.Ln, [("ps", bank0 + h), "EPSC"], [("RSTD", h)], scale=1.0 / D, bias=EPSC)
            ACT(r, r, AF.Exp, [("RSTD", h)], [("RSTD", h)], scale=-0.5)

    def proj_add(src, skey, kc, stats=False):
        for j in range(8):
            w, k = w_next()
            w3 = c3(w, 256)
            for jj in range(2):
                n = 2 * j + jj
                b = pair4()
                for hf in range(2):
                    ts = slice(hf * 512, (hf + 1) * 512)
                    run_deferred(2)
                    for c in range(kc):
                        MM(PS[b + hf][:, :], w3[:, c, jj * 128:(jj + 1) * 128], src[:, c, ts], c == 0, c == kc - 1,
                           [k, skey(c, hf)], [("ps", b + hf)])
                    DVE(lambda e, n=n, ts=ts, b=b, hf=hf: e.tensor_tensor(out=XT[:, n, ts], in0=XT[:, n, ts], in1=PS[b + hf][:, :],
                                                                        op=ALU.add), [kRX(n, hf), ("ps", b + hf)], [kRX(n, hf)])
                    if stats:
                        sq_inline(n, hf)
            w_done()
        run_deferred(0)

    proj_add(RPv, kRP, 16, stats=True)
    dump("x1", RX[:, :], rx_keys)

    rstd_from(6)
    rms_apply(HT, kRH, XT, kRX, C_GFFN)
    for q in range(4):
        H1, hkey = (RPv, kRP) if q % 2 == 0 else (RQv, kRQ)
        for j in range(8):
            w, k = w_next()
            w3 = c3(w, 256)
            for jj in range(2):
                n = 2 * j + jj
                b = pair4()
                for hf in range(2):
                    ts = slice(hf * 512, (hf + 1) * 512)
                    for c in range(16):
                        MM(PS[b + hf][:, :], w3[:, c, jj * 128:(jj + 1) * 128], HT[:, c, ts], c == 0, c == 15,
                           [k, kRH(c, hf)], [("ps", b + hf)])
                    r = RSTD[:, ts]
                    ACT(r, PS[b + hf][:, :], AF.Relu, [("ps", b + hf)], [("RSTD", hf)])
                    DVE(lambda e, r=r, n=n, ts=ts, H1=H1: e.tensor_tensor(out=H1[:, n, ts], in0=r, in1=r, op=ALU.mult),
                        [("RSTD", hf)], [hkey(n, hf)])
            w_done()
        proj_add(H1, hkey, 16, stats=(q == 3))
    dump("x2", RX[:, :], rx_keys)

    rstd_from(6)
    rms_apply(HT, kRH, XT, kRX, C_GPLE)
    PTB = c3(RP[:, 0:2048])
    pooldma(PTB, pT_d.rearrange("(c p) t -> p c t", p=128), [], [kRP(0, 0), kRP(0, 1), kRP(1, 0), kRP(1, 1)], "ptb")
    wpp, kpp = w_next()
    WPP = RP[:, 4096:8192]
    kwpp = [kRP(c, h) for c in range(4, 8) for h in range(2)]
    P.op("act", lambda e: e.activation(out=WPP, in_=wpp, func=AF.Copy), [kpp], kwpp)
    w_done()
    wpp3 = c3(WPP, 2048)
    GT = [RP[:, 2048 + i * 1024:2048 + (i + 1) * 1024].bitcast(F32) for i in range(2)]
    for j in range(8):
        w, k = w_next()
        w3 = c3(w, 256)
        for jj in range(2):
            n = 2 * j + jj
            for hf in range(2):
                b = pair4()
                ts = slice(hf * 512, (hf + 1) * 512)
                run_deferred(2)
                for c in range(16):
                    MM(PS[b][:, :], w3[:, c, jj * 128:(jj + 1) * 128], HT[:, c, ts], c == 0, c == 15,
                       [k, kRH(c, hf)], [("ps", b)])
                for c in range(2):
                    MM(PS[b + 1][:, :], wpp3[:, c, n * 128:(n + 1) * 128], PTB[:, c, ts], c == 0, c == 1,
                       kwpp + [kRP(c, hf)], [("ps", b + 1)])
                gt = GT[hf]
                gk = kRP(2, hf)
                ACT(gt, PS[b][:, :], AF.Sigmoid, [("ps", b)], [gk])
                DVE(lambda e, gt=gt, b=b: e.tensor_tensor(out=gt, in0=gt, in1=PS[b + 1][:, :], op=ALU.mult),
                    [gk, ("ps", b + 1)], [gk])
                DVE(lambda e, gt=gt, n=n, ts=ts: e.tensor_tensor(out=XT[:, n, ts], in0=XT[:, n, ts], in1=gt, op=ALU.add),
                    [gk, kRX(n, hf)], [kRX(n, hf)])
                sq_inline(n, hf)
        w_done()
    run_deferred(0)

    rstd_from(6)
    ov = out_d.rearrange("(c p) t -> p c t", p=128)
    for g in range(8):
        for c in range(2 * g, 2 * g + 2):
            for h in range(2):
                sl = slice(h * 512, (h + 1) * 512)
                DVE(lambda e, c=c, sl=sl: e.scalar_tensor_tensor(
                    out=XT[:, c, sl], in0=XT[:, c, sl], scalar=CV[:, C_GFIN + c:C_GFIN + c + 1], in1=RSTD[:, sl],
                    op0=ALU.mult, op1=ALU.mult), [kRX(c, h), ("RSTD", h), "CV"], [kRX(c, h)])
        t = spdma(ov[:, g * 2:(g + 1) * 2, :], XT[:, g * 2:(g + 1) * 2, :],
                  [kRX(c, h) for c in range(g * 2, g * 2 + 2) for h in range(2)], [], f"out{g}")
        final_toks.append(t)
    for t in final_toks:
        P.wait("sp", t)
    assert wstate["cur"] == NBLK and wstate["done"] == NBLK, wstate
    assert P.simulate(), "deadlock in plan"
    print("plan ops", {e: len(P.ops[e]) for e in ENGS}, "sems", {e: P.echan[e].val for e in P.echan})
    P.emit(block)
    es.close()
    return nc


def make_in_maps(inputs):
    f = lambda a: np.ascontiguousarray(np.asarray(a, dtype=np.float32))
    x = f(inputs["x"])
    p = f(inputs["p"])[0]
    ws = {
        "w_in": f(inputs["w_in"])[0], "w_a_out": f(inputs["w_a_out"])[0], "w_b_out": f(inputs["w_b_out"])[0],
        "w_o": f(inputs["w_o"])[0], "w_ff1": f(inputs["w_ff1"])[0], "w_ff2": f(inputs["w_ff2"])[0],
        "w_ple_gate": f(inputs["w_ple_gate"])[0], "w_ple_proj": f(inputs["w_ple_proj"])[0],
    }
    wb = pack_weights(ws)
    cv = np.zeros((128, NCV), np.float32)
    fm = lambda v: np.ascontiguousarray(v.reshape(-1, 128).T)
    cv[:, C_GMIX:C_GMIX + 16] = fm(f(inputs["norm_mix"])[0])
    cv[:, C_GFFN:C_GFFN + 16] = fm(f(inputs["norm_ffn"])[0])
    cv[:, C_GPLE:C_GPLE + 16] = fm(f(inputs["norm_ple"])[0])
    cv[:, C_GFIN:C_GFIN + 16] = fm(f(inputs["norm_final"]))
    lbl = f(inputs["lb_logits"])
    cv[:, C_L0:C_L0 + 8] = fm(lbl[0])
    cv[:, C_L1:C_L1 + 8] = fm(lbl[1])
    cv[:, C_HN] = f(inputs["hgrn_norm"])[0]
    cv[:, C_ID:C_ID + 128] = np.eye(128, dtype=np.float32)
    s_ = np.arange(128)[:, None]
    t_ = np.arange(128)[None, :]
    cv[:, C_MASK:C_MASK + 128] = ((s_ <= t_) & (s_ // 64 == t_ // 64)).astype(np.float32)
    lnrow = np.stack([f(inputs["gmlp_ln_g"])[0], f(inputs["gmlp_ln_b"])[0], f(inputs["b_spatial"])[0].reshape(-1)])
    wspT = np.ascontiguousarray(f(inputs["w_spatial"])[0].transpose(2, 0, 1).reshape(128, 1024))
    maps = []
    for core in range(8):
        b, half = core // 2, core % 2
        xs = x[b, half * T:(half + 1) * T, :]
        xp = x[b, 0:T, :] if half == 1 else np.zeros((T, D), np.float32)
        maps.append({
            "xT": np.ascontiguousarray(xs.T), "xpT": np.ascontiguousarray(xp.T),
            "pT": np.ascontiguousarray(p[b, half * T:(half + 1) * T, :].T),
            "wb": wb, "cvec": cv, "lnrow": np.ascontiguousarray(lnrow), "wspT": wspT,
        })
    return maps


_NC_CACHE = {}


def kernel(**inputs):
    maps = make_in_maps(inputs)
    if "nc" not in _NC_CACHE:
        _NC_CACHE["nc"] = build_program()
    res = run_bass_kernel_spmd(_NC_CACHE["nc"], maps, core_ids=list(range(8)))
    out = np.empty((4, 2 * T, D), np.float32)
    for core in range(8):
        b, half = core // 2, core % 2
        out[b, half * T:(half + 1) * T, :] = res.results[core]["outT"].T
    return out
```

```python
from contextlib import ExitStack

import numpy as np
import concourse.bass as bass
import concourse.mybir as mybir
from concourse.bass_utils import run_bass_kernel_spmd

F32 = mybir.dt.float32
BF16 = mybir.dt.bfloat16
AF = mybir.ActivationFunctionType
ALU = mybir.AluOpType

D = 2048
T = 1024
NH = 8
EPS = 1e-6
NSLOT = 4
BLK = 4096
ENGS = ["pe", "act", "dve", "pool", "sp"]

C_GMIX, C_GFFN, C_GPLE, C_GFIN, C_L0, C_L1, C_HN, C_ID, C_MASK, NCV = 0, 16, 32, 48, 64, 72, 80, 96, 224, 352


class Chan:
    def __init__(self, sem, step, name):
        self.sem, self.step, self.val, self.name = sem, step, 0, name


class Planner:
    def __init__(self, nc):
        self.nc = nc
        self.ops = {e: [] for e in ENGS}
        self.echan = {}
        self.waited = {e: {} for e in ENGS}
        self.res = {}
        for e in ["pe", "act", "dve", "pool"]:
            self.echan[e] = self.new_chan(1, "c_" + e)

    def new_chan(self, step, name):
        return Chan(self.nc.alloc_semaphore(name=name), step, name)

    def _deps(self, eng, reads, writes):
        deps = {}

        def add(tok, raw):
            if tok is None:
                return
            ch, v = tok
            if ch is self.echan.get(eng) and eng == "pe":
                return
            if deps.get(ch, 0) < v:
                deps[ch] = v

        for k in reads:
            r = self.res.get(k)
            if r is not None:
                add(r[0], True)
        for k in writes:
            r = self.res.get(k)
            if r is not None:
                add(r[0], False)
                for ch, v in r[1].items():
                    add((ch, v), False)
        waits = []
        for ch, v in deps.items():
            if self.waited[eng].get(ch, 0) < v:
                self.waited[eng][ch] = v
                waits.append((ch, v))
        return waits

    def _record(self, tok, reads, writes):
        for k in writes:
            self.res[k] = [tok, {}]
        ch, v = tok
        for k in reads:
            r = self.res.setdefault(k, [None, {}])
            if r[1].get(ch, 0) < v:
                r[1][ch] = v

    def op(self, eng, fn, reads=(), writes=(), signal=True):
        waits = self._deps(eng, reads, writes)
        ch = self.echan[eng]
        if signal:
            ch.val += 1
            tok = (ch, ch.val)
        else:
            tok = (ch, ch.val + 1)
        self.ops[eng].append((waits, fn, (ch, 1) if signal else None))
        self._record(tok, reads, writes)
        return tok

    def dma(self, eng, chan, fn, reads=(), writes=(), after=()):
        waits = self._deps(eng, reads, writes)
        for ch, v in after:
            if self.waited[eng].get(ch, 0) < v:
                self.waited[eng][ch] = v
                waits.append((ch, v))
        chan.val += chan.step
        tok = (chan, chan.val)
        self.ops[eng].append((waits, fn, (chan, chan.step)))
        self._record(tok, reads, writes)
        return tok

    def wait(self, eng, tok):
        ch, v = tok
        if self.waited[eng].get(ch, 0) < v:
            self.waited[eng][ch] = v
            self.ops[eng].append(([(ch, v)], None, None))

    def simulate(self):
        val = {}
        pc = {e: 0 for e in ENGS}
        progress = True
        while progress:
            progress = False
            for e in ENGS:
                ops = self.ops[e]
                while pc[e] < len(ops):
                    waits, fn, inc = ops[pc[e]]
                    if any(val.get(ch.name, 0) < v for ch, v in waits):
                        break
                    if inc is not None:
                        val[inc[0].name] = val.get(inc[0].name, 0) + inc[1]
                    pc[e] += 1
                    progress = True
        stuck = {e: (pc[e], len(self.ops[e])) for e in ENGS if pc[e] < len(self.ops[e])}
        for e, (i, n) in stuck.items():
            waits, fn, inc = self.ops[e][i]
            print("STUCK", e, i, "/", n, [(ch.name, v, val.get(ch.name, 0)) for ch, v in waits])
        return not stuck

    def emit(self, block):
        handles = {"pe": block.tensor, "act": block.scalar, "dve": block.vector,
                   "pool": block.gpsimd, "sp": block.sync}
        for e in ENGS:
            ops = self.ops[e]
            if not ops:
                continue

            def body(engine, ops=ops):
                for waits, fn, inc in ops:
                    for ch, v in waits:
                        engine.wait_ge(ch.sem, v)
                    if fn is None:
                        continue
                    ins = fn(engine)
                    if inc is not None:
                        ins.then_inc(inc[0].sem, inc[1])

            handles[e](body)


def weight_blocks():
    blks = []
    for h in range(NH):
        blks.append(("w_in", 0, D, [(1024 + h * 128, 128), (2048 + h * 128, 128)]))
        blks.append(("w_in", 0, D, [(h * 128, 128), (3072 + h * 128, 128)]))
    for j in range(4):
        blks.append(("w_in", 0, D, [(4096 + j * 256, 256)]))
    for j in range(4):
        blks.append(("w_in", 0, D, [(5120 + j * 256, 256)]))
    for n in range(16):
        blks.append(("w_in", 0, D, [(6144 + n * 128, 128), (8192 + n * 128, 128)]))
        blks.append(("w_ab", 0, 1024, [(n * 128, 128)]))
    for j in range(8):
        blks.append(("w_o", 0, D, [(j * 256, 256)]))
    for q in range(4):
        for j in range(8):
            blks.append(("w_ff1", 0, D, [(q * 2048 + j * 256, 256)]))
        for j in range(8):
            blks.append(("w_ff2", q * 2048, D, [(j * 256, 256)]))
    blks.append(("w_ple_proj", 0, 256, [(0, 2048)]))
    for j in range(8):
        blks.append(("w_ple_gate", 0, D, [(j * 256, 256)]))
    return blks


def pack_weights(ws):
    blks = weight_blocks()
    out = np.zeros((len(blks), 128, BLK), np.float32)
    for i, (name, r0, K, cols) in enumerate(blks):
        kc = K // 128
        if name == "w_ab":
            c0, ncl = cols[0]
            a = ws["w_a_out"][:, c0:c0 + ncl].reshape(kc, 128, ncl).transpose(1, 0, 2).reshape(128, kc * ncl)
            b = ws["w_b_out"][:, c0:c0 + ncl].reshape(kc, 128, ncl).transpose(1, 0, 2).reshape(128, kc * ncl)
            out[i, :, :kc * ncl] = a
            out[i, :, kc * ncl:2 * kc * ncl] = b
            continue
        W = ws[name]
        sub = np.concatenate([W[r0:r0 + K, c0:c0 + ncl] for c0, ncl in cols], axis=1)
        ncl = sub.shape[1]
        out[i, :, :kc * ncl] = sub.reshape(kc, 128, ncl).transpose(1, 0, 2).reshape(128, kc * ncl)
    return out


NBLK = len(weight_blocks())


def build_program(dbg=()):
    nc = bass.Bass("TRN2", target_bir_lowering=False)
    xT_d = nc.dram_tensor("xT", [D, T], F32, kind="ExternalInput").ap()
    xpT_d = nc.dram_tensor("xpT", [D, T], F32, kind="ExternalInput").ap()
    pT_d = nc.dram_tensor("pT", [256, T], F32, kind="ExternalInput").ap()
    wb_d = nc.dram_tensor("wb", [NBLK, 128, BLK], F32, kind="ExternalInput").ap()
    cv_d = nc.dram_tensor("cvec", [128, NCV], F32, kind="ExternalInput").ap()
    ln_d = nc.dram_tensor("lnrow", [3, 1024], F32, kind="ExternalInput").ap()
    wsp_d = nc.dram_tensor("wspT", [128, 1024], F32, kind="ExternalInput").ap()
    out_d = nc.dram_tensor("outT", [D, T], F32, kind="ExternalOutput").ap()
    dbg_d = {}
    for name, shape in dbg:
        dbg_d[name] = nc.dram_tensor("dbg_" + name, list(shape), F32, kind="ExternalOutput").ap()

    es = ExitStack()
    RX = es.enter_context(nc.sbuf_tensor("RX", [128, 16384], F32))
    RH = es.enter_context(nc.sbuf_tensor("RH", [128, 16384], BF16))
    RP = es.enter_context(nc.sbuf_tensor("RP", [128, 16384], BF16))
    RQ = es.enter_context(nc.sbuf_tensor("RQ", [128, 16384], BF16))
    RW = es.enter_context(nc.sbuf_tensor("RW", [128, NSLOT * BLK], BF16))
    RC = es.enter_context(nc.sbuf_tensor("RC", [128, 3200], F32))
    PS = [es.enter_context(nc.psum_tensor(f"ps{i}", [128, 512], F32)) for i in range(8)]
    P = Planner(nc)
    block = es.enter_context(nc.Block())

    def c3(ap, t=1024):
        return ap.rearrange("p (c t) -> p c t", t=t)

    XT = c3(RX[:, :])
    HT = c3(RH[:, :])
    RPv = c3(RP[:, :])
    RQv = c3(RQ[:, :])
    WS = [RW[:, s * BLK:(s + 1) * BLK] for s in range(NSLOT)]
    PT = PS[7][:, :].bitcast(BF16)

    CV = RC[:, 0:NCV]
    RSTD = RC[:, 352:1376]
    SQ = [RC[:, 1376 + i * 512:1376 + (i + 1) * 512].bitcast(BF16) for i in range(2)]
    RMASK = RC[:, 2400:2912].bitcast(BF16)
    ONESB = RC[:, 2912:2976].bitcast(BF16)
    IDB = RC[:, 2976:3040].bitcast(BF16)
    LB = RC[:, 3040:3048]
    OML = RC[:, 3048:3056]
    LBM1 = RC[:, 3056:3064]
    ONESF = RC[:, 3064:3192]
    MISC = RC[:, 3192:3200]
    MASKF = CV[:, C_MASK:C_MASK + 128]

    def rxf(i):
        return RX[:, i * 1024:(i + 1) * 1024]

    SG, KK, BB, DD, EP, QQ = [rxf(i) for i in range(6)]

    def rxb(i):
        return RX[:, 6144 + i * 512:6144 + (i + 1) * 512].bitcast(BF16)

    KT = [rxb(0), rxb(1)]
    QT = [rxb(2), rxb(3)]
    ZG = [rxb(4), rxb(5)]
    KHT = [rxb(6), rxb(7)]
    KH0 = [rxb(8), rxb(9)]
    KH1 = [rxb(10), rxb(11)]
    VTOK = [rxb(12), rxb(13), rxb(14)]
    INPT = rxb(15)
    o0 = 6144 + 16 * 512
    ATS = [RX[:, o0 + i * 64:o0 + (i + 1) * 64].bitcast(BF16) for i in range(2)]
    SF = [RX[:, o0 + 128 + i * 128:o0 + 128 + (i + 1) * 128] for i in range(4)]
    SBFR = [RX[:, o0 + 640 + i * 64:o0 + 640 + (i + 1) * 64].bitcast(BF16) for i in range(16)]
    GAM = [RX[:, o0 + 1664 + i * 16:o0 + 1664 + (i + 1) * 16] for i in range(3)]
    assert o0 + 1664 + 48 <= 16384
    OZ = c3(RQ[:, 0:8192])
    UU = c3(RQ[:, 8192:16384])
    GVT = [rxf(0), rxf(1)]
    GLN, BLN, BSP = rxf(2), rxf(3), rxf(4)
    WSP = RX[:, 5 * 1024:5 * 1024 + 512].bitcast(BF16)
    VTG = c3(RX[:, 6 * 1024:10 * 1024].bitcast(BF16))
    STATS = RX[:, 10 * 1024:10 * 1024 + 64]

    rx_keys = [("RX", c, h) for c in range(16) for h in range(2)]
    A1_KEYS = [("SG", 0), ("SG", 1), "KK", "BB", "DD", "EP", ("QQ", 0), ("QQ", 1), ("INPT", 0), ("INPT", 1)] + \
        [(n, i) for n in ["KT", "QT", "KHT", "KH0", "KH1", "ATS"] for i in range(2)] + \
        [("VTOK", i) for i in range(3)] + [("GAM", i) for i in range(3)] + [("SF", i) for i in range(4)] + \
        [("SBFR", i) for i in range(16)] + [("ZG", i, hf) for i in range(2) for hf in range(2)]
    A2_KEYS = [("GVT", 0), ("GVT", 1), "GLN", "BLN", "BSP", "WSP", "STATS"] + [("VTG", i) for i in range(8)]

    def ACT(out, in_, func, reads, writes, **kw):
        return P.op("act", lambda e: e.activation(out=out, in_=in_, func=func, **kw), reads, writes)

    def DVE(fn, reads, writes):
        return P.op("dve", fn, reads, writes)

    def MM(out, lhsT, rhs, start, stop, reads, writes, signal=None):
        return P.op("pe", lambda e: e.matmul(out, lhsT, rhs, start=start, stop=stop), reads, writes,
                    signal=stop if signal is None else signal)

    def fence(old, new):
        P.op("dve", lambda e: e.memset(MISC[:, 0:1], 0.0), reads=[], writes=list(old) + list(new) + ["MISC0"])

    def spdma(out, in_, reads, writes, name):
        ch = P.new_chan(16, name)
        return P.dma("sp", ch, lambda e: e.dma_start(out=out, in_=in_), reads, writes)

    def pooldma(out, in_, reads, writes, name):
        ch = P.new_chan(16, name)
        return P.dma("pool", ch, lambda e: e.dma_start(out=out, in_=in_), reads, writes)

    wch = [P.new_chan(16, f"w{s}") for s in range(NSLOT)]
    wstate = {"issued": 0, "done": 0, "cur": 0}

    def w_pump(limit=NBLK):
        while wstate["issued"] < min(NBLK, limit) and wstate["issued"] - NSLOT < wstate["done"]:
            j = wstate["issued"]
            s = j % NSLOT
            P.dma("pool", wch[s],
                  lambda e, j=j, s=s: e.dma_start(out=c3(WS[s], 2048), in_=c3(wb_d[j], 2048)),
                  reads=[], writes=[("ws", s)], after=wstate.get("gate", []) if 0 < j < NSLOT else [])
            wstate["issued"] += 1

    def w_next():
        i = wstate["cur"]
        wstate["cur"] += 1
        w_pump()
        assert wstate["issued"] > i, "weight block not issued (too many blocks held)"
        return WS[i % NSLOT], ("ws", i % NSLOT)

    def w_done(n=1):
        wstate["done"] += n
        w_pump()

    def dump(name, ap, reads):
        if name in dbg_d:
            if ap.dtype != F32:
                ap = ap.bitcast(F32)
            t = spdma(dbg_d[name], ap, reads, [], "dbg_" + name)
            final_toks.append(t)

    final_toks = []

    t_cv = spdma(CV, cv_d, [], ["CV"], "cv")
    EPSC = MISC[:, 1:2]
    P.op("dve", lambda e: e.memset(EPSC, EPS), [], ["EPSC"])
    P.op("dve", lambda e: e.memset(ONESB, 1.0), [], ["ONESB"])
    P.op("dve", lambda e: e.memset(ONESF, 1.0), [], ["ONESF"])
    P.op("dve", lambda e: e.memset(RMASK, 1.0), [], ["RMASK"])
    P.op("dve", lambda e: e.memset(RMASK[:, 0::64], 0.0), [], ["RMASK"])
    P.op("dve", lambda e: e.tensor_copy(out=IDB, in_=CV[:, C_ID:C_ID + 128]), ["CV"], ["IDB"])
    P.op("dve", lambda e: e.tensor_tensor(out=LBM1, in0=CV[:, C_L0:C_L0 + 8], in1=CV[:, C_L1:C_L1 + 8], op=ALU.subtract),
         ["CV"], ["LBM1"])
    ACT(LB, LBM1, AF.Sigmoid, ["LBM1"], ["LB"])
    P.op("dve", lambda e: e.tensor_scalar(out=OML, in0=LB, scalar1=-1.0, scalar2=1.0, op0=ALU.mult, op1=ALU.add),
         ["LB"], ["OML"])
    P.op("dve", lambda e: e.tensor_scalar(out=LBM1, in0=LB, scalar1=-1.0, scalar2=None, op0=ALU.add),
         ["LB"], ["LBM1"])

    sqi = {"i": 0}

    def rms_stats(src, src_key, bank0):
        for c in range(16):
            sq = SQ[sqi["i"] % 2]
            sk = ("SQ", sqi["i"] % 2)
            sqi["i"] += 1
            for h in range(2):
                ACT(sq[:, h * 512:(h + 1) * 512], src[:, c, h * 512:(h + 1) * 512], AF.Square,
                    [src_key(c, h)], [sk + (h,)])
            for h in range(2):
                MM(PS[bank0 + h][:, :], ONESB, sq[:, h * 512:(h + 1) * 512], c == 0, c == 15,
                   [sk + (h,), "ONESB"], [("ps", bank0 + h)], signal=True)
        for h in range(2):
            r = RSTD[:, h * 512:(h + 1) * 512]
            ACT(r, PS[bank0 + h][:, :], AF.Ln, [("ps", bank0 + h), "EPSC"], [("RSTD", h)], scale=1.0 / D, bias=EPSC)
            ACT(r, r, AF.Exp, [("RSTD", h)], [("RSTD", h)], scale=-0.5)

    def rms_apply(dst, dst_key, src, src_key, gcol):
        for h in range(2):
            for c in range(16):
                sl = slice(h * 512, (h + 1) * 512)
                DVE(lambda e, c=c, sl=sl: e.scalar_tensor_tensor(
                    out=dst[:, c, sl], in0=src[:, c, sl], scalar=CV[:, gcol + c:gcol + c + 1], in1=RSTD[:, sl],
                    op0=ALU.mult, op1=ALU.mult),
                    [src_key(c, h), ("RSTD", h), "CV"], [dst_key(c, h)])

    kRX = lambda c, h: ("RX", c, h)
    kRH = lambda c, h: ("RH", c, h)
    kRP = lambda c, h: ("RP", c, h)
    kRQ = lambda c, h: ("RQ", c, h)

    def load_x(src_d, tag):
        v = src_d.rearrange("(c p) t -> p c t", p=128)
        for g in range(4):
            spdma(XT[:, g * 4:(g + 1) * 4, :], v[:, g * 4:(g + 1) * 4, :], [],
                  [kRX(c, h) for c in range(g * 4, g * 4 + 4) for h in range(2)], f"x{tag}{g}")

    RQF = RQ[:, :].bitcast(F32)
    jobs = [(xpT_d, 0, 512, c3(RX[:, 0:8192], 512), "A", RPv, kRP, 0),
            (xpT_d, 512, 512, c3(RX[:, 8192:16384], 512), "B", RPv, kRP, 1)]
    for q in range(4):
        jobs.append((xT_d, q * 256, 256, c3(RQF[:, (q % 2) * 4096:(q % 2 + 1) * 4096], 256), "Q%d" % (q % 2), HT, kRH,
                     [3, 6, 4, 5][q]))
    nb_keys = [("NB", t, c) for t in ("A", "B") for c in range(16)]
    nq_keys = [("NB", t, c) for t in ("Q0", "Q1") for c in range(16)]

    def job_load(ji):
        src_d, tok0, ntok, buf, tag, dst, dkey, bank = jobs[ji]
        v = src_d.rearrange("(c p) t -> p c t", p=128)
        toks = []
        for g in range(2):
            toks.append(spdma(buf[:, g * 8:(g + 1) * 8, :], v[:, g * 8:(g + 1) * 8, tok0:tok0 + ntok], [],
                              [("NB", tag, c) for c in range(g * 8, g * 8 + 8)], f"nx{ji}{g}"))
        return toks

    def job_compute(ji):
        src_d, tok0, ntok, buf, tag, dst, dkey, bank = jobs[ji]
        hf = tok0 // 512
        for c in range(16):
            qi = sqi["i"] % 4
            sqi["i"] += 1
            sq = SQ[qi // 2][:, (qi % 2) * 512:(qi % 2) * 512 + ntok]
            sk = ("SQ", qi // 2, qi % 2)
            ACT(sq, buf[:, c, :], AF.Square, [("NB", tag, c)], [sk])
            MM(PS[bank][:, 0:ntok], ONESB, sq, c == 0, c == 15, [sk, "ONESB"], [("ps", bank)], signal=True)
        r = RSTD[:, tok0:tok0 + ntok]
        rk = ("RSTD", hf)
        ACT(r, PS[bank][:, 0:ntok], AF.Ln, [("ps", bank), "EPSC"], [rk], scale=1.0 / D, bias=EPSC)
        ACT(r, r, AF.Exp, [rk], [rk], scale=-0.5)
        for c in range(16):
            DVE(lambda e, c=c: e.scalar_tensor_tensor(
                out=dst[:, c, tok0:tok0 + ntok], in0=buf[:, c, :], scalar=CV[:, C_GMIX + c:C_GMIX + c + 1], in1=r,
                op0=ALU.mult, op1=ALU.mult), [("NB", tag, c), rk, "CV"], [dkey(c, hf)])

    gate = job_load(0) + job_load(1)
    w_pump(1)
    gate += job_load(2) + job_load(3)
    wstate["gate"] = gate
    job_compute(0)
    job_compute(1)
    dump("ht", RH[:, :], [kRH(c, h) for c in range(16) for h in range(2)])
    fence(nb_keys, A1_KEYS)

    units = [(h, s) for h in range(NH) for s in range(2)]
    uw = {}
    R4 = [0, 1, 2, 7]
    rot_i = {"i": 0}

    def rot():
        b = R4[rot_i["i"] % 4]
        rot_i["i"] += 1
        return b

    KVB = [3, 6]
    kvb_i = {"i": 0}
    for i in range(2):
        P.op("dve", lambda e, i=i: e.memset(KH0[i], 0.0), [], [("KH0", i)])
        P.op("dve", lambda e, i=i: e.memset(KH1[i], 0.0), [], [("KH1", i)])

    def proj_gen(ui):
        h, s = units[ui]
        hp, par2, set3 = h % 2, ui % 2, ui % 3
        src, skey = (RPv, kRP) if s == 0 else (HT, kRH)
        if s == 0:
            uw[h] = w_next()
        wa, ka = uw[h]
        wa3 = c3(wa, 256)

        def fm(w3, cs, hf, k, evac):
            b = rot()
            for c in range(16):
                MM(PS[b][:, :], w3[:, c, cs], src[:, c, hf * 512:(hf + 1) * 512], c == 0, c == 15,
                   [k, skey(c, hf)], [("ps", b)])
            evac(b)

        sgk = [("SG", 0), ("SG", 1)]
        bl = BB[:, 63:64]
        bl_bc = bass.AP(bl.tensor, bl.offset, [[bl.ap[0][0], 128], [64, 16], [0, 64]])
        b3 = BB.rearrange("p (c s) -> p c s", s=64)
        d3 = DD.rearrange("p (c s) -> p c s", s=64)

        def S1():
            DVE(lambda e: e.tensor_scalar(out=KK, in0=SG, scalar1=-1.0, scalar2=LBM1[:, h:h + 1], op0=ALU.add, op1=ALU.mult),
                sgk + ["LBM1"], ["KK"])
            DVE(lambda e: e.tensor_scalar(out=SG, in0=SG, scalar1=OML[:, h:h + 1], scalar2=LB[:, h:h + 1], op0=ALU.mult, op1=ALU.add),
                sgk + ["OML", "LB"], sgk)

        def S2():
            ACT(SG, SG, AF.Ln, sgk, sgk)

        def S3():
            DVE(lambda e: e.tensor_tensor_scan(out=BB, data0=RMASK, data1=SG, initial=0.0, op0=ALU.mult, op1=ALU.add),
                sgk + ["RMASK"], ["BB"])
            DVE(lambda e: e.tensor_tensor(out=d3, in0=bl_bc, in1=b3, op=ALU.subtract), ["BB"], ["DD"])

        def S4():
            ACT(DD, DD, AF.Exp, ["DD"], ["DD"])
            ACT(GAM[set3], BB[:, 63::64], AF.Exp, ["BB"], [("GAM", set3)])
            if s == 1:
                ACT(EP, BB, AF.Exp, ["BB"], ["EP"])

        def S5():
            DVE(lambda e: e.tensor_tensor(out=KHT[par2], in0=KK, in1=DD, op=ALU.mult), ["KK", "DD"], [("KHT", par2)])

        def S6():
            ACT(SG, BB, AF.Exp, ["BB"], sgk, scale=-1.0)

        def S7():
            DVE(lambda e: e.tensor_tensor(out=KT[hp], in0=KK, in1=SG, op=ALU.mult), ["KK"] + sgk, [("KT", hp)])

        def f_step(hf):
            fm(wa3, slice(0, 128), hf, ka,
               lambda b: ACT(SG[:, hf * 512:(hf + 1) * 512], PS[b][:, :], AF.Sigmoid, [("ps", b)], [("SG", hf)]))

        def inp_step(hf):
            fm(wa3, slice(128, 256), hf, ka,
               lambda b: ACT(INPT[:, hf * 512:(hf + 1) * 512], PS[b][:, :], AF.Copy, [("ps", b)], [("INPT", hf)]))

        def tv_step():
            b = rot()
            ptb = PS[b][:, :].bitcast(BF16)
            for i in range(8):
                P.op("pe", lambda e, i=i, ptb=ptb: e.transpose(ptb[:, i * 128:(i + 1) * 128], INPT[:, i * 128:(i + 1) * 128], IDB),
                     [("INPT", i // 4), "IDB"], [("ps", b)], signal=(i == 7))
            ACT(VTOK[set3], ptb, AF.Copy, [("ps", b)], [("VTOK", set3)])

        f_step(0)
        yield
        f_step(1)
        S1()
        yield
        inp_step(0)
        S2()
        yield
        inp_step(1)
        S3()
        yield
        if s == 0:
            tv_step()
            S4()
            S5()
            yield
            return
        w_done()
        wb, kb = w_next()
        wb3 = c3(wb, 256)

        def q_step(hf):
            fm(wb3, slice(0, 128), hf, kb,
               lambda b: ACT(QQ[:, hf * 512:(hf + 1) * 512], PS[b][:, :], AF.Silu, [("ps", b)], [("QQ", hf)]))

        def g_step(hf):
            fm(wb3, slice(128, 256), hf, kb,
               lambda b: ACT(ZG[hp][:, hf * 512:(hf + 1) * 512], PS[b][:, :], AF.Silu, [("ps", b)], [("ZG", hp, hf)]))

        tv_step()
        S4()
        S6()
        yield
        q_step(0)
        S5()
        S7()
        yield
        q_step(1)
        yield
        g_step(0)
        yield
        g_step(1)
        DVE(lambda e: e.tensor_tensor(out=QT[hp], in0=QQ, in1=EP, op=ALU.mult),
            [("QQ", 0), ("QQ", 1), "EP"], [("QT", hp)])
        w_done()
        yield

    def tk(ui):
        par2 = ui % 2
        b = rot()
        ptb = PS[b][:, :].bitcast(BF16)
        for i in range(8):
            P.op("pe", lambda e, i=i, ptb=ptb: e.transpose(ptb[:, i * 128:(i + 1) * 128], KHT[par2][:, i * 128:(i + 1) * 128], IDB),
                 [("KHT", par2), "IDB"], [("ps", b)], signal=(i == 7))
        ACT(KH0[par2][0:64, :], ptb[0:64, :], AF.Copy, [("ps", b)], [("KH0", par2)])
        ACT(KH1[par2][64:128, :], ptb[64:128, :], AF.Copy, [("ps", b)], [("KH1", par2)])

    def rec_gen(ui, pending):
        h, s = units[ui]
        hp, par2, set3 = h % 2, ui % 2, ui % 3
        kh = [KH0[par2].rearrange("p (i d) -> p i d", d=128), KH1[par2].rearrange("p (i d) -> p i d", d=128)]
        v3 = VTOK[set3].rearrange("p (i d) -> p i d", d=128)
        if s == 0:
            DVE(lambda e: e.memset(SF[0], 0.0), [], [("SF", 0)])

        def stageA(g):
            bk = KVB[kvb_i["i"] % 2]
            kvb_i["i"] += 1
            todo = [cc for cc in range(4) if not (s == 1 and 4 * g + cc == 15)]
            for cc in todo:
                c = 4 * g + cc
                MM(PS[bk][:, cc * 128:(cc + 1) * 128], kh[c % 2][:, c // 2, :], v3[:, c // 2, :], True, True,
                   [("KH0", par2), ("KH1", par2), ("VTOK", set3)], [("ps", bk)])
            for cc in todo:
                c = 4 * g + cc
                k = 16 * s + c
                kv = PS[bk][:, cc * 128:(cc + 1) * 128]
                g_ = GAM[set3][:, c:c + 1]
                DVE(lambda e, k=k, g_=g_, kv=kv: e.scalar_tensor_tensor(
                    out=SF[(k + 1) % 4], in0=SF[k % 4], scalar=g_, in1=kv, op0=ALU.mult, op1=ALU.add),
                    [("SF", k % 4), ("GAM", set3), ("ps", bk)], [("SF", (k + 1) % 4)])
                if 16 <= k + 1 <= 31:
                    P.op("pool", lambda e, k=k: e.tensor_copy(out=SBFR[(k + 1) % 16], in_=SF[(k + 1) % 4]),
                         [("SF", (k + 1) % 4)], [("SBFR", (k + 1) % 16)])

        def attn(i):
            tk_ = slice(i * 128, (i + 1) * 128)
            b = rot()
            MM(PS[b][:, 0:128], KT[hp][:, tk_], QT[hp][:, tk_], True, True, [("KT", hp), ("QT", hp)], [("ps", b)])
            DVE(lambda e, i=i, b=b: e.tensor_tensor(out=ATS[i % 2], in0=PS[b][:, 0:128], in1=MASKF, op=ALU.mult),
                [("ps", b), "CV"], [("ATS", i % 2)])

        def intra(i):
            ob = 4 + i // 4
            ocol = (i % 4) * 128
            MM(PS[ob][:, ocol:ocol + 128], v3[:, i, :], ATS[i % 2], i % 4 == 0, False,
               [("VTOK", set3), ("ATS", i % 2)], [("ps", ob)], signal=True)

        def stageC(g):
            for cc in range(4):
                c = 4 * g + cc
                k = 16 + c
                ob = 4 + c // 8
                ocol = (c % 8) * 64
                MM(PS[ob][:, ocol:ocol + 64], SBFR[k % 16], QT[hp][:, c * 64:(c + 1) * 64], False, c % 8 == 7,
                   [("SBFR", k % 16), ("QT", hp)], [("ps", ob)], signal=True)

        if s == 0:
            for g in range(4):
                stageA(g)
                yield
        else:
            seq = [("A", 0), ("B", 0), ("A", 1), ("B", 1), ("C", 0), ("A", 2), ("B", 2), ("C", 1), ("A", 3), ("B", 3),
                   ("C", 2), ("C", 3)]
            for kind, g in seq:
                if kind == "A":
                    stageA(g)
                    yield
                elif kind == "B":
                    attn(2 * g)
                    attn(2 * g + 1)
                    yield
                    intra(2 * g)
                    intra(2 * g + 1)
                else:
                    stageC(g)
                    yield
            for hf in range(2):
                ACT(SQ[0][:, hf * 512:(hf + 1) * 512], PS[4 + hf][:, :], AF.Square, [("ps", 4 + hf)], [("SQ", 0, hf)])
        if ui + 1 < len(units):
            tk(ui + 1)
        yield
        if s == 1:
            for hf in range(2):
                sl = slice(hf * 512, (hf + 1) * 512)
                b = rot()
                MM(PS[b][:, :], ONESB, SQ[0][:, sl], True, True, [("SQ", 0, hf), "ONESB"], [("ps", b)])
                r = RSTD[:, sl]
                ACT(r, PS[b][:, :], AF.Ln, [("ps", b), "EPSC"], [("RSTD", hf)], scale=1.0 / 128, bias=EPSC)
                ACT(r, r, AF.Exp, [("RSTD", hf)], [("RSTD", hf)], scale=-0.5)
                DVE(lambda e, r=r, hf=hf: e.scalar_tensor_tensor(
                    out=r, in0=PS[4 + hf][:, :], scalar=CV[:, C_HN:C_HN + 1], in1=r, op0=ALU.mult, op1=ALU.mult),
                    [("ps", 4 + hf), ("RSTD", hf), "CV"], [("RSTD", hf)])
                DVE(lambda e, r=r, sl=sl: e.tensor_tensor(out=OZ[:, h, sl], in0=r, in1=ZG[hp][:, sl], op=ALU.mult),
                    [("RSTD", hf), ("ZG", hp, hf)], [kRQ(h, hf)])
            yield

    def drain(g):
        for _ in g:
            pass

    def u_gen():
        for j in range(4):
            w, k = w_next()
            w3 = c3(w, 256)
            for jj in range(2):
                n = 2 * j + jj
                for hf in range(2):
                    b = rot()
                    for c in range(16):
                        MM(PS[b][:, :], w3[:, c, jj * 128:(jj + 1) * 128], HT[:, c, hf * 512:(hf + 1) * 512],
                           c == 0, c == 15, [k, kRH(c, hf)], [("ps", b)])
                    ACT(UU[:, n, hf * 512:(hf + 1) * 512], PS[b][:, :], AF.Gelu, [("ps", b)], [kRQ(8 + n, hf)])
                    if jj == 1 and hf == 1:
                        w_done()
                    yield

    ug = u_gen()
    NU = len(units)
    pg0 = proj_gen(0)
    for q in range(4):
        next(pg0)
        job_compute(2 + q)
        if q < 2:
            job_load(4 + q)
    drain(pg0)
    sqi["i"] = 0
    fence(nq_keys, [kRQ(c, h) for c in range(16) for h in range(2)])
    drain(proj_gen(1))
    tk(0)
    for ui in range(NU):
        rg = rec_gen(ui, None)
        n_y = 5 if units[ui][1] == 0 else 14
        if ui + 2 < NU:
            pg = proj_gen(ui + 2)
            n_p = 5 if units[ui + 2][1] == 0 else 9
        else:
            pg, n_p = ug, (8 if units[ui][1] == 0 else 8)
        yd, pd = 0, 0
        for _ in rg:
            yd += 1
            while pd < n_p and pd * n_y < yd * n_p:
                next(pg, None)
                pd += 1
        if pg is not ug:
            drain(pg)
    dump("oz", RQ[:, 0:8192], [kRQ(h, hf) for h in range(8) for hf in range(2)])

    TMPA = [RX[:, 14336 + i * 512:14336 + (i + 1) * 512] for i in range(4)]
    TMPS = [RX[:, 10368 + i * 512:10368 + (i + 1) * 512] for i in range(2)]
    tmpa_keys = [("TMPA", i) for i in range(4)]
    A2K = A2_KEYS + [("TMPS", 0), ("TMPS", 1)]
    fence(A1_KEYS, A2K + tmpa_keys)
    bcast = lambda r: bass.AP(ln_d.tensor, r * 1024, [[0, 128], [1, 1024]])
    spdma(GLN, bcast(0), [], ["GLN"], "gln")
    spdma(BLN, bcast(1), [], ["BLN"], "bln")
    spdma(BSP, bcast(2), [], ["BSP"], "bsp")
    pooldma(WSP, wsp_d, [], ["WSP"], "wsp")
    wsp3 = WSP.rearrange("p (h t) -> p h t", t=128)
    P.op("dve", lambda e: e.memset(wsp3[64:128, :, 0:64], 0.0), [], ["WSP"])
    rr = {"i": 0}

    def pair4():
        b = (rr["i"] % 3) * 2
        rr["i"] += 1
        return b

    spi = {"i": 0}

    def sp_group(h, g4):
        b = pair4() + (spi["i"] % 2)
        ti = spi["i"] % 2
        spi["i"] += 1
        for ii in range(4):
            i = g4 * 4 + ii
            o = PS[b][:, ii * 128:(ii + 1) * 128]
            MM(o, VTG[:, i, h * 128:(h + 1) * 128], wsp3[:, h, :], True, True, [("VTG", i), "WSP"], [("ps", b)])
        sl = slice(g4 * 512, (g4 + 1) * 512)
        tmp = TMPS[ti]
        bs = BSP[:, h * 128:h * 128 + 1]
        bs_bc = bass.AP(bs.tensor, bs.offset, [[bs.ap[0][0], 128], [0, 4], [1, 128]])
        DVE(lambda e: e.tensor_tensor(
            out=tmp.rearrange("p (i t) -> p i t", t=128), in0=PS[b][:, :].rearrange("p (i t) -> p i t", t=128),
            in1=bs_bc, op=ALU.add), [("ps", b), "BSP"], [("TMPS", ti)])
        DVE(lambda e: e.tensor_tensor(out=UU[:, h, sl], in0=UU[:, h, sl], in1=tmp, op=ALU.mult),
            [kRQ(8 + h, g4), ("TMPS", ti)], [kRQ(8 + h, g4)])

    drain(ug)
    wv = [w_next() for _ in range(4)]
    for i in range(8):
        b = pair4()
        gv = GVT[i % 2]
        gk = ("GVT", i % 2)
        for j in range(4):
            w3 = c3(wv[j][0], 256)
            o = PS[b + j // 2][:, (j % 2) * 256:(j % 2 + 1) * 256]
            for c in range(16):
                MM(o, HT[:, c, i * 128:(i + 1) * 128], w3[:, c, :], c == 0, c == 15,
                   [wv[j][1], kRH(c, i // 4)], [("ps", b + j // 2)])
        for hf in range(2):
            ACT(gv[:, hf * 512:(hf + 1) * 512], PS[b + hf][:, :], AF.Gelu, [("ps", b + hf)], [gk])
        for hf in range(2):
            DVE(lambda e, hf=hf, gv=gv: e.bn_stats(out=STATS[:, hf * 6:(hf + 1) * 6], in_=gv[:, hf * 512:(hf + 1) * 512]),
                [gk], ["STATS"])
        DVE(lambda e: e.bn_aggr(out=STATS[:, 16:18], in_=STATS[:, 0:12]), ["STATS"], ["STATS"])
        ACT(STATS[:, 18:19], STATS[:, 17:18], AF.Ln, ["STATS", "EPSC"], ["STATS"], bias=EPSC)
        ACT(STATS[:, 18:19], STATS[:, 18:19], AF.Exp, ["STATS"], ["STATS"], scale=-0.5)
        DVE(lambda e, gv=gv: e.tensor_scalar(out=gv, in0=gv, scalar1=STATS[:, 16:17], scalar2=STATS[:, 18:19],
                                             op0=ALU.subtract, op1=ALU.mult), [gk, "STATS"], [gk])
        DVE(lambda e, gv=gv: e.tensor_tensor(out=gv, in0=gv, in1=GLN, op=ALU.mult), [gk, "GLN"], [gk])
        DVE(lambda e, gv=gv, i=i: e.tensor_tensor(out=VTG[:, i, :], in0=gv, in1=BLN, op=ALU.add), [gk, "BLN"], [("VTG", i)])
        if i >= 4:
            sp_group(2 * (i - 4), 0)
            sp_group(2 * (i - 4) + 1, 0)
    w_done(4)

    rx_lo = [kRX(c, h) for c in range(14) for h in range(2)]
    rx_hi = [kRX(c, h) for c in range(14, 16) for h in range(2)]
    xv_ = xT_d.rearrange("(c p) t -> p c t", p=128)
    bset = {"i": 0}
    mw = {}

    def merge_group(n, hf):
        if hf == 0:
            mw["g"] = w_next()
            mw["ab"] = w_next()
        wg, kg = mw["g"]
        wab, kab = mw["ab"]
        g3 = c3(wg, 256)
        wa3 = c3(wab[:, 0:1024], 128)
        wb3 = c3(wab[:, 1024:2048], 128)
        b0 = (bset["i"] % 2) * 4
        bset["i"] += 1
        ts = slice(hf * 512, (hf + 1) * 512)
        for c in range(16):
            MM(PS[b0][:, :], g3[:, c, 0:128], HT[:, c, ts], c == 0, c == 15, [kg, kRH(c, hf)], [("ps", b0)])
        for c in range(8):
            MM(PS[b0 + 1][:, :], wa3[:, c, :], OZ[:, c, ts], c == 0, c == 7, [kab, kRQ(c, hf)], [("ps", b0 + 1)])
        for c in range(16):
            MM(PS[b0 + 2][:, :], g3[:, c, 128:256], HT[:, c, ts], c == 0, c == 15, [kg, kRH(c, hf)], [("ps", b0 + 2)])
        for c in range(8):
            MM(PS[b0 + 3][:, :], wb3[:, c, :], UU[:, c, ts], c == 0, c == 7, [kab, kRQ(8 + c, hf)], [("ps", b0 + 3)])
        ta, tb = TMPA[(b0 // 4) * 2], TMPA[(b0 // 4) * 2 + 1]
        ka_, kb_ = ("TMPA", (b0 // 4) * 2), ("TMPA", (b0 // 4) * 2 + 1)
        ACT(ta, PS[b0][:, :], AF.Sigmoid, [("ps", b0)], [ka_])
        DVE(lambda e: e.tensor_tensor(out=ta, in0=ta, in1=PS[b0 + 1][:, :], op=ALU.mult), [ka_, ("ps", b0 + 1)], [ka_])
        ACT(tb, PS[b0 + 2][:, :], AF.Sigmoid, [("ps", b0 + 2)], [kb_])
        DVE(lambda e: e.tensor_tensor(out=tb, in0=tb, in1=PS[b0 + 3][:, :], op=ALU.mult), [kb_, ("ps", b0 + 3)], [kb_])
        DVE(lambda e: e.tensor_tensor(out=RPv[:, n, ts], in0=ta, in1=tb, op=ALU.add), [ka_, kb_], [kRP(n, hf)])
        if hf == 1:
            w_done(2)

    merge_group(0, 0)
    for h in range(8):
        sp_group(h, 1)
    dump("prod", RQ[:, 8192:16384], [kRQ(8 + h, hf) for h in range(8) for hf in range(2)])
    fence(A2K, rx_lo)
    for g, (c0, c1) in enumerate([(0, 4), (4, 8), (8, 11), (11, 14)]):
        spdma(XT[:, c0:c1, :], xv_[:, c0:c1, :], [], [kRX(c, h) for c in range(c0, c1) for h in range(2)], f"xr{g}")
    merge_group(0, 1)
    for n in range(1, 16):
        merge_group(n, 0)
        merge_group(n, 1)
    dump("merged", RP[:, :], [kRP(c, h) for c in range(16) for h in range(2)])

    fence([("TMPA", i) for i in range(4)], rx_hi)
    spdma(XT[:, 14:16, :], xv_[:, 14:16, :], [], rx_hi, "xr4")

    def sq_inline(n, hf):
        qi = sqi["i"] % 4
        sqi["i"] += 1
        sq = SQ[qi // 2][:, (qi % 2) * 512:(qi % 2 + 1) * 512]
        sk = ("SQ", qi // 2, qi % 2)
        ACT(sq, XT[:, n, hf * 512:(hf + 1) * 512], AF.Square, [kRX(n, hf)], [sk])
        deferred.append(lambda: MM(PS[6 + hf][:, :], ONESB, sq, n == 0, n == 15, [sk, "ONESB"], [("ps", 6 + hf)],
                                   signal=True))

    deferred = []

    def run_deferred(keep):
        while len(deferred) > keep:
            deferred.pop(0)()

    def rstd_from(bank0):
        for h in range(2):
            r = RSTD[:, h * 512:(h + 1) * 512]
            ACT(r, PS[bank0 + h][:, :], AF.Ln, [("ps", bank0 + h), "EPSC"], [("RSTD", h)], scale=1.0 / D, bias=EPSC)
            ACT(r, r, AF.Exp, [("RSTD", h)], [("RSTD", h)], scale=-0.5)

    def proj_add(src, skey, kc, stats=False):
        for j in range(8):
            w, k = w_next()
            w3 = c3(w, 256)
            for jj in range(2):
                n = 2 * j + jj
                b = pair4()
                for hf in range(2):
                    ts = slice(hf * 512, (hf + 1) * 512)
                    run_deferred(2)
                    for c in range(kc):
                        MM(PS[b + hf][:, :], w3[:, c, jj * 128:(jj + 1) * 128], src[:, c, ts], c == 0, c == kc - 1,
                           [k, skey(c, hf)], [("ps", b + hf)])
                    DVE(lambda e, n=n, ts=ts, b=b, hf=hf: e.tensor_tensor(out=XT[:, n, ts], in0=XT[:, n, ts], in1=PS[b + hf][:, :],
                                                                        op=ALU.add), [kRX(n, hf), ("ps", b + hf)], [kRX(n, hf)])
                    if stats:
                        sq_inline(n, hf)
            w_done()
        run_deferred(0)

    proj_add(RPv, kRP, 16, stats=True)
    dump("x1", RX[:, :], rx_keys)

    rstd_from(6)
    rms_apply(HT, kRH, XT, kRX, C_GFFN)
    for q in range(4):
        H1, hkey = (RPv, kRP) if q % 2 == 0 else (RQv, kRQ)
        for j in range(8):
            w, k = w_next()
            w3 = c3(w, 256)
            for jj in range(2):
                n = 2 * j + jj
                b = pair4()
                for hf in range(2):
                    ts = slice(hf * 512, (hf + 1) * 512)
                    for c in range(16):
                        MM(PS[b + hf][:, :], w3[:, c, jj * 128:(jj + 1) * 128], HT[:, c, ts], c == 0, c == 15,
                           [k, kRH(c, hf)], [("ps", b + hf)])
                    r = RSTD[:, ts]
                    ACT(r, PS[b + hf][:, :], AF.Relu, [("ps", b + hf)], [("RSTD", hf)])
                    DVE(lambda e, r=r, n=n, ts=ts, H1=H1: e.tensor_tensor(out=H1[:, n, ts], in0=r, in1=r, op=ALU.mult),
                        [("RSTD", hf)], [hkey(n, hf)])
            w_done()
        proj_add(H1, hkey, 16, stats=(q == 3))
    dump("x2", RX[:, :], rx_keys)

    rstd_from(6)
    rms_apply(HT, kRH, XT, kRX, C_GPLE)
    PTB = c3(RP[:, 0:2048])
    pooldma(PTB, pT_d.rearrange("(c p) t -> p c t", p=128), [], [kRP(0, 0), kRP(0, 1), kRP(1, 0), kRP(1, 1)], "ptb")
    wpp, kpp = w_next()
    WPP = RP[:, 4096:8192]
    kwpp = [kRP(c, h) for c in range(4, 8) for h in range(2)]
    P.op("act", lambda e: e.activation(out=WPP, in_=wpp, func=AF.Copy), [kpp], kwpp)
    w_done()
    wpp3 = c3(WPP, 2048)
    GT = [RP[:, 2048 + i * 1024:2048 + (i + 1) * 1024].bitcast(F32) for i in range(2)]
    for j in range(8):
        w, k = w_next()
        w3 = c3(w, 256)
        for jj in range(2):
            n = 2 * j + jj
            for hf in range(2):
                b = pair4()
                ts = slice(hf * 512, (hf + 1) * 512)
                run_deferred(2)
                for c in range(16):
                    MM(PS[b][:, :], w3[:, c, jj * 128:(jj + 1) * 128], HT[:, c, ts], c == 0, c == 15,
                       [k, kRH(c, hf)], [("ps", b)])
                for c in range(2):
                    MM(PS[b + 1][:, :], wpp3[:, c, n * 128:(n + 1) * 128], PTB[:, c, ts], c == 0, c == 1,
                       kwpp + [kRP(c, hf)], [("ps", b + 1)])
                gt = GT[hf]
                gk = kRP(2, hf)
                ACT(gt, PS[b][:, :], AF.Sigmoid, [("ps", b)], [gk])
                DVE(lambda e, gt=gt, b=b: e.tensor_tensor(out=gt, in0=gt, in1=PS[b + 1][:, :], op=ALU.mult),
                    [gk, ("ps", b + 1)], [gk])
                DVE(lambda e, gt=gt, n=n, ts=ts: e.tensor_tensor(out=XT[:, n, ts], in0=XT[:, n, ts], in1=gt, op=ALU.add),
                    [gk, kRX(n, hf)], [kRX(n, hf)])
                sq_inline(n, hf)
        w_done()
    run_deferred(0)

    rstd_from(6)
    ov = out_d.rearrange("(c p) t -> p c t", p=128)
    for g in range(8):
        for c in range(2 * g, 2 * g + 2):
            for h in range(2):
                sl = slice(h * 512, (h + 1) * 512)
                DVE(lambda e, c=c, sl=sl: e.scalar_tensor_tensor(
                    out=XT[:, c, sl], in0=XT[:, c, sl], scalar=CV[:, C_GFIN + c:C_GFIN + c + 1], in1=RSTD[:, sl],
                    op0=ALU.mult, op1=ALU.mult), [kRX(c, h), ("RSTD", h), "CV"], [kRX(c, h)])
        t = spdma(ov[:, g * 2:(g + 1) * 2, :], XT[:, g * 2:(g + 1) * 2, :],
                  [kRX(c, h) for c in range(g * 2, g * 2 + 2) for h in range(2)], [], f"out{g}")
        final_toks.append(t)
    for t in final_toks:
        P.wait("sp", t)
    assert wstate["cur"] == NBLK and wstate["done"] == NBLK, wstate
    assert P.simulate(), "deadlock in plan"
    print("plan ops", {e: len(P.ops[e]) for e in ENGS}, "sems", {e: P.echan[e].val for e in P.echan})
    P.emit(block)
    es.close()
    return nc


def make_in_maps(inputs):
    f = lambda a: np.ascontiguousarray(np.asarray(a, dtype=np.float32))
    x = f(inputs["x"])
    p = f(inputs["p"])[0]
    ws = {
        "w_in": f(inputs["w_in"])[0], "w_a_out": f(inputs["w_a_out"])[0], "w_b_out": f(inputs["w_b_out"])[0],
        "w_o": f(inputs["w_o"])[0], "w_ff1": f(inputs["w_ff1"])[0], "w_ff2": f(inputs["w_ff2"])[0],
        "w_ple_gate": f(inputs["w_ple_gate"])[0], "w_ple_proj": f(inputs["w_ple_proj"])[0],
    }
    wb = pack_weights(ws)
    cv = np.zeros((128, NCV), np.float32)
    fm = lambda v: np.ascontiguousarray(v.reshape(-1, 128).T)
    cv[:, C_GMIX:C_GMIX + 16] = fm(f(inputs["norm_mix"])[0])
    cv[:, C_GFFN:C_GFFN + 16] = fm(f(inputs["norm_ffn"])[0])
    cv[:, C_GPLE:C_GPLE + 16] = fm(f(inputs["norm_ple"])[0])
    cv[:, C_GFIN:C_GFIN + 16] = fm(f(inputs["norm_final"]))
    lbl = f(inputs["lb_logits"])
    cv[:, C_L0:C_L0 + 8] = fm(lbl[0])
    cv[:, C_L1:C_L1 + 8] = fm(lbl[1])
    cv[:, C_HN] = f(inputs["hgrn_norm"])[0]
    cv[:, C_ID:C_ID + 128] = np.eye(128, dtype=np.float32)
    s_ = np.arange(128)[:, None]
    t_ = np.arange(128)[None, :]
    cv[:, C_MASK:C_MASK + 128] = ((s_ <= t_) & (s_ // 64 == t_ // 64)).astype(np.float32)
    lnrow = np.stack([f(inputs["gmlp_ln_g"])[0], f(inputs["gmlp_ln_b"])[0], f(inputs["b_spatial"])[0].reshape(-1)])
    wspT = np.ascontiguousarray(f(inputs["w_spatial"])[0].transpose(2, 0, 1).reshape(128, 1024))
    maps = []
    for core in range(8):
        b, half = core // 2, core % 2
        xs = x[b, half * T:(half + 1) * T, :]
        xp = x[b, 0:T, :] if half == 1 else np.zeros((T, D), np.float32)
        maps.append({
            "xT": np.ascontiguousarray(xs.T), "xpT": np.ascontiguousarray(xp.T),
            "pT": np.ascontiguousarray(p[b, half * T:(half + 1) * T, :].T),
            "wb": wb, "cvec": cv, "lnrow": np.ascontiguousarray(lnrow), "wspT": wspT,
        })
    return maps


_NC_CACHE = {}


def kernel(**inputs):
    maps = make_in_maps(inputs)
    if "nc" not in _NC_CACHE:
        _NC_CACHE["nc"] = build_program()
    res = run_bass_kernel_spmd(_NC_CACHE["nc"], maps, core_ids=list(range(8)))
    out = np.empty((4, 2 * T, D), np.float32)
    for core in range(8):
        b, half = core // 2, core % 2
        out[b, half * T:(half + 1) * T, :] = res.results[core]["outT"].T
    return out
```

```python
from contextlib import ExitStack

import numpy as np
import concourse.bass as bass
import concourse.mybir as mybir
from concourse.bass_utils import run_bass_kernel_spmd

F32 = mybir.dt.float32
BF16 = mybir.dt.bfloat16
AF = mybir.ActivationFunctionType
ALU = mybir.AluOpType

D = 2048
T = 1024
NH = 8
EPS = 1e-6
NSLOT = 4
BLK = 4096
ENGS = ["pe", "act", "dve", "pool", "sp"]

C_GMIX, C_GFFN, C_GPLE, C_GFIN, C_L0, C_L1, C_HN, C_ID, C_MASK, NCV = 0, 16, 32, 48, 64, 72, 80, 96, 224, 352


class Chan:
    def __init__(self, sem, step, name):
        self.sem, self.step, self.val, self.name = sem, step, 0, name


class Planner:
    def __init__(self, nc):
        self.nc = nc
        self.ops = {e: [] for e in ENGS}
        self.echan = {}
        self.waited = {e: {} for e in ENGS}
        self.res = {}
        for e in ["pe", "act", "dve", "pool"]:
            self.echan[e] = self.new_chan(1, "c_" + e)

    def new_chan(self, step, name):
        return Chan(self.nc.alloc_semaphore(name=name), step, name)

    def _deps(self, eng, reads, writes):
        deps = {}

        def add(tok, raw):
            if tok is None:
                return
            ch, v = tok
            if ch is self.echan.get(eng) and eng == "pe":
                return
            if deps.get(ch, 0) < v:
                deps[ch] = v

        for k in reads:
            r = self.res.get(k)
            if r is not None:
                add(r[0], True)
        for k in writes:
            r = self.res.get(k)
            if r is not None:
                add(r[0], False)
                for ch, v in r[1].items():
                    add((ch, v), False)
        waits = []
        for ch, v in deps.items():
            if self.waited[eng].get(ch, 0) < v:
                self.waited[eng][ch] = v
                waits.append((ch, v))
        return waits

    def _record(self, tok, reads, writes):
        for k in writes:
            self.res[k] = [tok, {}]
        ch, v = tok
        for k in reads:
            r = self.res.setdefault(k, [None, {}])
            if r[1].get(ch, 0) < v:
                r[1][ch] = v

    def op(self, eng, fn, reads=(), writes=(), signal=True):
        waits = self._deps(eng, reads, writes)
        ch = self.echan[eng]
        if signal:
            ch.val += 1
            tok = (ch, ch.val)
        else:
            tok = (ch, ch.val + 1)
        self.ops[eng].append((waits, fn, (ch, 1) if signal else None))
        self._record(tok, reads, writes)
        return tok

    def dma(self, eng, chan, fn, reads=(), writes=(), after=()):
        waits = self._deps(eng, reads, writes)
        for ch, v in after:
            if self.waited[eng].get(ch, 0) < v:
                self.waited[eng][ch] = v
                waits.append((ch, v))
        chan.val += chan.step
        tok = (chan, chan.val)
        self.ops[eng].append((waits, fn, (chan, chan.step)))
        self._record(tok, reads, writes)
        return tok

    def wait(self, eng, tok):
        ch, v = tok
        if self.waited[eng].get(ch, 0) < v:
            self.waited[eng][ch] = v
            self.ops[eng].append(([(ch, v)], None, None))

    def simulate(self):
        val = {}
        pc = {e: 0 for e in ENGS}
        progress = True
        while progress:
            progress = False
            for e in ENGS:
                ops = self.ops[e]
                while pc[e] < len(ops):
                    waits, fn, inc = ops[pc[e]]
                    if any(val.get(ch.name, 0) < v for ch, v in waits):
                        break
                    if inc is not None:
                        val[inc[0].name] = val.get(inc[0].name, 0) + inc[1]
                    pc[e] += 1
                    progress = True
        stuck = {e: (pc[e], len(self.ops[e])) for e in ENGS if pc[e] < len(self.ops[e])}
        for e, (i, n) in stuck.items():
            waits, fn, inc = self.ops[e][i]
            print("STUCK", e, i, "/", n, [(ch.name, v, val.get(ch.name, 0)) for ch, v in waits])
        return not stuck

    def emit(self, block):
        handles = {"pe": block.tensor, "act": block.scalar, "dve": block.vector,
                   "pool": block.gpsimd, "sp": block.sync}
        for e in ENGS:
            ops = self.ops[e]
            if not ops:
                continue

            def body(engine, ops=ops):
                for waits, fn, inc in ops:
                    for ch, v in waits:
                        engine.wait_ge(ch.sem, v)
                    if fn is None:
                        continue
                    ins = fn(engine)
                    if inc is not None:
                        ins.then_inc(inc[0].sem, inc[1])

            handles[e](body)


def weight_blocks():
    blks = []
    for h in range(NH):
        blks.append(("w_in", 0, D, [(1024 + h * 128, 128), (2048 + h * 128, 128)]))
        blks.append(("w_in", 0, D, [(h * 128, 128), (3072 + h * 128, 128)]))
    for j in range(4):
        blks.append(("w_in", 0, D, [(4096 + j * 256, 256)]))
    for j in range(4):
        blks.append(("w_in", 0, D, [(5120 + j * 256, 256)]))
    for n in range(16):
        blks.append(("w_in", 0, D, [(6144 + n * 128, 128), (8192 + n * 128, 128)]))
        blks.append(("w_ab", 0, 1024, [(n * 128, 128)]))
    for j in range(8):
        blks.append(("w_o", 0, D, [(j * 256, 256)]))
    for q in range(4):
        for j in range(8):
            blks.append(("w_ff1", 0, D, [(q * 2048 + j * 256, 256)]))
        for j in range(8):
            blks.append(("w_ff2", q * 2048, D, [(j * 256, 256)]))
    blks.append(("w_ple_proj", 0, 256, [(0, 2048)]))
    for j in range(8):
        blks.append(("w_ple_gate", 0, D, [(j * 256, 256)]))
    return blks


def pack_weights(ws):
    blks = weight_blocks()
    out = np.zeros((len(blks), 128, BLK), np.float32)
    for i, (name, r0, K, cols) in enumerate(blks):
        kc = K // 128
        if name == "w_ab":
            c0, ncl = cols[0]
            a = ws["w_a_out"][:, c0:c0 + ncl].reshape(kc, 128, ncl).transpose(1, 0, 2).reshape(128, kc * ncl)
            b = ws["w_b_out"][:, c0:c0 + ncl].reshape(kc, 128, ncl).transpose(1, 0, 2).reshape(128, kc * ncl)
            out[i, :, :kc * ncl] = a
            out[i, :, kc * ncl:2 * kc * ncl] = b
            continue
        W = ws[name]
        sub = np.concatenate([W[r0:r0 + K, c0:c0 + ncl] for c0, ncl in cols], axis=1)
        ncl = sub.shape[1]
        out[i, :, :kc * ncl] = sub.reshape(kc, 128, ncl).transpose(1, 0, 2).reshape(128, kc * ncl)
    return out


NBLK = len(weight_blocks())


def build_program(dbg=()):
    nc = bass.Bass("TRN2", target_bir_lowering=False)
    xT_d = nc.dram_tensor("xT", [D, T], F32, kind="ExternalInput").ap()
    xpT_d = nc.dram_tensor("xpT", [D, T], F32, kind="ExternalInput").ap()
    pT_d = nc.dram_tensor("pT", [256, T], F32, kind="ExternalInput").ap()
    wb_d = nc.dram_tensor("wb", [NBLK, 128, BLK], F32, kind="ExternalInput").ap()
    cv_d = nc.dram_tensor("cvec", [128, NCV], F32, kind="ExternalInput").ap()
    ln_d = nc.dram_tensor("lnrow", [3, 1024], F32, kind="ExternalInput").ap()
    wsp_d = nc.dram_tensor("wspT", [128, 1024], F32, kind="ExternalInput").ap()
    out_d = nc.dram_tensor("outT", [D, T], F32, kind="ExternalOutput").ap()
    dbg_d = {}
    for name, shape in dbg:
        dbg_d[name] = nc.dram_tensor("dbg_" + name, list(shape), F32, kind="ExternalOutput").ap()

    es = ExitStack()
    RX = es.enter_context(nc.sbuf_tensor("RX", [128, 16384], F32))
    RH = es.enter_context(nc.sbuf_tensor("RH", [128, 16384], BF16))
    RP = es.enter_context(nc.sbuf_tensor("RP", [128, 16384], BF16))
    RQ = es.enter_context(nc.sbuf_tensor("RQ", [128, 16384], BF16))
    RW = es.enter_context(nc.sbuf_tensor("RW", [128, NSLOT * BLK], BF16))
    RC = es.enter_context(nc.sbuf_tensor("RC", [128, 3200], F32))
    PS = [es.enter_context(nc.psum_tensor(f"ps{i}", [128, 512], F32)) for i in range(8)]
    P = Planner(nc)
    block = es.enter_context(nc.Block())

    def c3(ap, t=1024):
        return ap.rearrange("p (c t) -> p c t", t=t)

    XT = c3(RX[:, :])
    HT = c3(RH[:, :])
    RPv = c3(RP[:, :])
    RQv = c3(RQ[:, :])
    WS = [RW[:, s * BLK:(s + 1) * BLK] for s in range(NSLOT)]
    PT = PS[7][:, :].bitcast(BF16)

    CV = RC[:, 0:NCV]
    RSTD = RC[:, 352:1376]
    SQ = [RC[:, 1376 + i * 512:1376 + (i + 1) * 512].bitcast(BF16) for i in range(2)]
    RMASK = RC[:, 2400:2912].bitcast(BF16)
    ONESB = RC[:, 2912:2976].bitcast(BF16)
    IDB = RC[:, 2976:3040].bitcast(BF16)
    LB = RC[:, 3040:3048]
    OML = RC[:, 3048:3056]
    LBM1 = RC[:, 3056:3064]
    ONESF = RC[:, 3064:3192]
    MISC = RC[:, 3192:3200]
    MASKF = CV[:, C_MASK:C_MASK + 128]

    def rxf(i):
        return RX[:, i * 1024:(i + 1) * 1024]

    SG, KK, BB, DD, EP, QQ = [rxf(i) for i in range(6)]

    def rxb(i):
        return RX[:, 6144 + i * 512:6144 + (i + 1) * 512].bitcast(BF16)

    KT = [rxb(0), rxb(1)]
    QT = [rxb(2), rxb(3)]
    ZG = [rxb(4), rxb(5)]
    KHT = [rxb(6), rxb(7)]
    KH0 = [rxb(8), rxb(9)]
    KH1 = [rxb(10), rxb(11)]
    VTOK = [rxb(12), rxb(13), rxb(14)]
    INPT = rxb(15)
    o0 = 6144 + 16 * 512
    ATS = [RX[:, o0 + i * 64:o0 + (i + 1) * 64].bitcast(BF16) for i in range(2)]
    SF = [RX[:, o0 + 128 + i * 128:o0 + 128 + (i + 1) * 128] for i in range(4)]
    SBFR = [RX[:, o0 + 640 + i * 64:o0 + 640 + (i + 1) * 64].bitcast(BF16) for i in range(16)]
    GAM = [RX[:, o0 + 1664 + i * 16:o0 + 1664 + (i + 1) * 16] for i in range(3)]
    assert o0 + 1664 + 48 <= 16384
    OZ = c3(RQ[:, 0:8192])
    UU = c3(RQ[:, 8192:16384])
    GVT = [rxf(0), rxf(1)]
    GLN, BLN, BSP = rxf(2), rxf(3), rxf(4)
    WSP = RX[:, 5 * 1024:5 * 1024 + 512].bitcast(BF16)
    VTG = c3(RX[:, 6 * 1024:10 * 1024].bitcast(BF16))
    STATS = RX[:, 10 * 1024:10 * 1024 + 64]

    rx_keys = [("RX", c, h) for c in range(16) for h in range(2)]
    A1_KEYS = [("SG", 0), ("SG", 1), "KK", "BB", "DD", "EP", ("QQ", 0), ("QQ", 1), ("INPT", 0), ("INPT", 1)] + \
        [(n, i) for n in ["KT", "QT", "KHT", "KH0", "KH1", "ATS"] for i in range(2)] + \
        [("VTOK", i) for i in range(3)] + [("GAM", i) for i in range(3)] + [("SF", i) for i in range(4)] + \
        [("SBFR", i) for i in range(16)] + [("ZG", i, hf) for i in range(2) for hf in range(2)]
    A2_KEYS = [("GVT", 0), ("GVT", 1), "GLN", "BLN", "BSP", "WSP", "STATS"] + [("VTG", i) for i in range(8)]

    def ACT(out, in_, func, reads, writes, **kw):
        return P.op("act", lambda e: e.activation(out=out, in_=in_, func=func, **kw), reads, writes)

    def DVE(fn, reads, writes):
        return P.op("dve", fn, reads, writes)

    def MM(out, lhsT, rhs, start, stop, reads, writes, signal=None):
        return P.op("pe", lambda e: e.matmul(out, lhsT, rhs, start=start, stop=stop), reads, writes,
                    signal=stop if signal is None else signal)

    def fence(old, new):
        P.op("dve", lambda e: e.memset(MISC[:, 0:1], 0.0), reads=[], writes=list(old) + list(new) + ["MISC0"])

    def spdma(out, in_, reads, writes, name):
        ch = P.new_chan(16, name)
        return P.dma("sp", ch, lambda e: e.dma_start(out=out, in_=in_), reads, writes)

    def pooldma(out, in_, reads, writes, name):
        ch = P.new_chan(16, name)
        return P.dma("pool", ch, lambda e: e.dma_start(out=out, in_=in_), reads, writes)

    wch = [P.new_chan(16, f"w{s}") for s in range(NSLOT)]
    wstate = {"issued": 0, "done": 0, "cur": 0}

    def w_pump(limit=NBLK):
        while wstate["issued"] < min(NBLK, limit) and wstate["issued"] - NSLOT < wstate["done"]:
            j = wstate["issued"]
            s = j % NSLOT
            P.dma("pool", wch[s],
                  lambda e, j=j, s=s: e.dma_start(out=c3(WS[s], 2048), in_=c3(wb_d[j], 2048)),
                  reads=[], writes=[("ws", s)], after=wstate.get("gate", []) if 0 < j < NSLOT else [])
            wstate["issued"] += 1

    def w_next():
        i = wstate["cur"]
        wstate["cur"] += 1
        w_pump()
        assert wstate["issued"] > i, "weight block not issued (too many blocks held)"
        return WS[i % NSLOT], ("ws", i % NSLOT)

    def w_done(n=1):
        wstate["done"] += n
        w_pump()

    def dump(name, ap, reads):
        if name in dbg_d:
            if ap.dtype != F32:
                ap = ap.bitcast(F32)
            t = spdma(dbg_d[name], ap, reads, [], "dbg_" + name)
            final_toks.append(t)

    final_toks = []

    t_cv = spdma(CV, cv_d, [], ["CV"], "cv")
    EPSC = MISC[:, 1:2]
    P.op("dve", lambda e: e.memset(EPSC, EPS), [], ["EPSC"])
    P.op("dve", lambda e: e.memset(ONESB, 1.0), [], ["ONESB"])
    P.op("dve", lambda e: e.memset(ONESF, 1.0), [], ["ONESF"])
    P.op("dve", lambda e: e.memset(RMASK, 1.0), [], ["RMASK"])
    P.op("dve", lambda e: e.memset(RMASK[:, 0::64], 0.0), [], ["RMASK"])
    P.op("dve", lambda e: e.tensor_copy(out=IDB, in_=CV[:, C_ID:C_ID + 128]), ["CV"], ["IDB"])
    P.op("dve", lambda e: e.tensor_tensor(out=LBM1, in0=CV[:, C_L0:C_L0 + 8], in1=CV[:, C_L1:C_L1 + 8], op=ALU.subtract),
         ["CV"], ["LBM1"])
    ACT(LB, LBM1, AF.Sigmoid, ["LBM1"], ["LB"])
    P.op("dve", lambda e: e.tensor_scalar(out=OML, in0=LB, scalar1=-1.0, scalar2=1.0, op0=ALU.mult, op1=ALU.add),
         ["LB"], ["OML"])
    P.op("dve", lambda e: e.tensor_scalar(out=LBM1, in0=LB, scalar1=-1.0, scalar2=None, op0=ALU.add),
         ["LB"], ["LBM1"])

    sqi = {"i": 0}

    def rms_stats(src, src_key, bank0):
        for c in range(16):
            sq = SQ[sqi["i"] % 2]
            sk = ("SQ", sqi["i"] % 2)
            sqi["i"] += 1
            for h in range(2):
                ACT(sq[:, h * 512:(h + 1) * 512], src[:, c, h * 512:(h + 1) * 512], AF.Square,
                    [src_key(c, h)], [sk + (h,)])
            for h in range(2):
                MM(PS[bank0 + h][:, :], ONESB, sq[:, h * 512:(h + 1) * 512], c == 0, c == 15,
                   [sk + (h,), "ONESB"], [("ps", bank0 + h)], signal=True)
        for h in range(2):
            r = RSTD[:, h * 512:(h + 1) * 512]
            ACT(r, PS[bank0 + h][:, :], AF.Ln, [("ps", bank0 + h), "EPSC"], [("RSTD", h)], scale=1.0 / D, bias=EPSC)
            ACT(r, r, AF.Exp, [("RSTD", h)], [("RSTD", h)], scale=-0.5)

    def rms_apply(dst, dst_key, src, src_key, gcol):
        for h in range(2):
            for c in range(16):
                sl = slice(h * 512, (h + 1) * 512)
                DVE(lambda e, c=c, sl=sl: e.scalar_tensor_tensor(
                    out=dst[:, c, sl], in0=src[:, c, sl], scalar=CV[:, gcol + c:gcol + c + 1], in1=RSTD[:, sl],
                    op0=ALU.mult, op1=ALU.mult),
                    [src_key(c, h), ("RSTD", h), "CV"], [dst_key(c, h)])

    kRX = lambda c, h: ("RX", c, h)
    kRH = lambda c, h: ("RH", c, h)
    kRP = lambda c, h: ("RP", c, h)
    kRQ = lambda c, h: ("RQ", c, h)

    def load_x(src_d, tag):
        v = src_d.rearrange("(c p) t -> p c t", p=128)
        for g in range(4):
            spdma(XT[:, g * 4:(g + 1) * 4, :], v[:, g * 4:(g + 1) * 4, :], [],
                  [kRX(c, h) for c in range(g * 4, g * 4 + 4) for h in range(2)], f"x{tag}{g}")

    NB = [c3(RX[:, 0:8192], 512), c3(RX[:, 8192:16384], 512), c3(RQ[:, :].bitcast(F32), 512)]
    nb_keys = [("NB", i, c) for i in range(3) for c in range(16)]
    jobs = [(xpT_d, 0, RPv, kRP), (xpT_d, 1, RPv, kRP), (xT_d, 0, HT, kRH), (xT_d, 1, HT, kRH)]
    def job_load(ji):
        src_d, hf, dst, dkey = jobs[ji]
        buf, bi = NB[ji % 3], ji % 3
        v = src_d.rearrange("(c p) t -> p c t", p=128)
        toks = []
        for g in range(2):
            toks.append(spdma(buf[:, g * 8:(g + 1) * 8, :], v[:, g * 8:(g + 1) * 8, hf * 512:(hf + 1) * 512], [],
                              [("NB", bi, c) for c in range(g * 8, g * 8 + 8)], f"nx{ji}{g}"))
        return toks

    w_pump(1)
    gate = []
    for ji in range(3):
        gate += job_load(ji)
    for ji, (src_d, hf, dst, dkey) in enumerate(jobs):
        buf = NB[ji % 3]
        bi = ji % 3
        if ji == 3:
            gate += job_load(ji)
            wstate["gate"] = gate
            w_pump()
        bank = ji % 4
        for c in range(16):
            qi = sqi["i"] % 4
            sqi["i"] += 1
            sq = SQ[qi // 2][:, (qi % 2) * 512:(qi % 2 + 1) * 512]
            sk = ("SQ", qi // 2, qi % 2)
            ACT(sq, buf[:, c, :], AF.Square, [("NB", bi, c)], [sk])
            MM(PS[bank][:, :], ONESB, sq, c == 0, c == 15, [sk, "ONESB"], [("ps", bank)], signal=True)
        r = RSTD[:, (ji % 2) * 512:(ji % 2 + 1) * 512]
        rk = ("RSTD", ji % 2)
        ACT(r, PS[bank][:, :], AF.Ln, [("ps", bank), "EPSC"], [rk], scale=1.0 / D, bias=EPSC)
        ACT(r, r, AF.Exp, [rk], [rk], scale=-0.5)
        for c in range(16):
            DVE(lambda e, c=c, buf=buf, dst=dst, hf=hf, r=r: e.scalar_tensor_tensor(
                out=dst[:, c, hf * 512:(hf + 1) * 512], in0=buf[:, c, :], scalar=CV[:, C_GMIX + c:C_GMIX + c + 1], in1=r,
                op0=ALU.mult, op1=ALU.mult), [("NB", bi, c), rk, "CV"], [dkey(c, hf)])
    sqi["i"] = 0
    dump("ht", RH[:, :], [kRH(c, h) for c in range(16) for h in range(2)])
    fence(nb_keys, A1_KEYS + [kRQ(c, h) for c in range(16) for h in range(2)])

    units = [(h, s) for h in range(NH) for s in range(2)]
    uw = {}
    R4 = [0, 1, 2, 7]
    rot_i = {"i": 0}

    def rot():
        b = R4[rot_i["i"] % 4]
        rot_i["i"] += 1
        return b

    KVB = [3, 6]
    kvb_i = {"i": 0}
    for i in range(2):
        P.op("dve", lambda e, i=i: e.memset(KH0[i], 0.0), [], [("KH0", i)])
        P.op("dve", lambda e, i=i: e.memset(KH1[i], 0.0), [], [("KH1", i)])

    def proj_gen(ui):
        h, s = units[ui]
        hp, par2, set3 = h % 2, ui % 2, ui % 3
        src, skey = (RPv, kRP) if s == 0 else (HT, kRH)
        if s == 0:
            uw[h] = w_next()
        wa, ka = uw[h]
        wa3 = c3(wa, 256)

        def fm(w3, cs, hf, k, evac):
            b = rot()
            for c in range(16):
                MM(PS[b][:, :], w3[:, c, cs], src[:, c, hf * 512:(hf + 1) * 512], c == 0, c == 15,
                   [k, skey(c, hf)], [("ps", b)])
            evac(b)

        sgk = [("SG", 0), ("SG", 1)]
        bl = BB[:, 63:64]
        bl_bc = bass.AP(bl.tensor, bl.offset, [[bl.ap[0][0], 128], [64, 16], [0, 64]])
        b3 = BB.rearrange("p (c s) -> p c s", s=64)
        d3 = DD.rearrange("p (c s) -> p c s", s=64)

        def S1():
            DVE(lambda e: e.tensor_scalar(out=KK, in0=SG, scalar1=-1.0, scalar2=LBM1[:, h:h + 1], op0=ALU.add, op1=ALU.mult),
                sgk + ["LBM1"], ["KK"])
            DVE(lambda e: e.tensor_scalar(out=SG, in0=SG, scalar1=OML[:, h:h + 1], scalar2=LB[:, h:h + 1], op0=ALU.mult, op1=ALU.add),
                sgk + ["OML", "LB"], sgk)

        def S2():
            ACT(SG, SG, AF.Ln, sgk, sgk)

        def S3():
            DVE(lambda e: e.tensor_tensor_scan(out=BB, data0=RMASK, data1=SG, initial=0.0, op0=ALU.mult, op1=ALU.add),
                sgk + ["RMASK"], ["BB"])
            DVE(lambda e: e.tensor_tensor(out=d3, in0=bl_bc, in1=b3, op=ALU.subtract), ["BB"], ["DD"])

        def S4():
            ACT(DD, DD, AF.Exp, ["DD"], ["DD"])
            ACT(GAM[set3], BB[:, 63::64], AF.Exp, ["BB"], [("GAM", set3)])
            if s == 1:
                ACT(EP, BB, AF.Exp, ["BB"], ["EP"])

        def S5():
            DVE(lambda e: e.tensor_tensor(out=KHT[par2], in0=KK, in1=DD, op=ALU.mult), ["KK", "DD"], [("KHT", par2)])

        def S6():
            ACT(SG, BB, AF.Exp, ["BB"], sgk, scale=-1.0)

        def S7():
            DVE(lambda e: e.tensor_tensor(out=KT[hp], in0=KK, in1=SG, op=ALU.mult), ["KK"] + sgk, [("KT", hp)])

        def f_step(hf):
            fm(wa3, slice(0, 128), hf, ka,
               lambda b: ACT(SG[:, hf * 512:(hf + 1) * 512], PS[b][:, :], AF.Sigmoid, [("ps", b)], [("SG", hf)]))

        def inp_step(hf):
            fm(wa3, slice(128, 256), hf, ka,
               lambda b: ACT(INPT[:, hf * 512:(hf + 1) * 512], PS[b][:, :], AF.Copy, [("ps", b)], [("INPT", hf)]))

        def tv_step():
            b = rot()
            ptb = PS[b][:, :].bitcast(BF16)
            for i in range(8):
                P.op("pe", lambda e, i=i, ptb=ptb: e.transpose(ptb[:, i * 128:(i + 1) * 128], INPT[:, i * 128:(i + 1) * 128], IDB),
                     [("INPT", i // 4), "IDB"], [("ps", b)], signal=(i == 7))
            ACT(VTOK[set3], ptb, AF.Copy, [("ps", b)], [("VTOK", set3)])

        f_step(0)
        yield
        f_step(1)
        S1()
        yield
        inp_step(0)
        S2()
        yield
        inp_step(1)
        S3()
        yield
        if s == 0:
            tv_step()
            S4()
            S5()
            yield
            return
        w_done()
        wb, kb = w_next()
        wb3 = c3(wb, 256)

        def q_step(hf):
            fm(wb3, slice(0, 128), hf, kb,
               lambda b: ACT(QQ[:, hf * 512:(hf + 1) * 512], PS[b][:, :], AF.Silu, [("ps", b)], [("QQ", hf)]))

        def g_step(hf):
            fm(wb3, slice(128, 256), hf, kb,
               lambda b: ACT(ZG[hp][:, hf * 512:(hf + 1) * 512], PS[b][:, :], AF.Silu, [("ps", b)], [("ZG", hp, hf)]))

        tv_step()
        S4()
        S6()
        yield
        q_step(0)
        S5()
        S7()
        yield
        q_step(1)
        yield
        g_step(0)
        yield
        g_step(1)
        DVE(lambda e: e.tensor_tensor(out=QT[hp], in0=QQ, in1=EP, op=ALU.mult),
            [("QQ", 0), ("QQ", 1), "EP"], [("QT", hp)])
        w_done()
        yield

    def tk(ui):
        par2 = ui % 2
        b = rot()
        ptb = PS[b][:, :].bitcast(BF16)
        for i in range(8):
            P.op("pe", lambda e, i=i, ptb=ptb: e.transpose(ptb[:, i * 128:(i + 1) * 128], KHT[par2][:, i * 128:(i + 1) * 128], IDB),
                 [("KHT", par2), "IDB"], [("ps", b)], signal=(i == 7))
        ACT(KH0[par2][0:64, :], ptb[0:64, :], AF.Copy, [("ps", b)], [("KH0", par2)])
        ACT(KH1[par2][64:128, :], ptb[64:128, :], AF.Copy, [("ps", b)], [("KH1", par2)])

    def rec_gen(ui, pending):
        h, s = units[ui]
        hp, par2, set3 = h % 2, ui % 2, ui % 3
        kh = [KH0[par2].rearrange("p (i d) -> p i d", d=128), KH1[par2].rearrange("p (i d) -> p i d", d=128)]
        v3 = VTOK[set3].rearrange("p (i d) -> p i d", d=128)
        if s == 0:
            DVE(lambda e: e.memset(SF[0], 0.0), [], [("SF", 0)])

        def stageA(g):
            bk = KVB[kvb_i["i"] % 2]
            kvb_i["i"] += 1
            todo = [cc for cc in range(4) if not (s == 1 and 4 * g + cc == 15)]
            for cc in todo:
                c = 4 * g + cc
                MM(PS[bk][:, cc * 128:(cc + 1) * 128], kh[c % 2][:, c // 2, :], v3[:, c // 2, :], True, True,
                   [("KH0", par2), ("KH1", par2), ("VTOK", set3)], [("ps", bk)])
            for cc in todo:
                c = 4 * g + cc
                k = 16 * s + c
                kv = PS[bk][:, cc * 128:(cc + 1) * 128]
                g_ = GAM[set3][:, c:c + 1]
                DVE(lambda e, k=k, g_=g_, kv=kv: e.scalar_tensor_tensor(
                    out=SF[(k + 1) % 4], in0=SF[k % 4], scalar=g_, in1=kv, op0=ALU.mult, op1=ALU.add),
                    [("SF", k % 4), ("GAM", set3), ("ps", bk)], [("SF", (k + 1) % 4)])
                if 16 <= k + 1 <= 31:
                    P.op("pool", lambda e, k=k: e.tensor_copy(out=SBFR[(k + 1) % 16], in_=SF[(k + 1) % 4]),
                         [("SF", (k + 1) % 4)], [("SBFR", (k + 1) % 16)])

        def attn(i):
            tk_ = slice(i * 128, (i + 1) * 128)
            b = rot()
            MM(PS[b][:, 0:128], KT[hp][:, tk_], QT[hp][:, tk_], True, True, [("KT", hp), ("QT", hp)], [("ps", b)])
            DVE(lambda e, i=i, b=b: e.tensor_tensor(out=ATS[i % 2], in0=PS[b][:, 0:128], in1=MASKF, op=ALU.mult),
                [("ps", b), "CV"], [("ATS", i % 2)])

        def intra(i):
            ob = 4 + i // 4
            ocol = (i % 4) * 128
            MM(PS[ob][:, ocol:ocol + 128], v3[:, i, :], ATS[i % 2], i % 4 == 0, False,
               [("VTOK", set3), ("ATS", i % 2)], [("ps", ob)], signal=True)

        def stageC(g):
            for cc in range(4):
                c = 4 * g + cc
                k = 16 + c
                ob = 4 + c // 8
                ocol = (c % 8) * 64
                MM(PS[ob][:, ocol:ocol + 64], SBFR[k % 16], QT[hp][:, c * 64:(c + 1) * 64], False, c % 8 == 7,
                   [("SBFR", k % 16), ("QT", hp)], [("ps", ob)], signal=True)

        if s == 0:
            for g in range(4):
                stageA(g)
                yield
        else:
            seq = [("A", 0), ("B", 0), ("A", 1), ("B", 1), ("C", 0), ("A", 2), ("B", 2), ("C", 1), ("A", 3), ("B", 3),
                   ("C", 2), ("C", 3)]
            for kind, g in seq:
                if kind == "A":
                    stageA(g)
                    yield
                elif kind == "B":
                    attn(2 * g)
                    attn(2 * g + 1)
                    yield
                    intra(2 * g)
                    intra(2 * g + 1)
                else:
                    stageC(g)
                    yield
            for hf in range(2):
                ACT(SQ[0][:, hf * 512:(hf + 1) * 512], PS[4 + hf][:, :], AF.Square, [("ps", 4 + hf)], [("SQ", 0, hf)])
        if ui + 1 < len(units):
            tk(ui + 1)
        yield
        if s == 1:
            for hf in range(2):
                sl = slice(hf * 512, (hf + 1) * 512)
                b = rot()
                MM(PS[b][:, :], ONESB, SQ[0][:, sl], True, True, [("SQ", 0, hf), "ONESB"], [("ps", b)])
                r = RSTD[:, sl]
                ACT(r, PS[b][:, :], AF.Ln, [("ps", b), "EPSC"], [("RSTD", hf)], scale=1.0 / 128, bias=EPSC)
                ACT(r, r, AF.Exp, [("RSTD", hf)], [("RSTD", hf)], scale=-0.5)
                DVE(lambda e, r=r, hf=hf: e.scalar_tensor_tensor(
                    out=r, in0=PS[4 + hf][:, :], scalar=CV[:, C_HN:C_HN + 1], in1=r, op0=ALU.mult, op1=ALU.mult),
                    [("ps", 4 + hf), ("RSTD", hf), "CV"], [("RSTD", hf)])
                DVE(lambda e, r=r, sl=sl: e.tensor_tensor(out=OZ[:, h, sl], in0=r, in1=ZG[hp][:, sl], op=ALU.mult),
                    [("RSTD", hf), ("ZG", hp, hf)], [kRQ(h, hf)])
            yield

    def drain(g):
        for _ in g:
            pass

    def u_gen():
        for j in range(4):
            w, k = w_next()
            w3 = c3(w, 256)
            for jj in range(2):
                n = 2 * j + jj
                for hf in range(2):
                    b = rot()
                    for c in range(16):
                        MM(PS[b][:, :], w3[:, c, jj * 128:(jj + 1) * 128], HT[:, c, hf * 512:(hf + 1) * 512],
                           c == 0, c == 15, [k, kRH(c, hf)], [("ps", b)])
                    ACT(UU[:, n, hf * 512:(hf + 1) * 512], PS[b][:, :], AF.Gelu, [("ps", b)], [kRQ(8 + n, hf)])
                    if jj == 1 and hf == 1:
                        w_done()
                    yield

    ug = u_gen()
    NU = len(units)
    drain(proj_gen(0))
    drain(proj_gen(1))
    tk(0)
    for ui in range(NU):
        rg = rec_gen(ui, None)
        n_y = 5 if units[ui][1] == 0 else 14
        if ui + 2 < NU:
            pg = proj_gen(ui + 2)
            n_p = 5 if units[ui + 2][1] == 0 else 9
        else:
            pg, n_p = ug, (8 if units[ui][1] == 0 else 8)
        yd, pd = 0, 0
        for _ in rg:
            yd += 1
            while pd < n_p and pd * n_y < yd * n_p:
                next(pg, None)
                pd += 1
        if pg is not ug:
            drain(pg)
    dump("oz", RQ[:, 0:8192], [kRQ(h, hf) for h in range(8) for hf in range(2)])

    TMPA = [RX[:, 14336 + i * 512:14336 + (i + 1) * 512] for i in range(4)]
    TMPS = [RX[:, 10368 + i * 512:10368 + (i + 1) * 512] for i in range(2)]
    tmpa_keys = [("TMPA", i) for i in range(4)]
    A2K = A2_KEYS + [("TMPS", 0), ("TMPS", 1)]
    fence(A1_KEYS, A2K + tmpa_keys)
    bcast = lambda r: bass.AP(ln_d.tensor, r * 1024, [[0, 128], [1, 1024]])
    spdma(GLN, bcast(0), [], ["GLN"], "gln")
    spdma(BLN, bcast(1), [], ["BLN"], "bln")
    spdma(BSP, bcast(2), [], ["BSP"], "bsp")
    pooldma(WSP, wsp_d, [], ["WSP"], "wsp")
    wsp3 = WSP.rearrange("p (h t) -> p h t", t=128)
    P.op("dve", lambda e: e.memset(wsp3[64:128, :, 0:64], 0.0), [], ["WSP"])
    rr = {"i": 0}

    def pair4():
        b = (rr["i"] % 3) * 2
        rr["i"] += 1
        return b

    spi = {"i": 0}

    def sp_group(h, g4):
        b = pair4() + (spi["i"] % 2)
        ti = spi["i"] % 2
        spi["i"] += 1
        for ii in range(4):
            i = g4 * 4 + ii
            o = PS[b][:, ii * 128:(ii + 1) * 128]
            MM(o, VTG[:, i, h * 128:(h + 1) * 128], wsp3[:, h, :], True, True, [("VTG", i), "WSP"], [("ps", b)])
        sl = slice(g4 * 512, (g4 + 1) * 512)
        tmp = TMPS[ti]
        bs = BSP[:, h * 128:h * 128 + 1]
        bs_bc = bass.AP(bs.tensor, bs.offset, [[bs.ap[0][0], 128], [0, 4], [1, 128]])
        DVE(lambda e: e.tensor_tensor(
            out=tmp.rearrange("p (i t) -> p i t", t=128), in0=PS[b][:, :].rearrange("p (i t) -> p i t", t=128),
            in1=bs_bc, op=ALU.add), [("ps", b), "BSP"], [("TMPS", ti)])
        DVE(lambda e: e.tensor_tensor(out=UU[:, h, sl], in0=UU[:, h, sl], in1=tmp, op=ALU.mult),
            [kRQ(8 + h, g4), ("TMPS", ti)], [kRQ(8 + h, g4)])

    drain(ug)
    wv = [w_next() for _ in range(4)]
    for i in range(8):
        b = pair4()
        gv = GVT[i % 2]
        gk = ("GVT", i % 2)
        for j in range(4):
            w3 = c3(wv[j][0], 256)
            o = PS[b + j // 2][:, (j % 2) * 256:(j % 2 + 1) * 256]
            for c in range(16):
                MM(o, HT[:, c, i * 128:(i + 1) * 128], w3[:, c, :], c == 0, c == 15,
                   [wv[j][1], kRH(c, i // 4)], [("ps", b + j // 2)])
        for hf in range(2):
            ACT(gv[:, hf * 512:(hf + 1) * 512], PS[b + hf][:, :], AF.Gelu, [("ps", b + hf)], [gk])
        for hf in range(2):
            DVE(lambda e, hf=hf, gv=gv: e.bn_stats(out=STATS[:, hf * 6:(hf + 1) * 6], in_=gv[:, hf * 512:(hf + 1) * 512]),
                [gk], ["STATS"])
        DVE(lambda e: e.bn_aggr(out=STATS[:, 16:18], in_=STATS[:, 0:12]), ["STATS"], ["STATS"])
        ACT(STATS[:, 18:19], STATS[:, 17:18], AF.Ln, ["STATS", "EPSC"], ["STATS"], bias=EPSC)
        ACT(STATS[:, 18:19], STATS[:, 18:19], AF.Exp, ["STATS"], ["STATS"], scale=-0.5)
        DVE(lambda e, gv=gv: e.tensor_scalar(out=gv, in0=gv, scalar1=STATS[:, 16:17], scalar2=STATS[:, 18:19],
                                             op0=ALU.subtract, op1=ALU.mult), [gk, "STATS"], [gk])
        DVE(lambda e, gv=gv: e.tensor_tensor(out=gv, in0=gv, in1=GLN, op=ALU.mult), [gk, "GLN"], [gk])
        DVE(lambda e, gv=gv, i=i: e.tensor_tensor(out=VTG[:, i, :], in0=gv, in1=BLN, op=ALU.add), [gk, "BLN"], [("VTG", i)])
        if i >= 4:
            sp_group(2 * (i - 4), 0)
            sp_group(2 * (i - 4) + 1, 0)
    w_done(4)

    rx_lo = [kRX(c, h) for c in range(14) for h in range(2)]
    rx_hi = [kRX(c, h) for c in range(14, 16) for h in range(2)]
    xv_ = xT_d.rearrange("(c p) t -> p c t", p=128)
    bset = {"i": 0}
    mw = {}

    def merge_group(n, hf):
        if hf == 0:
            mw["g"] = w_next()
            mw["ab"] = w_next()
        wg, kg = mw["g"]
        wab, kab = mw["ab"]
        g3 = c3(wg, 256)
        wa3 = c3(wab[:, 0:1024], 128)
        wb3 = c3(wab[:, 1024:2048], 128)
        b0 = (bset["i"] % 2) * 4
        bset["i"] += 1
        ts = slice(hf * 512, (hf + 1) * 512)
        for c in range(16):
            MM(PS[b0][:, :], g3[:, c, 0:128], HT[:, c, ts], c == 0, c == 15, [kg, kRH(c, hf)], [("ps", b0)])
        for c in range(8):
            MM(PS[b0 + 1][:, :], wa3[:, c, :], OZ[:, c, ts], c == 0, c == 7, [kab, kRQ(c, hf)], [("ps", b0 + 1)])
        for c in range(16):
            MM(PS[b0 + 2][:, :], g3[:, c, 128:256], HT[:, c, ts], c == 0, c == 15, [kg, kRH(c, hf)], [("ps", b0 + 2)])
        for c in range(8):
            MM(PS[b0 + 3][:, :], wb3[:, c, :], UU[:, c, ts], c == 0, c == 7, [kab, kRQ(8 + c, hf)], [("ps", b0 + 3)])
        ta, tb = TMPA[(b0 // 4) * 2], TMPA[(b0 // 4) * 2 + 1]
        ka_, kb_ = ("TMPA", (b0 // 4) * 2), ("TMPA", (b0 // 4) * 2 + 1)
        ACT(ta, PS[b0][:, :], AF.Sigmoid, [("ps", b0)], [ka_])
        DVE(lambda e: e.tensor_tensor(out=ta, in0=ta, in1=PS[b0 + 1][:, :], op=ALU.mult), [ka_, ("ps", b0 + 1)], [ka_])
        ACT(tb, PS[b0 + 2][:, :], AF.Sigmoid, [("ps", b0 + 2)], [kb_])
        DVE(lambda e: e.tensor_tensor(out=tb, in0=tb, in1=PS[b0 + 3][:, :], op=ALU.mult), [kb_, ("ps", b0 + 3)], [kb_])
        DVE(lambda e: e.tensor_tensor(out=RPv[:, n, ts], in0=ta, in1=tb, op=ALU.add), [ka_, kb_], [kRP(n, hf)])
        if hf == 1:
            w_done(2)

    merge_group(0, 0)
    for h in range(8):
        sp_group(h, 1)
    dump("prod", RQ[:, 8192:16384], [kRQ(8 + h, hf) for h in range(8) for hf in range(2)])
    fence(A2K, rx_lo)
    for g, (c0, c1) in enumerate([(0, 4), (4, 8), (8, 11), (11, 14)]):
        spdma(XT[:, c0:c1, :], xv_[:, c0:c1, :], [], [kRX(c, h) for c in range(c0, c1) for h in range(2)], f"xr{g}")
    merge_group(0, 1)
    for n in range(1, 16):
        merge_group(n, 0)
        merge_group(n, 1)
    dump("merged", RP[:, :], [kRP(c, h) for c in range(16) for h in range(2)])

    fence([("TMPA", i) for i in range(4)], rx_hi)
    spdma(XT[:, 14:16, :], xv_[:, 14:16, :], [], rx_hi, "xr4")

    def sq_inline(n, hf):
        qi = sqi["i"] % 4
        sqi["i"] += 1
        sq = SQ[qi // 2][:, (qi % 2) * 512:(qi % 2 + 1) * 512]
        sk = ("SQ", qi // 2, qi % 2)
        ACT(sq, XT[:, n, hf * 512:(hf + 1) * 512], AF.Square, [kRX(n, hf)], [sk])
        deferred.append(lambda: MM(PS[6 + hf][:, :], ONESB, sq, n == 0, n == 15, [sk, "ONESB"], [("ps", 6 + hf)],
                                   signal=True))

    deferred = []

    def run_deferred(keep):
        while len(deferred) > keep:
            deferred.pop(0)()

    def rstd_from(bank0):
        for h in range(2):
            r = RSTD[:, h * 512:(h + 1) * 512]
            ACT(r, PS[bank0 + h][:, :], AF.Ln, [("ps", bank0 + h), "EPSC"], [("RSTD", h)], scale=1.0 / D, bias=EPSC)
            ACT(r, r, AF.Exp, [("RSTD", h)], [("RSTD", h)], scale=-0.5)

    def proj_add(src, skey, kc, stats=False, gnext=None):
        for j in range(8):
            w, k = w_next()
            w3 = c3(w, 256)
            for jj in range(2):
                n = 2 * j + jj
                b = pair4()
                for hf in range(2):
                    ts = slice(hf * 512, (hf + 1) * 512)
                    run_deferred(2)
                    for c in range(kc):
                        MM(PS[b + hf][:, :], w3[:, c, jj * 128:(jj + 1) * 128], src[:, c, ts], c == 0, c == kc - 1,
                           [k, skey(c, hf)], [("ps", b + hf)])
                    DVE(lambda e, n=n, ts=ts, b=b, hf=hf: e.tensor_tensor(out=XT[:, n, ts], in0=XT[:, n, ts], in1=PS[b + hf][:, :],
                                                                        op=ALU.add), [kRX(n, hf), ("ps", b + hf)], [kRX(n, hf)])
                    if stats:
                        sq_inline(n, hf)
                    if gnext is not None:
                        ACT(HT[:, n, ts], XT[:, n, ts], AF.Copy, [kRX(n, hf), "CV"], [kRH(n, hf)],
                            scale=CV[:, gnext + n:gnext + n + 1])
            w_done()
        run_deferred(0)

    proj_add(RPv, kRP, 16, stats=True, gnext=C_GFFN)
    dump("x1", RX[:, :], rx_keys)

    rstd_from(6)
    for q in range(4):
        H1, hkey = (RPv, kRP) if q % 2 == 0 else (RQv, kRQ)
        for j in range(8):
            w, k = w_next()
            w3 = c3(w, 256)
            for jj in range(2):
                n = 2 * j + jj
                b = pair4()
                for hf in range(2):
                    ts = slice(hf * 512, (hf + 1) * 512)
                    for c in range(16):
                        MM(PS[b + hf][:, :], w3[:, c, jj * 128:(jj + 1) * 128], HT[:, c, ts], c == 0, c == 15,
                           [k, kRH(c, hf)], [("ps", b + hf)])
                    pb = PS[b + hf][:, :]
                    ACT(pb, pb, AF.Relu, [("ps", b + hf)], [("ps", b + hf)])
                    DVE(lambda e, pb=pb, ts=ts: e.tensor_tensor(out=pb, in0=pb, in1=RSTD[:, ts], op=ALU.mult),
                        [("ps", b + hf), ("RSTD", hf)], [("ps", b + hf)])
                    ACT(H1[:, n, ts], pb, AF.Square, [("ps", b + hf)], [hkey(n, hf)])
            w_done()
        proj_add(H1, hkey, 16, stats=(q == 3), gnext=(C_GPLE if q == 3 else None))
    dump("x2", RX[:, :], rx_keys)

    rstd_from(6)
    PTB = c3(RP[:, 0:2048])
    pooldma(PTB, pT_d.rearrange("(c p) t -> p c t", p=128), [], [kRP(0, 0), kRP(0, 1), kRP(1, 0), kRP(1, 1)], "ptb")
    wpp, kpp = w_next()
    WPP = RP[:, 4096:8192]
    kwpp = [kRP(c, h) for c in range(4, 8) for h in range(2)]
    P.op("act", lambda e: e.activation(out=WPP, in_=wpp, func=AF.Copy), [kpp], kwpp)
    w_done()
    wpp3 = c3(WPP, 2048)
    GT = [RP[:, 2048 + i * 1024:2048 + (i + 1) * 1024].bitcast(F32) for i in range(2)]
    for j in range(8):
        w, k = w_next()
        w3 = c3(w, 256)
        for jj in range(2):
            n = 2 * j + jj
            for hf in range(2):
                b = pair4()
                ts = slice(hf * 512, (hf + 1) * 512)
                run_deferred(2)
                for c in range(16):
                    MM(PS[b][:, :], w3[:, c, jj * 128:(jj + 1) * 128], HT[:, c, ts], c == 0, c == 15,
                       [k, kRH(c, hf)], [("ps", b)])
                for c in range(2):
                    MM(PS[b + 1][:, :], wpp3[:, c, n * 128:(n + 1) * 128], PTB[:, c, ts], c == 0, c == 1,
                       kwpp + [kRP(c, hf)], [("ps", b + 1)])
                gt = GT[hf]
                gk = kRP(2, hf)
                DVE(lambda e, gt=gt, b=b, ts=ts: e.tensor_tensor(out=gt, in0=PS[b][:, :], in1=RSTD[:, ts], op=ALU.mult),
                    [("ps", b), ("RSTD", hf)], [gk])
                ACT(gt, gt, AF.Sigmoid, [gk], [gk])
                DVE(lambda e, gt=gt, b=b: e.tensor_tensor(out=gt, in0=gt, in1=PS[b + 1][:, :], op=ALU.mult),
                    [gk, ("ps", b + 1)], [gk])
                DVE(lambda e, gt=gt, n=n, ts=ts: e.tensor_tensor(out=XT[:, n, ts], in0=XT[:, n, ts], in1=gt, op=ALU.add),
                    [gk, kRX(n, hf)], [kRX(n, hf)])
                sq_inline(n, hf)
        w_done()
    run_deferred(0)

    rstd_from(6)
    ov = out_d.rearrange("(c p) t -> p c t", p=128)
    TMPF = [RP[:, 8192 + i * 1024:8192 + (i + 1) * 1024].bitcast(F32) for i in range(2)]
    for g in range(8):
        for c in range(2 * g, 2 * g + 2):
            for h in range(2):
                sl = slice(h * 512, (h + 1) * 512)
                it = 2 * c + h
                if it % 3 == 2:
                    ti = (it // 3) % 2
                    ACT(TMPF[ti], XT[:, c, sl], AF.Copy, [kRX(c, h), "CV"], [kRP(8, ti)], scale=CV[:, C_GFIN + c:C_GFIN + c + 1])
                    P.op("pool", lambda e, c=c, sl=sl, ti=ti: e.tensor_tensor(out=XT[:, c, sl], in0=TMPF[ti], in1=RSTD[:, sl],
                                                                         op=ALU.mult),
                         [kRP(8, ti), ("RSTD", h)], [kRX(c, h)])
                    continue
                DVE(lambda e, c=c, sl=sl: e.scalar_tensor_tensor(
                    out=XT[:, c, sl], in0=XT[:, c, sl], scalar=CV[:, C_GFIN + c:C_GFIN + c + 1], in1=RSTD[:, sl],
                    op0=ALU.mult, op1=ALU.mult), [kRX(c, h), ("RSTD", h), "CV"], [kRX(c, h)])
        t = spdma(ov[:, g * 2:(g + 1) * 2, :], XT[:, g * 2:(g + 1) * 2, :],
                  [kRX(c, h) for c in range(g * 2, g * 2 + 2) for h in range(2)], [], f"out{g}")
        final_toks.append(t)
    for t in final_toks:
        P.wait("sp", t)
    assert wstate["cur"] == NBLK and wstate["done"] == NBLK, wstate
    assert P.simulate(), "deadlock in plan"
    print("plan ops", {e: len(P.ops[e]) for e in ENGS}, "sems", {e: P.echan[e].val for e in P.echan})
    P.emit(block)
    es.close()
    return nc


def make_in_maps(inputs):
    f = lambda a: np.ascontiguousarray(np.asarray(a, dtype=np.float32))
    x = f(inputs["x"])
    p = f(inputs["p"])[0]
    ws = {
        "w_in": f(inputs["w_in"])[0], "w_a_out": f(inputs["w_a_out"])[0], "w_b_out": f(inputs["w_b_out"])[0],
        "w_o": f(inputs["w_o"])[0], "w_ff1": f(inputs["w_ff1"])[0], "w_ff2": f(inputs["w_ff2"])[0],
        "w_ple_gate": f(inputs["w_ple_gate"])[0], "w_ple_proj": f(inputs["w_ple_proj"])[0],
    }
    wb = pack_weights(ws)
    cv = np.zeros((128, NCV), np.float32)
    fm = lambda v: np.ascontiguousarray(v.reshape(-1, 128).T)
    cv[:, C_GMIX:C_GMIX + 16] = fm(f(inputs["norm_mix"])[0])
    cv[:, C_GFFN:C_GFFN + 16] = fm(f(inputs["norm_ffn"])[0])
    cv[:, C_GPLE:C_GPLE + 16] = fm(f(inputs["norm_ple"])[0])
    cv[:, C_GFIN:C_GFIN + 16] = fm(f(inputs["norm_final"]))
    lbl = f(inputs["lb_logits"])
    cv[:, C_L0:C_L0 + 8] = fm(lbl[0])
    cv[:, C_L1:C_L1 + 8] = fm(lbl[1])
    cv[:, C_HN] = f(inputs["hgrn_norm"])[0]
    cv[:, C_ID:C_ID + 128] = np.eye(128, dtype=np.float32)
    s_ = np.arange(128)[:, None]
    t_ = np.arange(128)[None, :]
    cv[:, C_MASK:C_MASK + 128] = ((s_ <= t_) & (s_ // 64 == t_ // 64)).astype(np.float32)
    lnrow = np.stack([f(inputs["gmlp_ln_g"])[0], f(inputs["gmlp_ln_b"])[0], f(inputs["b_spatial"])[0].reshape(-1)])
    wspT = np.ascontiguousarray(f(inputs["w_spatial"])[0].transpose(2, 0, 1).reshape(128, 1024))
    maps = []
    for core in range(8):
        b, half = core // 2, core % 2
        xs = x[b, half * T:(half + 1) * T, :]
        xp = x[b, 0:T, :] if half == 1 else np.zeros((T, D), np.float32)
        maps.append({
            "xT": np.ascontiguousarray(xs.T), "xpT": np.ascontiguousarray(xp.T),
            "pT": np.ascontiguousarray(p[b, half * T:(half + 1) * T, :].T),
            "wb": wb, "cvec": cv, "lnrow": np.ascontiguousarray(lnrow), "wspT": wspT,
        })
    return maps


_NC_CACHE = {}


def kernel(**inputs):
    maps = make_in_maps(inputs)
    if "nc" not in _NC_CACHE:
        _NC_CACHE["nc"] = build_program()
    res = run_bass_kernel_spmd(_NC_CACHE["nc"], maps, core_ids=list(range(8)))
    out = np.empty((4, 2 * T, D), np.float32)
    for core in range(8):
        b, half = core // 2, core % 2
        out[b, half * T:(half + 1) * T, :] = res.results[core]["outT"].T
    return out
```

```python
from contextlib import ExitStack

import numpy as np
import concourse.bass as bass
import concourse.mybir as mybir
from concourse.bass_utils import run_bass_kernel_spmd

F32 = mybir.dt.float32
BF16 = mybir.dt.bfloat16
AF = mybir.ActivationFunctionType
ALU = mybir.AluOpType

D = 2048
T = 1024
NH = 8
EPS = 1e-6
NSLOT = 4
BLK = 4096
ENGS = ["pe", "act", "dve", "pool", "sp"]

C_GMIX, C_GFFN, C_GPLE, C_GFIN, C_L0, C_L1, C_HN, C_ID, C_MASK, NCV = 0, 16, 32, 48, 64, 72, 80, 96, 224, 352


class Chan:
    def __init__(self, sem, step, name):
        self.sem, self.step, self.val, self.name = sem, step, 0, name


class Planner:
    def __init__(self, nc):
        self.nc = nc
        self.ops = {e: [] for e in ENGS}
        self.echan = {}
        self.waited = {e: {} for e in ENGS}
        self.res = {}
        for e in ["pe", "act", "dve", "pool"]:
            self.echan[e] = self.new_chan(1, "c_" + e)

    def new_chan(self, step, name):
        return Chan(self.nc.alloc_semaphore(name=name), step, name)

    def _deps(self, eng, reads, writes):
        deps = {}

        def add(tok, raw):
            if tok is None:
                return
            ch, v = tok
            if ch is self.echan.get(eng) and eng == "pe":
                return
            if deps.get(ch, 0) < v:
                deps[ch] = v

        for k in reads:
            r = self.res.get(k)
            if r is not None:
                add(r[0], True)
        for k in writes:
            r = self.res.get(k)
            if r is not None:
                add(r[0], False)
                for ch, v in r[1].items():
                    add((ch, v), False)
        waits = []
        for ch, v in deps.items():
            if self.waited[eng].get(ch, 0) < v:
                self.waited[eng][ch] = v
                waits.append((ch, v))
        return waits

    def _record(self, tok, reads, writes):
        for k in writes:
            self.res[k] = [tok, {}]
        ch, v = tok
        for k in reads:
            r = self.res.setdefault(k, [None, {}])
            if r[1].get(ch, 0) < v:
                r[1][ch] = v

    def op(self, eng, fn, reads=(), writes=(), signal=True):
        waits = self._deps(eng, reads, writes)
        ch = self.echan[eng]
        if signal:
            ch.val += 1
            tok = (ch, ch.val)
        else:
            tok = (ch, ch.val + 1)
        self.ops[eng].append((waits, fn, (ch, 1) if signal else None))
        self._record(tok, reads, writes)
        return tok

    def dma(self, eng, chan, fn, reads=(), writes=(), after=()):
        waits = self._deps(eng, reads, writes)
        for ch, v in after:
            if self.waited[eng].get(ch, 0) < v:
                self.waited[eng][ch] = v
                waits.append((ch, v))
        chan.val += chan.step
        tok = (chan, chan.val)
        self.ops[eng].append((waits, fn, (chan, chan.step)))
        self._record(tok, reads, writes)
        return tok

    def wait(self, eng, tok):
        ch, v = tok
        if self.waited[eng].get(ch, 0) < v:
            self.waited[eng][ch] = v
            self.ops[eng].append(([(ch, v)], None, None))

    def simulate(self):
        val = {}
        pc = {e: 0 for e in ENGS}
        progress = True
        while progress:
            progress = False
            for e in ENGS:
                ops = self.ops[e]
                while pc[e] < len(ops):
                    waits, fn, inc = ops[pc[e]]
                    if any(val.get(ch.name, 0) < v for ch, v in waits):
                        break
                    if inc is not None:
                        val[inc[0].name] = val.get(inc[0].name, 0) + inc[1]
                    pc[e] += 1
                    progress = True
        stuck = {e: (pc[e], len(self.ops[e])) for e in ENGS if pc[e] < len(self.ops[e])}
        for e, (i, n) in stuck.items():
            waits, fn, inc = self.ops[e][i]
            print("STUCK", e, i, "/", n, [(ch.name, v, val.get(ch.name, 0)) for ch, v in waits])
        return not stuck

    def emit(self, block):
        handles = {"pe": block.tensor, "act": block.scalar, "dve": block.vector,
                   "pool": block.gpsimd, "sp": block.sync}
        for e in ENGS:
            ops = self.ops[e]
            if not ops:
                continue

            def body(engine, ops=ops):
                for waits, fn, inc in ops:
                    for ch, v in waits:
                        engine.wait_ge(ch.sem, v)
                    if fn is None:
                        continue
                    ins = fn(engine)
                    if inc is not None:
                        ins.then_inc(inc[0].sem, inc[1])

            handles[e](body)


def weight_blocks():
    blks = []
    for h in range(NH):
        blks.append(("w_in", 0, D, [(1024 + h * 128, 128), (2048 + h * 128, 128)]))
        blks.append(("w_in", 0, D, [(h * 128, 128), (3072 + h * 128, 128)]))
    for j in range(4):
        blks.append(("w_in", 0, D, [(4096 + j * 256, 256)]))
    for j in range(4):
        blks.append(("w_in", 0, D, [(5120 + j * 256, 256)]))
    for n in range(16):
        blks.append(("w_in", 0, D, [(6144 + n * 128, 128), (8192 + n * 128, 128)]))
        blks.append(("w_ab", 0, 1024, [(n * 128, 128)]))
    for j in range(8):
        blks.append(("w_o", 0, D, [(j * 256, 256)]))
    for q in range(4):
        for j in range(8):
            blks.append(("w_ff1", 0, D, [(q * 2048 + j * 256, 256)]))
        for j in range(8):
            blks.append(("w_ff2", q * 2048, D, [(j * 256, 256)]))
    blks.append(("w_ple_proj", 0, 256, [(0, 2048)]))
    for j in range(8):
        blks.append(("w_ple_gate", 0, D, [(j * 256, 256)]))
    return blks


def pack_weights(ws):
    blks = weight_blocks()
    out = np.zeros((len(blks), 128, BLK), np.float32)
    for i, (name, r0, K, cols) in enumerate(blks):
        kc = K // 128
        if name == "w_ab":
            c0, ncl = cols[0]
            a = ws["w_a_out"][:, c0:c0 + ncl].reshape(kc, 128, ncl).transpose(1, 0, 2).reshape(128, kc * ncl)
            b = ws["w_b_out"][:, c0:c0 + ncl].reshape(kc, 128, ncl).transpose(1, 0, 2).reshape(128, kc * ncl)
            out[i, :, :kc * ncl] = a
            out[i, :, kc * ncl:2 * kc * ncl] = b
            continue
        W = ws[name]
        sub = np.concatenate([W[r0:r0 + K, c0:c0 + ncl] for c0, ncl in cols], axis=1)
        ncl = sub.shape[1]
        out[i, :, :kc * ncl] = sub.reshape(kc, 128, ncl).transpose(1, 0, 2).reshape(128, kc * ncl)
    return out


NBLK = len(weight_blocks())


def build_program(dbg=()):
    nc = bass.Bass("TRN2", target_bir_lowering=False)
    xT_d = nc.dram_tensor("xT", [D, T], F32, kind="ExternalInput").ap()
    xpT_d = nc.dram_tensor("xpT", [D, T], F32, kind="ExternalInput").ap()
    pT_d = nc.dram_tensor("pT", [256, T], F32, kind="ExternalInput").ap()
    wb_d = nc.dram_tensor("wb", [NBLK, 128, BLK], F32, kind="ExternalInput").ap()
    cv_d = nc.dram_tensor("cvec", [128, NCV], F32, kind="ExternalInput").ap()
    ln_d = nc.dram_tensor("lnrow", [3, 1024], F32, kind="ExternalInput").ap()
    wsp_d = nc.dram_tensor("wspT", [128, 1024], F32, kind="ExternalInput").ap()
    out_d = nc.dram_tensor("outT", [D, T], F32, kind="ExternalOutput").ap()
    dbg_d = {}
    for name, shape in dbg:
        dbg_d[name] = nc.dram_tensor("dbg_" + name, list(shape), F32, kind="ExternalOutput").ap()

    es = ExitStack()
    RX = es.enter_context(nc.sbuf_tensor("RX", [128, 16384], F32))
    RH = es.enter_context(nc.sbuf_tensor("RH", [128, 16384], BF16))
    RP = es.enter_context(nc.sbuf_tensor("RP", [128, 16384], BF16))
    RQ = es.enter_context(nc.sbuf_tensor("RQ", [128, 16384], BF16))
    RW = es.enter_context(nc.sbuf_tensor("RW", [128, NSLOT * BLK], BF16))
    RC = es.enter_context(nc.sbuf_tensor("RC", [128, 3200], F32))
    PS = [es.enter_context(nc.psum_tensor(f"ps{i}", [128, 512], F32)) for i in range(8)]
    P = Planner(nc)
    block = es.enter_context(nc.Block())

    def c3(ap, t=1024):
        return ap.rearrange("p (c t) -> p c t", t=t)

    XT = c3(RX[:, :])
    HT = c3(RH[:, :])
    RPv = c3(RP[:, :])
    RQv = c3(RQ[:, :])
    WS = [RW[:, s * BLK:(s + 1) * BLK] for s in range(NSLOT)]
    PT = PS[7][:, :].bitcast(BF16)

    CV = RC[:, 0:NCV]
    RSTD = RC[:, 352:1376]
    SQ = [RC[:, 1376 + i * 512:1376 + (i + 1) * 512].bitcast(BF16) for i in range(2)]
    RMASK = RC[:, 2400:2912].bitcast(BF16)
    ONESB = RC[:, 2912:2976].bitcast(BF16)
    IDB = RC[:, 2976:3040].bitcast(BF16)
    LB = RC[:, 3040:3048]
    OML = RC[:, 3048:3056]
    LBM1 = RC[:, 3056:3064]
    ONESF = RC[:, 3064:3192]
    MISC = RC[:, 3192:3200]
    MASKF = CV[:, C_MASK:C_MASK + 128]

    def rxf(i):
        return RX[:, i * 1024:(i + 1) * 1024]

    SG, KK, BB, DD, EP, QQ = [rxf(i) for i in range(6)]

    def rxb(i):
        return RX[:, 6144 + i * 512:6144 + (i + 1) * 512].bitcast(BF16)

    KT = [rxb(0), rxb(1)]
    QT = [rxb(2), rxb(3)]
    ZG = [rxb(4), rxb(5)]
    KHT = [rxb(6), rxb(7)]
    KH0 = [rxb(8), rxb(9)]
    KH1 = [rxb(10), rxb(11)]
    VTOK = [rxb(12), rxb(13), rxb(14)]
    INPT = rxb(15)
    o0 = 6144 + 16 * 512
    ATS = [RX[:, o0 + i * 64:o0 + (i + 1) * 64].bitcast(BF16) for i in range(2)]
    SF = [RX[:, o0 + 128 + i * 128:o0 + 128 + (i + 1) * 128] for i in range(4)]
    SBFR = [RX[:, o0 + 640 + i * 64:o0 + 640 + (i + 1) * 64].bitcast(BF16) for i in range(16)]
    GAM = [RX[:, o0 + 1664 + i * 16:o0 + 1664 + (i + 1) * 16] for i in range(3)]
    assert o0 + 1664 + 48 <= 16384
    OZ = c3(RQ[:, 0:8192])
    UU = c3(RQ[:, 8192:16384])
    GVT = [rxf(0), rxf(1)]
    GLN, BLN, BSP = rxf(2), rxf(3), rxf(4)
    WSP = RX[:, 5 * 1024:5 * 1024 + 512].bitcast(BF16)
    VTG = c3(RX[:, 6 * 1024:10 * 1024].bitcast(BF16))
    STATS = RX[:, 10 * 1024:10 * 1024 + 64]

    rx_keys = [("RX", c, h) for c in range(16) for h in range(2)]
    A1_KEYS = [("SG", 0), ("SG", 1), "KK", "BB", "DD", "EP", ("QQ", 0), ("QQ", 1), ("INPT", 0), ("INPT", 1)] + \
        [(n, i) for n in ["KT", "QT", "KHT", "KH0", "KH1", "ATS"] for i in range(2)] + \
        [("VTOK", i) for i in range(3)] + [("GAM", i) for i in range(3)] + [("SF", i) for i in range(4)] + \
        [("SBFR", i) for i in range(16)] + [("ZG", i, hf) for i in range(2) for hf in range(2)]
    A2_KEYS = [("GVT", 0), ("GVT", 1), "GLN", "BLN", "BSP", "WSP", "STATS"] + [("VTG", i) for i in range(8)]

    def ACT(out, in_, func, reads, writes, **kw):
        return P.op("act", lambda e: e.activation(out=out, in_=in_, func=func, **kw), reads, writes)

    def DVE(fn, reads, writes):
        return P.op("dve", fn, reads, writes)

    def MM(out, lhsT, rhs, start, stop, reads, writes, signal=None):
        return P.op("pe", lambda e: e.matmul(out, lhsT, rhs, start=start, stop=stop), reads, writes,
                    signal=stop if signal is None else signal)

    def fence(old, new):
        P.op("dve", lambda e: e.memset(MISC[:, 0:1], 0.0), reads=[], writes=list(old) + list(new) + ["MISC0"])

    def spdma(out, in_, reads, writes, name):
        ch = P.new_chan(16, name)
        return P.dma("sp", ch, lambda e: e.dma_start(out=out, in_=in_), reads, writes)

    def pooldma(out, in_, reads, writes, name):
        ch = P.new_chan(16, name)
        return P.dma("pool", ch, lambda e: e.dma_start(out=out, in_=in_), reads, writes)

    wch = [P.new_chan(16, f"w{s}") for s in range(NSLOT)]
    wstate = {"issued": 0, "done": 0, "cur": 0}

    def w_pump(limit=NBLK):
        while wstate["issued"] < min(NBLK, limit) and wstate["issued"] - NSLOT < wstate["done"]:
            j = wstate["issued"]
            s = j % NSLOT
            P.dma("pool", wch[s],
                  lambda e, j=j, s=s: e.dma_start(out=c3(WS[s], 2048), in_=c3(wb_d[j], 2048)),
                  reads=[], writes=[("ws", s)], after=wstate.get("gate", []) if 0 < j < NSLOT else [])
            wstate["issued"] += 1

    def w_next():
        i = wstate["cur"]
        wstate["cur"] += 1
        w_pump()
        assert wstate["issued"] > i, "weight block not issued (too many blocks held)"
        return WS[i % NSLOT], ("ws", i % NSLOT)

    def w_done(n=1):
        wstate["done"] += n
        w_pump()

    def dump(name, ap, reads):
        if name in dbg_d:
            if ap.dtype != F32:
                ap = ap.bitcast(F32)
            t = spdma(dbg_d[name], ap, reads, [], "dbg_" + name)
            final_toks.append(t)

    final_toks = []

    t_cv = spdma(CV, cv_d, [], ["CV"], "cv")
    EPSC = MISC[:, 1:2]
    P.op("dve", lambda e: e.memset(EPSC, EPS), [], ["EPSC"])
    P.op("dve", lambda e: e.memset(ONESB, 1.0), [], ["ONESB"])
    P.op("dve", lambda e: e.memset(ONESF, 1.0), [], ["ONESF"])
    P.op("dve", lambda e: e.memset(RMASK, 1.0), [], ["RMASK"])
    P.op("dve", lambda e: e.memset(RMASK[:, 0::64], 0.0), [], ["RMASK"])
    P.op("dve", lambda e: e.tensor_copy(out=IDB, in_=CV[:, C_ID:C_ID + 128]), ["CV"], ["IDB"])
    P.op("dve", lambda e: e.tensor_tensor(out=LBM1, in0=CV[:, C_L0:C_L0 + 8], in1=CV[:, C_L1:C_L1 + 8], op=ALU.subtract),
         ["CV"], ["LBM1"])
    ACT(LB, LBM1, AF.Sigmoid, ["LBM1"], ["LB"])
    P.op("dve", lambda e: e.tensor_scalar(out=OML, in0=LB, scalar1=-1.0, scalar2=1.0, op0=ALU.mult, op1=ALU.add),
         ["LB"], ["OML"])
    P.op("dve", lambda e: e.tensor_scalar(out=LBM1, in0=LB, scalar1=-1.0, scalar2=None, op0=ALU.add),
         ["LB"], ["LBM1"])

    sqi = {"i": 0}

    def rms_stats(src, src_key, bank0):
        for c in range(16):
            sq = SQ[sqi["i"] % 2]
            sk = ("SQ", sqi["i"] % 2)
            sqi["i"] += 1
            for h in range(2):
                ACT(sq[:, h * 512:(h + 1) * 512], src[:, c, h * 512:(h + 1) * 512], AF.Square,
                    [src_key(c, h)], [sk + (h,)])
            for h in range(2):
                MM(PS[bank0 + h][:, :], ONESB, sq[:, h * 512:(h + 1) * 512], c == 0, c == 15,
                   [sk + (h,), "ONESB"], [("ps", bank0 + h)], signal=True)
        for h in range(2):
            r = RSTD[:, h * 512:(h + 1) * 512]
            ACT(r, PS[bank0 + h][:, :], AF.Ln, [("ps", bank0 + h), "EPSC"], [("RSTD", h)], scale=1.0 / D, bias=EPSC)
            ACT(r, r, AF.Exp, [("RSTD", h)], [("RSTD", h)], scale=-0.5)

    def rms_apply(dst, dst_key, src, src_key, gcol):
        for h in range(2):
            for c in range(16):
                sl = slice(h * 512, (h + 1) * 512)
                DVE(lambda e, c=c, sl=sl: e.scalar_tensor_tensor(
                    out=dst[:, c, sl], in0=src[:, c, sl], scalar=CV[:, gcol + c:gcol + c + 1], in1=RSTD[:, sl],
                    op0=ALU.mult, op1=ALU.mult),
                    [src_key(c, h), ("RSTD", h), "CV"], [dst_key(c, h)])

    kRX = lambda c, h: ("RX", c, h)
    kRH = lambda c, h: ("RH", c, h)
    kRP = lambda c, h: ("RP", c, h)
    kRQ = lambda c, h: ("RQ", c, h)

    def load_x(src_d, tag):
        v = src_d.rearrange("(c p) t -> p c t", p=128)
        for g in range(4):
            spdma(XT[:, g * 4:(g + 1) * 4, :], v[:, g * 4:(g + 1) * 4, :], [],
                  [kRX(c, h) for c in range(g * 4, g * 4 + 4) for h in range(2)], f"x{tag}{g}")

    NB = [c3(RX[:, 0:8192], 512), c3(RX[:, 8192:16384], 512), c3(RQ[:, :].bitcast(F32), 512)]
    nb_keys = [("NB", i, c) for i in range(3) for c in range(16)]
    jobs = [(xpT_d, 0, RPv, kRP), (xpT_d, 1, RPv, kRP), (xT_d, 0, HT, kRH), (xT_d, 1, HT, kRH)]
    def job_load(ji):
        src_d, hf, dst, dkey = jobs[ji]
        buf, bi = NB[ji % 3], ji % 3
        v = src_d.rearrange("(c p) t -> p c t", p=128)
        toks = []
        for g in range(2):
            toks.append(spdma(buf[:, g * 8:(g + 1) * 8, :], v[:, g * 8:(g + 1) * 8, hf * 512:(hf + 1) * 512], [],
                              [("NB", bi, c) for c in range(g * 8, g * 8 + 8)], f"nx{ji}{g}"))
        return toks

    w_pump(1)
    gate = []
    for ji in range(3):
        gate += job_load(ji)
    for ji, (src_d, hf, dst, dkey) in enumerate(jobs):
        buf = NB[ji % 3]
        bi = ji % 3
        if ji == 3:
            gate += job_load(ji)
            wstate["gate"] = gate
            w_pump()
        bank = ji % 4
        for c in range(16):
            qi = sqi["i"] % 4
            sqi["i"] += 1
            sq = SQ[qi // 2][:, (qi % 2) * 512:(qi % 2 + 1) * 512]
            sk = ("SQ", qi // 2, qi % 2)
            ACT(sq, buf[:, c, :], AF.Square, [("NB", bi, c)], [sk])
            MM(PS[bank][:, :], ONESB, sq, c == 0, c == 15, [sk, "ONESB"], [("ps", bank)], signal=True)
        r = RSTD[:, (ji % 2) * 512:(ji % 2 + 1) * 512]
        rk = ("RSTD", ji % 2)
        ACT(r, PS[bank][:, :], AF.Ln, [("ps", bank), "EPSC"], [rk], scale=1.0 / D, bias=EPSC)
        ACT(r, r, AF.Exp, [rk], [rk], scale=-0.5)
        for c in range(16):
            DVE(lambda e, c=c, buf=buf, dst=dst, hf=hf, r=r: e.scalar_tensor_tensor(
                out=dst[:, c, hf * 512:(hf + 1) * 512], in0=buf[:, c, :], scalar=CV[:, C_GMIX + c:C_GMIX + c + 1], in1=r,
                op0=ALU.mult, op1=ALU.mult), [("NB", bi, c), rk, "CV"], [dkey(c, hf)])
    sqi["i"] = 0
    dump("ht", RH[:, :], [kRH(c, h) for c in range(16) for h in range(2)])
    fence(nb_keys, A1_KEYS + [kRQ(c, h) for c in range(16) for h in range(2)])

    units = [(h, s) for h in range(NH) for s in range(2)]
    uw = {}
    R4 = [0, 1, 2, 7]
    rot_i = {"i": 0}

    def rot():
        b = R4[rot_i["i"] % 4]
        rot_i["i"] += 1
        return b

    KVB = [3, 6]
    kvb_i = {"i": 0}
    for i in range(2):
        P.op("dve", lambda e, i=i: e.memset(KH0[i], 0.0), [], [("KH0", i)])
        P.op("dve", lambda e, i=i: e.memset(KH1[i], 0.0), [], [("KH1", i)])

    def proj_gen(ui):
        h, s = units[ui]
        hp, par2, set3 = h % 2, ui % 2, ui % 3
        src, skey = (RPv, kRP) if s == 0 else (HT, kRH)
        if s == 0:
            uw[h] = w_next()
        wa, ka = uw[h]
        wa3 = c3(wa, 256)

        def fm(w3, cs, hf, k, evac):
            b = rot()
            for c in range(16):
                MM(PS[b][:, :], w3[:, c, cs], src[:, c, hf * 512:(hf + 1) * 512], c == 0, c == 15,
                   [k, skey(c, hf)], [("ps", b)])
            evac(b)

        sgk = [("SG", 0), ("SG", 1)]
        bl = BB[:, 63:64]
        bl_bc = bass.AP(bl.tensor, bl.offset, [[bl.ap[0][0], 128], [64, 16], [0, 64]])
        b3 = BB.rearrange("p (c s) -> p c s", s=64)
        d3 = DD.rearrange("p (c s) -> p c s", s=64)

        def S1():
            DVE(lambda e: e.tensor_scalar(out=KK, in0=SG, scalar1=-1.0, scalar2=LBM1[:, h:h + 1], op0=ALU.add, op1=ALU.mult),
                sgk + ["LBM1"], ["KK"])
            DVE(lambda e: e.tensor_scalar(out=SG, in0=SG, scalar1=OML[:, h:h + 1], scalar2=LB[:, h:h + 1], op0=ALU.mult, op1=ALU.add),
                sgk + ["OML", "LB"], sgk)

        def S2():
            ACT(SG, SG, AF.Ln, sgk, sgk)

        def S3():
            DVE(lambda e: e.tensor_tensor_scan(out=BB, data0=RMASK, data1=SG, initial=0.0, op0=ALU.mult, op1=ALU.add),
                sgk + ["RMASK"], ["BB"])
            DVE(lambda e: e.tensor_tensor(out=d3, in0=bl_bc, in1=b3, op=ALU.subtract), ["BB"], ["DD"])

        def S4():
            ACT(DD, DD, AF.Exp, ["DD"], ["DD"])
            ACT(GAM[set3], BB[:, 63::64], AF.Exp, ["BB"], [("GAM", set3)])
            if s == 1:
                ACT(EP, BB, AF.Exp, ["BB"], ["EP"])

        def S5():
            DVE(lambda e: e.tensor_tensor(out=KHT[par2], in0=KK, in1=DD, op=ALU.mult), ["KK", "DD"], [("KHT", par2)])

        def S6():
            ACT(SG, BB, AF.Exp, ["BB"], sgk, scale=-1.0)

        def S7():
            DVE(lambda e: e.tensor_tensor(out=KT[hp], in0=KK, in1=SG, op=ALU.mult), ["KK"] + sgk, [("KT", hp)])

        def f_step(hf):
            fm(wa3, slice(0, 128), hf, ka,
               lambda b: ACT(SG[:, hf * 512:(hf + 1) * 512], PS[b][:, :], AF.Sigmoid, [("ps", b)], [("SG", hf)]))

        def inp_step(hf):
            fm(wa3, slice(128, 256), hf, ka,
               lambda b: ACT(INPT[:, hf * 512:(hf + 1) * 512], PS[b][:, :], AF.Copy, [("ps", b)], [("INPT", hf)]))

        def tv_step():
            b = rot()
            ptb = PS[b][:, :].bitcast(BF16)
            for i in range(8):
                P.op("pe", lambda e, i=i, ptb=ptb: e.transpose(ptb[:, i * 128:(i + 1) * 128], INPT[:, i * 128:(i + 1) * 128], IDB),
                     [("INPT", i // 4), "IDB"], [("ps", b)], signal=(i == 7))
            ACT(VTOK[set3], ptb, AF.Copy, [("ps", b)], [("VTOK", set3)])

        f_step(0)
        yield
        f_step(1)
        S1()
        yield
        inp_step(0)
        S2()
        yield
        inp_step(1)
        S3()
        yield
        if s == 0:
            tv_step()
            S4()
            S5()
            yield
            return
        w_done()
        wb, kb = w_next()
        wb3 = c3(wb, 256)

        def q_step(hf):
            fm(wb3, slice(0, 128), hf, kb,
               lambda b: ACT(QQ[:, hf * 512:(hf + 1) * 512], PS[b][:, :], AF.Silu, [("ps", b)], [("QQ", hf)]))

        def g_step(hf):
            fm(wb3, slice(128, 256), hf, kb,
               lambda b: ACT(ZG[hp][:, hf * 512:(hf + 1) * 512], PS[b][:, :], AF.Silu, [("ps", b)], [("ZG", hp, hf)]))

        tv_step()
        S4()
        S6()
        yield
        q_step(0)
        S5()
        S7()
        yield
        q_step(1)
        yield
        g_step(0)
        yield
        g_step(1)
        DVE(lambda e: e.tensor_tensor(out=QT[hp], in0=QQ, in1=EP, op=ALU.mult),
            [("QQ", 0), ("QQ", 1), "EP"], [("QT", hp)])
        w_done()
        yield

    def tk(ui):
        par2 = ui % 2
        b = rot()
        ptb = PS[b][:, :].bitcast(BF16)
        for i in range(8):
            P.op("pe", lambda e, i=i, ptb=ptb: e.transpose(ptb[:, i * 128:(i + 1) * 128], KHT[par2][:, i * 128:(i + 1) * 128], IDB),
                 [("KHT", par2), "IDB"], [("ps", b)], signal=(i == 7))
        ACT(KH0[par2][0:64, :], ptb[0:64, :], AF.Copy, [("ps", b)], [("KH0", par2)])
        ACT(KH1[par2][64:128, :], ptb[64:128, :], AF.Copy, [("ps", b)], [("KH1", par2)])

    def rec_gen(ui, pending):
        h, s = units[ui]
        hp, par2, set3 = h % 2, ui % 2, ui % 3
        kh = [KH0[par2].rearrange("p (i d) -> p i d", d=128), KH1[par2].rearrange("p (i d) -> p i d", d=128)]
        v3 = VTOK[set3].rearrange("p (i d) -> p i d", d=128)
        if s == 0:
            DVE(lambda e: e.memset(SF[0], 0.0), [], [("SF", 0)])

        def stageA(g):
            bk = KVB[kvb_i["i"] % 2]
            kvb_i["i"] += 1
            todo = [cc for cc in range(4) if not (s == 1 and 4 * g + cc == 15)]
            for cc in todo:
                c = 4 * g + cc
                MM(PS[bk][:, cc * 128:(cc + 1) * 128], kh[c % 2][:, c // 2, :], v3[:, c // 2, :], True, True,
                   [("KH0", par2), ("KH1", par2), ("VTOK", set3)], [("ps", bk)])
            for cc in todo:
                c = 4 * g + cc
                k = 16 * s + c
                kv = PS[bk][:, cc * 128:(cc + 1) * 128]
                g_ = GAM[set3][:, c:c + 1]
                DVE(lambda e, k=k, g_=g_, kv=kv: e.scalar_tensor_tensor(
                    out=SF[(k + 1) % 4], in0=SF[k % 4], scalar=g_, in1=kv, op0=ALU.mult, op1=ALU.add),
                    [("SF", k % 4), ("GAM", set3), ("ps", bk)], [("SF", (k + 1) % 4)])
                if 16 <= k + 1 <= 31:
                    P.op("pool", lambda e, k=k: e.tensor_copy(out=SBFR[(k + 1) % 16], in_=SF[(k + 1) % 4]),
                         [("SF", (k + 1) % 4)], [("SBFR", (k + 1) % 16)])

        def attn(i):
            tk_ = slice(i * 128, (i + 1) * 128)
            b = rot()
            MM(PS[b][:, 0:128], KT[hp][:, tk_], QT[hp][:, tk_], True, True, [("KT", hp), ("QT", hp)], [("ps", b)])
            DVE(lambda e, i=i, b=b: e.tensor_tensor(out=ATS[i % 2], in0=PS[b][:, 0:128], in1=MASKF, op=ALU.mult),
                [("ps", b), "CV"], [("ATS", i % 2)])

        def intra(i):
            ob = 4 + i // 4
            ocol = (i % 4) * 128
            MM(PS[ob][:, ocol:ocol + 128], v3[:, i, :], ATS[i % 2], i % 4 == 0, False,
               [("VTOK", set3), ("ATS", i % 2)], [("ps", ob)], signal=True)

        def stageC(g):
            for cc in range(4):
                c = 4 * g + cc
                k = 16 + c
                ob = 4 + c // 8
                ocol = (c % 8) * 64
                MM(PS[ob][:, ocol:ocol + 64], SBFR[k % 16], QT[hp][:, c * 64:(c + 1) * 64], False, c % 8 == 7,
                   [("SBFR", k % 16), ("QT", hp)], [("ps", ob)], signal=True)

        if s == 0:
            for g in range(4):
                stageA(g)
                yield
        else:
            seq = [("A", 0), ("B", 0), ("A", 1), ("B", 1), ("C", 0), ("A", 2), ("B", 2), ("C", 1), ("A", 3), ("B", 3),
                   ("C", 2), ("C", 3)]
            for kind, g in seq:
                if kind == "A":
                    stageA(g)
                    yield
                elif kind == "B":
                    attn(2 * g)
                    attn(2 * g + 1)
                    yield
                    intra(2 * g)
                    intra(2 * g + 1)
                else:
                    stageC(g)
                    yield
            for hf in range(2):
                ACT(SQ[0][:, hf * 512:(hf + 1) * 512], PS[4 + hf][:, :], AF.Square, [("ps", 4 + hf)], [("SQ", 0, hf)])
        if ui + 1 < len(units):
            tk(ui + 1)
        yield
        if s == 1:
            for hf in range(2):
                sl = slice(hf * 512, (hf + 1) * 512)
                b = rot()
                MM(PS[b][:, :], ONESB, SQ[0][:, sl], True, True, [("SQ", 0, hf), "ONESB"], [("ps", b)])
                r = RSTD[:, sl]
                ACT(r, PS[b][:, :], AF.Ln, [("ps", b), "EPSC"], [("RSTD", hf)], scale=1.0 / 128, bias=EPSC)
                ACT(r, r, AF.Exp, [("RSTD", hf)], [("RSTD", hf)], scale=-0.5)
                DVE(lambda e, r=r, hf=hf: e.scalar_tensor_tensor(
                    out=r, in0=PS[4 + hf][:, :], scalar=CV[:, C_HN:C_HN + 1], in1=r, op0=ALU.mult, op1=ALU.mult),
                    [("ps", 4 + hf), ("RSTD", hf), "CV"], [("RSTD", hf)])
                DVE(lambda e, r=r, sl=sl: e.tensor_tensor(out=OZ[:, h, sl], in0=r, in1=ZG[hp][:, sl], op=ALU.mult),
                    [("RSTD", hf), ("ZG", hp, hf)], [kRQ(h, hf)])
            yield

    def drain(g):
        for _ in g:
            pass

    def u_gen():
        for j in range(4):
            w, k = w_next()
            w3 = c3(w, 256)
            for jj in range(2):
                n = 2 * j + jj
                for hf in range(2):
                    b = rot()
                    for c in range(16):
                        MM(PS[b][:, :], w3[:, c, jj * 128:(jj + 1) * 128], HT[:, c, hf * 512:(hf + 1) * 512],
                           c == 0, c == 15, [k, kRH(c, hf)], [("ps", b)])
                    ACT(UU[:, n, hf * 512:(hf + 1) * 512], PS[b][:, :], AF.Gelu, [("ps", b)], [kRQ(8 + n, hf)])
                    if jj == 1 and hf == 1:
                        w_done()
                    yield

    ug = u_gen()
    NU = len(units)
    drain(proj_gen(0))
    drain(proj_gen(1))
    tk(0)
    for ui in range(NU):
        rg = rec_gen(ui, None)
        n_y = 5 if units[ui][1] == 0 else 14
        if ui + 2 < NU:
            pg = proj_gen(ui + 2)
            n_p = 5 if units[ui + 2][1] == 0 else 9
        else:
            pg, n_p = ug, (8 if units[ui][1] == 0 else 8)
        yd, pd = 0, 0
        for _ in rg:
            yd += 1
            while pd < n_p and pd * n_y < yd * n_p:
                next(pg, None)
                pd += 1
        if pg is not ug:
            drain(pg)
    dump("oz", RQ[:, 0:8192], [kRQ(h, hf) for h in range(8) for hf in range(2)])

    TMPA = [RX[:, 14336 + i * 512:14336 + (i + 1) * 512] for i in range(4)]
    TMPS = [RX[:, 10368 + i * 512:10368 + (i + 1) * 512] for i in range(2)]
    tmpa_keys = [("TMPA", i) for i in range(4)]
    A2K = A2_KEYS + [("TMPS", 0), ("TMPS", 1)]
    fence(A1_KEYS, A2K + tmpa_keys)
    bcast = lambda r: bass.AP(ln_d.tensor, r * 1024, [[0, 128], [1, 1024]])
    spdma(GLN, bcast(0), [], ["GLN"], "gln")
    spdma(BLN, bcast(1), [], ["BLN"], "bln")
    spdma(BSP, bcast(2), [], ["BSP"], "bsp")
    pooldma(WSP, wsp_d, [], ["WSP"], "wsp")
    wsp3 = WSP.rearrange("p (h t) -> p h t", t=128)
    P.op("dve", lambda e: e.memset(wsp3[64:128, :, 0:64], 0.0), [], ["WSP"])
    rr = {"i": 0}

    def pair4():
        b = (rr["i"] % 3) * 2
        rr["i"] += 1
        return b

    spi = {"i": 0}

    def sp_group(h, g4):
        b = pair4() + (spi["i"] % 2)
        ti = spi["i"] % 2
        spi["i"] += 1
        for ii in range(4):
            i = g4 * 4 + ii
            o = PS[b][:, ii * 128:(ii + 1) * 128]
            MM(o, VTG[:, i, h * 128:(h + 1) * 128], wsp3[:, h, :], True, True, [("VTG", i), "WSP"], [("ps", b)])
        sl = slice(g4 * 512, (g4 + 1) * 512)
        tmp = TMPS[ti]
        bs = BSP[:, h * 128:h * 128 + 1]
        bs_bc = bass.AP(bs.tensor, bs.offset, [[bs.ap[0][0], 128], [0, 4], [1, 128]])
        DVE(lambda e: e.tensor_tensor(
            out=tmp.rearrange("p (i t) -> p i t", t=128), in0=PS[b][:, :].rearrange("p (i t) -> p i t", t=128),
            in1=bs_bc, op=ALU.add), [("ps", b), "BSP"], [("TMPS", ti)])
        DVE(lambda e: e.tensor_tensor(out=UU[:, h, sl], in0=UU[:, h, sl], in1=tmp, op=ALU.mult),
            [kRQ(8 + h, g4), ("TMPS", ti)], [kRQ(8 + h, g4)])

    drain(ug)
    wv = [w_next() for _ in range(4)]
    for i in range(8):
        b = pair4()
        gv = GVT[i % 2]
        gk = ("GVT", i % 2)
        for j in range(4):
            w3 = c3(wv[j][0], 256)
            o = PS[b + j // 2][:, (j % 2) * 256:(j % 2 + 1) * 256]
            for c in range(16):
                MM(o, HT[:, c, i * 128:(i + 1) * 128], w3[:, c, :], c == 0, c == 15,
                   [wv[j][1], kRH(c, i // 4)], [("ps", b + j // 2)])
        for hf in range(2):
            ACT(gv[:, hf * 512:(hf + 1) * 512], PS[b + hf][:, :], AF.Gelu, [("ps", b + hf)], [gk])
        for hf in range(2):
            DVE(lambda e, hf=hf, gv=gv: e.bn_stats(out=STATS[:, hf * 6:(hf + 1) * 6], in_=gv[:, hf * 512:(hf + 1) * 512]),
                [gk], ["STATS"])
        DVE(lambda e: e.bn_aggr(out=STATS[:, 16:18], in_=STATS[:, 0:12]), ["STATS"], ["STATS"])
        ACT(STATS[:, 18:19], STATS[:, 17:18], AF.Ln, ["STATS", "EPSC"], ["STATS"], bias=EPSC)
        ACT(STATS[:, 18:19], STATS[:, 18:19], AF.Exp, ["STATS"], ["STATS"], scale=-0.5)
        DVE(lambda e, gv=gv: e.tensor_scalar(out=gv, in0=gv, scalar1=STATS[:, 16:17], scalar2=STATS[:, 18:19],
                                             op0=ALU.subtract, op1=ALU.mult), [gk, "STATS"], [gk])
        DVE(lambda e, gv=gv: e.tensor_tensor(out=gv, in0=gv, in1=GLN, op=ALU.mult), [gk, "GLN"], [gk])
        DVE(lambda e, gv=gv, i=i: e.tensor_tensor(out=VTG[:, i, :], in0=gv, in1=BLN, op=ALU.add), [gk, "BLN"], [("VTG", i)])
        if i >= 4:
            sp_group(2 * (i - 4), 0)
            sp_group(2 * (i - 4) + 1, 0)
    w_done(4)

    rx_lo = [kRX(c, h) for c in range(14) for h in range(2)]
    rx_hi = [kRX(c, h) for c in range(14, 16) for h in range(2)]
    xv_ = xT_d.rearrange("(c p) t -> p c t", p=128)
    bset = {"i": 0}
    mw = {}

    def merge_group(n, hf):
        if hf == 0:
            mw["g"] = w_next()
            mw["ab"] = w_next()
        wg, kg = mw["g"]
        wab, kab = mw["ab"]
        g3 = c3(wg, 256)
        wa3 = c3(wab[:, 0:1024], 128)
        wb3 = c3(wab[:, 1024:2048], 128)
        b0 = (bset["i"] % 2) * 4
        bset["i"] += 1
        ts = slice(hf * 512, (hf + 1) * 512)
        for c in range(16):
            MM(PS[b0][:, :], g3[:, c, 0:128], HT[:, c, ts], c == 0, c == 15, [kg, kRH(c, hf)], [("ps", b0)])
        for c in range(8):
            MM(PS[b0 + 1][:, :], wa3[:, c, :], OZ[:, c, ts], c == 0, c == 7, [kab, kRQ(c, hf)], [("ps", b0 + 1)])
        for c in range(16):
            MM(PS[b0 + 2][:, :], g3[:, c, 128:256], HT[:, c, ts], c == 0, c == 15, [kg, kRH(c, hf)], [("ps", b0 + 2)])
        for c in range(8):
            MM(PS[b0 + 3][:, :], wb3[:, c, :], UU[:, c, ts], c == 0, c == 7, [kab, kRQ(8 + c, hf)], [("ps", b0 + 3)])
        ta, tb = TMPA[(b0 // 4) * 2], TMPA[(b0 // 4) * 2 + 1]
        ka_, kb_ = ("TMPA", (b0 // 4) * 2), ("TMPA", (b0 // 4) * 2 + 1)
        ACT(ta, PS[b0][:, :], AF.Sigmoid, [("ps", b0)], [ka_])
        DVE(lambda e: e.tensor_tensor(out=ta, in0=ta, in1=PS[b0 + 1][:, :], op=ALU.mult), [ka_, ("ps", b0 + 1)], [ka_])
        ACT(tb, PS[b0 + 2][:, :], AF.Sigmoid, [("ps", b0 + 2)], [kb_])
        DVE(lambda e: e.tensor_tensor(out=tb, in0=tb, in1=PS[b0 + 3][:, :], op=ALU.mult), [kb_, ("ps", b0 + 3)], [kb_])
        DVE(lambda e: e.tensor_tensor(out=RPv[:, n, ts], in0=ta, in1=tb, op=ALU.add), [ka_, kb_], [kRP(n, hf)])
        if hf == 1:
            w_done(2)

    merge_group(0, 0)
    for h in range(8):
        sp_group(h, 1)
    dump("prod", RQ[:, 8192:16384], [kRQ(8 + h, hf) for h in range(8) for hf in range(2)])
    fence(A2K, rx_lo)
    for g, (c0, c1) in enumerate([(0, 4), (4, 8), (8, 11), (11, 14)]):
        spdma(XT[:, c0:c1, :], xv_[:, c0:c1, :], [], [kRX(c, h) for c in range(c0, c1) for h in range(2)], f"xr{g}")
    merge_group(0, 1)
    for n in range(1, 16):
        merge_group(n, 0)
        merge_group(n, 1)
    dump("merged", RP[:, :], [kRP(c, h) for c in range(16) for h in range(2)])

    fence([("TMPA", i) for i in range(4)], rx_hi)
    spdma(XT[:, 14:16, :], xv_[:, 14:16, :], [], rx_hi, "xr4")

    def sq_inline(n, hf):
        qi = sqi["i"] % 4
        sqi["i"] += 1
        sq = SQ[qi // 2][:, (qi % 2) * 512:(qi % 2 + 1) * 512]
        sk = ("SQ", qi // 2, qi % 2)
        ACT(sq, XT[:, n, hf * 512:(hf + 1) * 512], AF.Square, [kRX(n, hf)], [sk])
        deferred.append(lambda: MM(PS[6 + hf][:, :], ONESB, sq, n == 0, n == 15, [sk, "ONESB"], [("ps", 6 + hf)],
                                   signal=True))

    deferred = []

    def run_deferred(keep):
        while len(deferred) > keep:
            deferred.pop(0)()

    def rstd_from(bank0):
        for h in range(2):
            r = RSTD[:, h * 512:(h + 1) * 512]
            ACT(r, PS[bank0 + h][:, :], AF.Ln, [("ps", bank0 + h), "EPSC"], [("RSTD", h)], scale=1.0 / D, bias=EPSC)
            ACT(r, r, AF.Exp, [("RSTD", h)], [("RSTD", h)], scale=-0.5)

    def proj_add(src, skey, kc, stats=False, gnext=None):
        for j in range(8):
            w, k = w_next()
            w3 = c3(w, 256)
            for jj in range(2):
                n = 2 * j + jj
                b = pair4()
                for hf in range(2):
                    ts = slice(hf * 512, (hf + 1) * 512)
                    run_deferred(2)
                    for c in range(kc):
                        MM(PS[b + hf][:, :], w3[:, c, jj * 128:(jj + 1) * 128], src[:, c, ts], c == 0, c == kc - 1,
                           [k, skey(c, hf)], [("ps", b + hf)])
                    DVE(lambda e, n=n, ts=ts, b=b, hf=hf: e.tensor_tensor(out=XT[:, n, ts], in0=XT[:, n, ts], in1=PS[b + hf][:, :],
                                                                        op=ALU.add), [kRX(n, hf), ("ps", b + hf)], [kRX(n, hf)])
                    if stats:
                        sq_inline(n, hf)
                    if gnext is not None:
                        ACT(HT[:, n, ts], XT[:, n, ts], AF.Copy, [kRX(n, hf), "CV"], [kRH(n, hf)],
                            scale=CV[:, gnext + n:gnext + n + 1])
            w_done()
        run_deferred(0)

    proj_add(RPv, kRP, 16, stats=True, gnext=C_GFFN)
    dump("x1", RX[:, :], rx_keys)

    rstd_from(6)
    PTB = c3(RP[:, 0:2048])
    for q in range(4):
        H1, hkey = (RPv, kRP) if q % 2 == 0 else (RQv, kRQ)
        if q == 3:
            pooldma(PTB, pT_d.rearrange("(c p) t -> p c t", p=128), [], [kRP(0, 0), kRP(0, 1), kRP(1, 0), kRP(1, 1)], "ptb")
        for j in range(8):
            w, k = w_next()
            w3 = c3(w, 256)
            for jj in range(2):
                n = 2 * j + jj
                b = pair4()
                for hf in range(2):
                    ts = slice(hf * 512, (hf + 1) * 512)
                    for c in range(16):
                        MM(PS[b + hf][:, :], w3[:, c, jj * 128:(jj + 1) * 128], HT[:, c, ts], c == 0, c == 15,
                           [k, kRH(c, hf)], [("ps", b + hf)])
                    pb = PS[b + hf][:, :]
                    ACT(pb, pb, AF.Relu, [("ps", b + hf)], [("ps", b + hf)])
                    DVE(lambda e, pb=pb, ts=ts: e.tensor_tensor(out=pb, in0=pb, in1=RSTD[:, ts], op=ALU.mult),
                        [("ps", b + hf), ("RSTD", hf)], [("ps", b + hf)])
                    ACT(H1[:, n, ts], pb, AF.Square, [("ps", b + hf)], [hkey(n, hf)])
            w_done()
        proj_add(H1, hkey, 16, stats=(q == 3), gnext=(C_GPLE if q == 3 else None))
    dump("x2", RX[:, :], rx_keys)

    rstd_from(6)
    wpp, kpp = w_next()
    WPP = RP[:, 4096:8192]
    kwpp = [kRP(c, h) for c in range(4, 8) for h in range(2)]
    P.op("act", lambda e: e.activation(out=WPP, in_=wpp, func=AF.Copy), [kpp], kwpp)
    w_done()
    wpp3 = c3(WPP, 2048)
    GT = [RP[:, 2048 + i * 1024:2048 + (i + 1) * 1024].bitcast(F32) for i in range(2)]
    for j in range(8):
        w, k = w_next()
        w3 = c3(w, 256)
        for jj in range(2):
            n = 2 * j + jj
            for hf in range(2):
                b = pair4()
                ts = slice(hf * 512, (hf + 1) * 512)
                run_deferred(2)
                for c in range(16):
                    MM(PS[b][:, :], w3[:, c, jj * 128:(jj + 1) * 128], HT[:, c, ts], c == 0, c == 15,
                       [k, kRH(c, hf)], [("ps", b)])
                for c in range(2):
                    MM(PS[b + 1][:, :], wpp3[:, c, n * 128:(n + 1) * 128], PTB[:, c, ts], c == 0, c == 1,
                       kwpp + [kRP(c, hf)], [("ps", b + 1)])
                gt = GT[hf]
                gk = kRP(2, hf)
                DVE(lambda e, gt=gt, b=b, ts=ts: e.tensor_tensor(out=gt, in0=PS[b][:, :], in1=RSTD[:, ts], op=ALU.mult),
                    [("ps", b), ("RSTD", hf)], [gk])
                ACT(gt, gt, AF.Sigmoid, [gk], [gk])
                DVE(lambda e, gt=gt, b=b: e.tensor_tensor(out=gt, in0=gt, in1=PS[b + 1][:, :], op=ALU.mult),
                    [gk, ("ps", b + 1)], [gk])
                DVE(lambda e, gt=gt, n=n, ts=ts: e.tensor_tensor(out=XT[:, n, ts], in0=XT[:, n, ts], in1=gt, op=ALU.add),
                    [gk, kRX(n, hf)], [kRX(n, hf)])
                sq_inline(n, hf)
        w_done()
    run_deferred(0)

    rstd_from(6)
    ov = out_d.rearrange("(c p) t -> p c t", p=128)
    TMPF = [RP[:, 8192 + i * 1024:8192 + (i + 1) * 1024].bitcast(F32) for i in range(2)]
    for g in range(8):
        for c in range(2 * g, 2 * g + 2):
            for h in range(2):
                sl = slice(h * 512, (h + 1) * 512)
                it = 2 * c + h
                if False:
                    ti = (it // 3) % 2
                    ACT(TMPF[ti], XT[:, c, sl], AF.Copy, [kRX(c, h), "CV"], [kRP(8, ti)], scale=CV[:, C_GFIN + c:C_GFIN + c + 1])
                    P.op("pool", lambda e, c=c, sl=sl, ti=ti: e.tensor_tensor(out=XT[:, c, sl], in0=TMPF[ti], in1=RSTD[:, sl],
                                                                         op=ALU.mult),
                         [kRP(8, ti), ("RSTD", h)], [kRX(c, h)])
                    continue
                DVE(lambda e, c=c, sl=sl: e.scalar_tensor_tensor(
                    out=XT[:, c, sl], in0=XT[:, c, sl], scalar=CV[:, C_GFIN + c:C_GFIN + c + 1], in1=RSTD[:, sl],
                    op0=ALU.mult, op1=ALU.mult), [kRX(c, h), ("RSTD", h), "CV"], [kRX(c, h)])
        t = spdma(ov[:, g * 2:(g + 1) * 2, :], XT[:, g * 2:(g + 1) * 2, :],
                  [kRX(c, h) for c in range(g * 2, g * 2 + 2) for h in range(2)], [], f"out{g}")
        final_toks.append(t)
    for t in final_toks:
        P.wait("sp", t)
    assert wstate["cur"] == NBLK and wstate["done"] == NBLK, wstate
    assert P.simulate(), "deadlock in plan"
    print("plan ops", {e: len(P.ops[e]) for e in ENGS}, "sems", {e: P.echan[e].val for e in P.echan})
    P.emit(block)
    es.close()
    return nc


def make_in_maps(inputs):
    f = lambda a: np.ascontiguousarray(np.asarray(a, dtype=np.float32))
    x = f(inputs["x"])
    p = f(inputs["p"])[0]
    ws = {
        "w_in": f(inputs["w_in"])[0], "w_a_out": f(inputs["w_a_out"])[0], "w_b_out": f(inputs["w_b_out"])[0],
        "w_o": f(inputs["w_o"])[0], "w_ff1": f(inputs["w_ff1"])[0], "w_ff2": f(inputs["w_ff2"])[0],
        "w_ple_gate": f(inputs["w_ple_gate"])[0], "w_ple_proj": f(inputs["w_ple_proj"])[0],
    }
    wb = pack_weights(ws)
    cv = np.zeros((128, NCV), np.float32)
    fm = lambda v: np.ascontiguousarray(v.reshape(-1, 128).T)
    cv[:, C_GMIX:C_GMIX + 16] = fm(f(inputs["norm_mix"])[0])
    cv[:, C_GFFN:C_GFFN + 16] = fm(f(inputs["norm_ffn"])[0])
    cv[:, C_GPLE:C_GPLE + 16] = fm(f(inputs["norm_ple"])[0])
    cv[:, C_GFIN:C_GFIN + 16] = fm(f(inputs["norm_final"]))
    lbl = f(inputs["lb_logits"])
    cv[:, C_L0:C_L0 + 8] = fm(lbl[0])
    cv[:, C_L1:C_L1 + 8] = fm(lbl[1])
    cv[:, C_HN] = f(inputs["hgrn_norm"])[0]
    cv[:, C_ID:C_ID + 128] = np.eye(128, dtype=np.float32)
    s_ = np.arange(128)[:, None]
    t_ = np.arange(128)[None, :]
    cv[:, C_MASK:C_MASK + 128] = ((s_ <= t_) & (s_ // 64 == t_ // 64)).astype(np.float32)
    lnrow = np.stack([f(inputs["gmlp_ln_g"])[0], f(inputs["gmlp_ln_b"])[0], f(inputs["b_spatial"])[0].reshape(-1)])
    wspT = np.ascontiguousarray(f(inputs["w_spatial"])[0].transpose(2, 0, 1).reshape(128, 1024))
    maps = []
    for core in range(8):
        b, half = core // 2, core % 2
        xs = x[b, half * T:(half + 1) * T, :]
        xp = x[b, 0:T, :] if half == 1 else np.zeros((T, D), np.float32)
        maps.append({
            "xT": np.ascontiguousarray(xs.T), "xpT": np.ascontiguousarray(xp.T),
            "pT": np.ascontiguousarray(p[b, half * T:(half + 1) * T, :].T),
            "wb": wb, "cvec": cv, "lnrow": np.ascontiguousarray(lnrow), "wspT": wspT,
        })
    return maps


_NC_CACHE = {}


def kernel(**inputs):
    maps = make_in_maps(inputs)
    if "nc" not in _NC_CACHE:
        _NC_CACHE["nc"] = build_program()
    res = run_bass_kernel_spmd(_NC_CACHE["nc"], maps, core_ids=list(range(8)))
    out = np.empty((4, 2 * T, D), np.float32)
    for core in range(8):
        b, half = core // 2, core % 2
        out[b, half * T:(half + 1) * T, :] = res.results[core]["outT"].T
    return out
```

```python
from contextlib import ExitStack

import numpy as np
import concourse.bass as bass
import concourse.mybir as mybir
from concourse.bass_utils import run_bass_kernel_spmd

F32 = mybir.dt.float32
BF16 = mybir.dt.bfloat16
AF = mybir.ActivationFunctionType
ALU = mybir.AluOpType

D = 2048
T = 1024
NH = 8
EPS = 1e-6
NSLOT = 4
BLK = 4096
ENGS = ["pe", "act", "dve", "pool", "sp"]

C_GMIX, C_GFFN, C_GPLE, C_GFIN, C_L0, C_L1, C_HN, C_ID, C_MASK, NCV = 0, 16, 32, 48, 64, 72, 80, 96, 224, 352


class Chan:
    def __init__(self, sem, step, name):
        self.sem, self.step, self.val, self.name = sem, step, 0, name


class Planner:
    def __init__(self, nc):
        self.nc = nc
        self.ops = {e: [] for e in ENGS}
        self.echan = {}
        self.waited = {e: {} for e in ENGS}
        self.res = {}
        for e in ["pe", "act", "dve", "pool"]:
            self.echan[e] = self.new_chan(1, "c_" + e)

    def new_chan(self, step, name):
        return Chan(self.nc.alloc_semaphore(name=name), step, name)

    def _deps(self, eng, reads, writes):
        deps = {}

        def add(tok, raw):
            if tok is None:
                return
            ch, v = tok
            if ch is self.echan.get(eng) and eng == "pe":
                return
            if deps.get(ch, 0) < v:
                deps[ch] = v

        for k in reads:
            r = self.res.get(k)
            if r is not None:
                add(r[0], True)
        for k in writes:
            r = self.res.get(k)
            if r is not None:
                add(r[0], False)
                for ch, v in r[1].items():
                    add((ch, v), False)
        waits = []
        for ch, v in deps.items():
            if self.waited[eng].get(ch, 0) < v:
                self.waited[eng][ch] = v
                waits.append((ch, v))
        return waits

    def _record(self, tok, reads, writes):
        for k in writes:
            self.res[k] = [tok, {}]
        ch, v = tok
        for k in reads:
            r = self.res.setdefault(k, [None, {}])
            if r[1].get(ch, 0) < v:
                r[1][ch] = v

    def op(self, eng, fn, reads=(), writes=(), signal=True):
        waits = self._deps(eng, reads, writes)
        ch = self.echan[eng]
        if signal:
            ch.val += 1
            tok = (ch, ch.val)
        else:
            tok = (ch, ch.val + 1)
        self.ops[eng].append((waits, fn, (ch, 1) if signal else None))
        self._record(tok, reads, writes)
        return tok

    def dma(self, eng, chan, fn, reads=(), writes=(), after=()):
        waits = self._deps(eng, reads, writes)
        for ch, v in after:
            if self.waited[eng].get(ch, 0) < v:
                self.waited[eng][ch] = v
                waits.append((ch, v))
        chan.val += chan.step
        tok = (chan, chan.val)
        self.ops[eng].append((waits, fn, (chan, chan.step)))
        self._record(tok, reads, writes)
        return tok

    def wait(self, eng, tok):
        ch, v = tok
        if self.waited[eng].get(ch, 0) < v:
            self.waited[eng][ch] = v
            self.ops[eng].append(([(ch, v)], None, None))

    def simulate(self):
        val = {}
        pc = {e: 0 for e in ENGS}
        progress = True
        while progress:
            progress = False
            for e in ENGS:
                ops = self.ops[e]
                while pc[e] < len(ops):
                    waits, fn, inc = ops[pc[e]]
                    if any(val.get(ch.name, 0) < v for ch, v in waits):
                        break
                    if inc is not None:
                        val[inc[0].name] = val.get(inc[0].name, 0) + inc[1]
                    pc[e] += 1
                    progress = True
        stuck = {e: (pc[e], len(self.ops[e])) for e in ENGS if pc[e] < len(self.ops[e])}
        for e, (i, n) in stuck.items():
            waits, fn, inc = self.ops[e][i]
            print("STUCK", e, i, "/", n, [(ch.name, v, val.get(ch.name, 0)) for ch, v in waits])
        return not stuck

    def emit(self, block):
        handles = {"pe": block.tensor, "act": block.scalar, "dve": block.vector,
                   "pool": block.gpsimd, "sp": block.sync}
        for e in ENGS:
            ops = self.ops[e]
            if not ops:
                continue

            def body(engine, ops=ops):
                for waits, fn, inc in ops:
                    for ch, v in waits:
                        engine.wait_ge(ch.sem, v)
                    if fn is None:
                        continue
                    ins = fn(engine)
                    if inc is not None:
                        ins.then_inc(inc[0].sem, inc[1])

            handles[e](body)


def weight_blocks():
    blks = []
    for h in range(NH):
        blks.append(("w_in", 0, D, [(1024 + h * 128, 128), (2048 + h * 128, 128)]))
        blks.append(("w_in", 0, D, [(h * 128, 128), (3072 + h * 128, 128)]))
    for j in range(4):
        blks.append(("w_in", 0, D, [(4096 + j * 256, 256)]))
    for j in range(4):
        blks.append(("w_in", 0, D, [(5120 + j * 256, 256)]))
    for n in range(16):
        blks.append(("w_in", 0, D, [(6144 + n * 128, 128), (8192 + n * 128, 128)]))
        blks.append(("w_ab", 0, 1024, [(n * 128, 128)]))
    for j in range(8):
        blks.append(("w_o", 0, D, [(j * 256, 256)]))
    for q in range(4):
        for j in range(8):
            blks.append(("w_ff1", 0, D, [(q * 2048 + j * 256, 256)]))
        for j in range(8):
            blks.append(("w_ff2", q * 2048, D, [(j * 256, 256)]))
    blks.append(("w_ple_proj", 0, 256, [(0, 2048)]))
    for j in range(8):
        blks.append(("w_ple_gate", 0, D, [(j * 256, 256)]))
    return blks


def pack_weights(ws):
    blks = weight_blocks()
    out = np.zeros((len(blks), 128, BLK), np.float32)
    for i, (name, r0, K, cols) in enumerate(blks):
        kc = K // 128
        if name == "w_ab":
            c0, ncl = cols[0]
            a = ws["w_a_out"][:, c0:c0 + ncl].reshape(kc, 128, ncl).transpose(1, 0, 2).reshape(128, kc * ncl)
            b = ws["w_b_out"][:, c0:c0 + ncl].reshape(kc, 128, ncl).transpose(1, 0, 2).reshape(128, kc * ncl)
            out[i, :, :kc * ncl] = a
            out[i, :, kc * ncl:2 * kc * ncl] = b
            continue
        W = ws[name]
        sub = np.concatenate([W[r0:r0 + K, c0:c0 + ncl] for c0, ncl in cols], axis=1)
        ncl = sub.shape[1]
        out[i, :, :kc * ncl] = sub.reshape(kc, 128, ncl).transpose(1, 0, 2).reshape(128, kc * ncl)
    return out


NBLK = len(weight_blocks())


def build_program(dbg=()):
    nc = bass.Bass("TRN2", target_bir_lowering=False)
    xT_d = nc.dram_tensor("xT", [D, T], F32, kind="ExternalInput").ap()
    xpT_d = nc.dram_tensor("xpT", [D, T], F32, kind="ExternalInput").ap()
    pT_d = nc.dram_tensor("pT", [256, T], F32, kind="ExternalInput").ap()
    wb_d = nc.dram_tensor("wb", [NBLK, 128, BLK], F32, kind="ExternalInput").ap()
    cv_d = nc.dram_tensor("cvec", [128, NCV], F32, kind="ExternalInput").ap()
    ln_d = nc.dram_tensor("lnrow", [3, 1024], F32, kind="ExternalInput").ap()
    wsp_d = nc.dram_tensor("wspT", [128, 1024], F32, kind="ExternalInput").ap()
    out_d = nc.dram_tensor("outT", [D, T], F32, kind="ExternalOutput").ap()
    dbg_d = {}
    for name, shape in dbg:
        dbg_d[name] = nc.dram_tensor("dbg_" + name, list(shape), F32, kind="ExternalOutput").ap()

    es = ExitStack()
    RX = es.enter_context(nc.sbuf_tensor("RX", [128, 16384], F32))
    RH = es.enter_context(nc.sbuf_tensor("RH", [128, 16384], BF16))
    RP = es.enter_context(nc.sbuf_tensor("RP", [128, 16384], BF16))
    RQ = es.enter_context(nc.sbuf_tensor("RQ", [128, 16384], BF16))
    RW = es.enter_context(nc.sbuf_tensor("RW", [128, NSLOT * BLK], BF16))
    RC = es.enter_context(nc.sbuf_tensor("RC", [128, 3200], F32))
    PS = [es.enter_context(nc.psum_tensor(f"ps{i}", [128, 512], F32)) for i in range(8)]
    P = Planner(nc)
    block = es.enter_context(nc.Block())

    def c3(ap, t=1024):
        return ap.rearrange("p (c t) -> p c t", t=t)

    XT = c3(RX[:, :])
    HT = c3(RH[:, :])
    RPv = c3(RP[:, :])
    RQv = c3(RQ[:, :])
    WS = [RW[:, s * BLK:(s + 1) * BLK] for s in range(NSLOT)]
    PT = PS[7][:, :].bitcast(BF16)

    CV = RC[:, 0:NCV]
    RSTD = RC[:, 352:1376]
    SQ = [RC[:, 1376 + i * 512:1376 + (i + 1) * 512].bitcast(BF16) for i in range(2)]
    RMASK = RC[:, 2400:2912].bitcast(BF16)
    ONESB = RC[:, 2912:2976].bitcast(BF16)
    IDB = RC[:, 2976:3040].bitcast(BF16)
    LB = RC[:, 3040:3048]
    OML = RC[:, 3048:3056]
    LBM1 = RC[:, 3056:3064]
    ONESF = RC[:, 3064:3192]
    MISC = RC[:, 3192:3200]
    MASKF = CV[:, C_MASK:C_MASK + 128]

    def rxf(i):
        return RX[:, i * 1024:(i + 1) * 1024]

    SG, KK, BB, DD, EP, QQ = [rxf(i) for i in range(6)]

    def rxb(i):
        return RX[:, 6144 + i * 512:6144 + (i + 1) * 512].bitcast(BF16)

    KT = [rxb(0), rxb(1)]
    QT = [rxb(2), rxb(3)]
    ZG = [rxb(4), rxb(5)]
    KHT = [rxb(6), rxb(7)]
    KH0 = [rxb(8), rxb(9)]
    KH1 = [rxb(10), rxb(11)]
    VTOK = [rxb(12), rxb(13), rxb(14)]
    INPT = rxb(15)
    o0 = 6144 + 16 * 512
    ATS = [RX[:, o0 + i * 64:o0 + (i + 1) * 64].bitcast(BF16) for i in range(2)] + \
        [RX[:, o0 + 1712 + i * 64:o0 + 1712 + (i + 1) * 64].bitcast(BF16) for i in range(4)]
    SF = [RX[:, o0 + 128 + i * 128:o0 + 128 + (i + 1) * 128] for i in range(4)]
    SBFR = [RX[:, o0 + 640 + i * 64:o0 + 640 + (i + 1) * 64].bitcast(BF16) for i in range(16)]
    GAM = [RX[:, o0 + 1664 + i * 16:o0 + 1664 + (i + 1) * 16] for i in range(3)]
    assert o0 + 1712 + 256 <= 16384
    OZ = c3(RQ[:, 0:8192])
    UU = c3(RQ[:, 8192:16384])
    GVT = [rxf(0), rxf(1)]
    GLN, BLN, BSP = rxf(2), rxf(3), rxf(4)
    WSP = RX[:, 5 * 1024:5 * 1024 + 512].bitcast(BF16)
    VTG = c3(RX[:, 6 * 1024:10 * 1024].bitcast(BF16))
    STATS = RX[:, 10 * 1024:10 * 1024 + 64]

    rx_keys = [("RX", c, h) for c in range(16) for h in range(2)]
    A1_KEYS = [("SG", 0), ("SG", 1), "KK", "BB", "DD", "EP", ("QQ", 0), ("QQ", 1), ("INPT", 0), ("INPT", 1)] + \
        [(n, i) for n in ["KT", "QT", "KHT", "KH0", "KH1"] for i in range(2)] + [("ATS", i) for i in range(6)] + \
        [("VTOK", i) for i in range(3)] + [("GAM", i) for i in range(3)] + [("SF", i) for i in range(4)] + \
        [("SBFR", i) for i in range(16)] + [("ZG", i, hf) for i in range(2) for hf in range(2)]
    A2_KEYS = [("GVT", 0), ("GVT", 1), "GLN", "BLN", "BSP", "WSP", "STATS"] + [("VTG", i) for i in range(8)]

    def ACT(out, in_, func, reads, writes, **kw):
        return P.op("act", lambda e: e.activation(out=out, in_=in_, func=func, **kw), reads, writes)

    def DVE(fn, reads, writes):
        return P.op("dve", fn, reads, writes)

    def MM(out, lhsT, rhs, start, stop, reads, writes, signal=None):
        return P.op("pe", lambda e: e.matmul(out, lhsT, rhs, start=start, stop=stop), reads, writes,
                    signal=stop if signal is None else signal)

    def fence(old, new):
        P.op("dve", lambda e: e.memset(MISC[:, 0:1], 0.0), reads=[], writes=list(old) + list(new) + ["MISC0"])

    def spdma(out, in_, reads, writes, name):
        ch = P.new_chan(16, name)
        return P.dma("sp", ch, lambda e: e.dma_start(out=out, in_=in_), reads, writes)

    def pooldma(out, in_, reads, writes, name):
        ch = P.new_chan(16, name)
        return P.dma("pool", ch, lambda e: e.dma_start(out=out, in_=in_), reads, writes)

    wch = [P.new_chan(16, f"w{s}") for s in range(NSLOT)]
    wstate = {"issued": 0, "done": 0, "cur": 0}

    def w_pump(limit=NBLK):
        while wstate["issued"] < min(NBLK, limit) and wstate["issued"] - NSLOT < wstate["done"]:
            j = wstate["issued"]
            s = j % NSLOT
            P.dma("pool", wch[s],
                  lambda e, j=j, s=s: e.dma_start(out=c3(WS[s], 2048), in_=c3(wb_d[j], 2048)),
                  reads=[], writes=[("ws", s)], after=wstate.get("gate", []) if 0 < j < NSLOT else [])
            wstate["issued"] += 1

    def w_next():
        i = wstate["cur"]
        wstate["cur"] += 1
        w_pump()
        assert wstate["issued"] > i, "weight block not issued (too many blocks held)"
        return WS[i % NSLOT], ("ws", i % NSLOT)

    def w_done(n=1):
        wstate["done"] += n
        w_pump()

    def dump(name, ap, reads):
        if name in dbg_d:
            if ap.dtype != F32:
                ap = ap.bitcast(F32)
            t = spdma(dbg_d[name], ap, reads, [], "dbg_" + name)
            final_toks.append(t)

    final_toks = []

    t_cv = spdma(CV, cv_d, [], ["CV"], "cv")
    EPSC = MISC[:, 1:2]
    P.op("dve", lambda e: e.memset(EPSC, EPS), [], ["EPSC"])
    P.op("dve", lambda e: e.memset(ONESB, 1.0), [], ["ONESB"])
    P.op("dve", lambda e: e.memset(ONESF, 1.0), [], ["ONESF"])
    P.op("dve", lambda e: e.memset(RMASK, 1.0), [], ["RMASK"])
    P.op("dve", lambda e: e.memset(RMASK[:, 0::64], 0.0), [], ["RMASK"])
    P.op("dve", lambda e: e.tensor_copy(out=IDB, in_=CV[:, C_ID:C_ID + 128]), ["CV"], ["IDB"])
    P.op("dve", lambda e: e.tensor_tensor(out=LBM1, in0=CV[:, C_L0:C_L0 + 8], in1=CV[:, C_L1:C_L1 + 8], op=ALU.subtract),
         ["CV"], ["LBM1"])
    ACT(LB, LBM1, AF.Sigmoid, ["LBM1"], ["LB"])
    P.op("dve", lambda e: e.tensor_scalar(out=OML, in0=LB, scalar1=-1.0, scalar2=1.0, op0=ALU.mult, op1=ALU.add),
         ["LB"], ["OML"])
    P.op("dve", lambda e: e.tensor_scalar(out=LBM1, in0=LB, scalar1=-1.0, scalar2=None, op0=ALU.add),
         ["LB"], ["LBM1"])

    sqi = {"i": 0}

    def rms_stats(src, src_key, bank0):
        for c in range(16):
            sq = SQ[sqi["i"] % 2]
            sk = ("SQ", sqi["i"] % 2)
            sqi["i"] += 1
            for h in range(2):
                ACT(sq[:, h * 512:(h + 1) * 512], src[:, c, h * 512:(h + 1) * 512], AF.Square,
                    [src_key(c, h)], [sk + (h,)])
            for h in range(2):
                MM(PS[bank0 + h][:, :], ONESB, sq[:, h * 512:(h + 1) * 512], c == 0, c == 15,
                   [sk + (h,), "ONESB"], [("ps", bank0 + h)], signal=True)
        for h in range(2):
            r = RSTD[:, h * 512:(h + 1) * 512]
            ACT(r, PS[bank0 + h][:, :], AF.Ln, [("ps", bank0 + h), "EPSC"], [("RSTD", h)], scale=1.0 / D, bias=EPSC)
            ACT(r, r, AF.Exp, [("RSTD", h)], [("RSTD", h)], scale=-0.5)

    def rms_apply(dst, dst_key, src, src_key, gcol):
        for h in range(2):
            for c in range(16):
                sl = slice(h * 512, (h + 1) * 512)
                DVE(lambda e, c=c, sl=sl: e.scalar_tensor_tensor(
                    out=dst[:, c, sl], in0=src[:, c, sl], scalar=CV[:, gcol + c:gcol + c + 1], in1=RSTD[:, sl],
                    op0=ALU.mult, op1=ALU.mult),
                    [src_key(c, h), ("RSTD", h), "CV"], [dst_key(c, h)])

    kRX = lambda c, h: ("RX", c, h)
    kRH = lambda c, h: ("RH", c, h)
    kRP = lambda c, h: ("RP", c, h)
    kRQ = lambda c, h: ("RQ", c, h)

    def load_x(src_d, tag):
        v = src_d.rearrange("(c p) t -> p c t", p=128)
        for g in range(4):
            spdma(XT[:, g * 4:(g + 1) * 4, :], v[:, g * 4:(g + 1) * 4, :], [],
                  [kRX(c, h) for c in range(g * 4, g * 4 + 4) for h in range(2)], f"x{tag}{g}")

    NB = [c3(RX[:, 0:8192], 512), c3(RX[:, 8192:16384], 512), c3(RQ[:, :].bitcast(F32), 512)]
    nb_keys = [("NB", i, c) for i in range(3) for c in range(16)]
    jobs = [(xpT_d, 0, RPv, kRP), (xpT_d, 1, RPv, kRP), (xT_d, 0, HT, kRH), (xT_d, 1, HT, kRH)]
    def job_load(ji):
        src_d, hf, dst, dkey = jobs[ji]
        buf, bi = NB[ji % 3], ji % 3
        v = src_d.rearrange("(c p) t -> p c t", p=128)
        toks = []
        for g in range(2):
            toks.append(spdma(buf[:, g * 8:(g + 1) * 8, :], v[:, g * 8:(g + 1) * 8, hf * 512:(hf + 1) * 512], [],
                              [("NB", bi, c) for c in range(g * 8, g * 8 + 8)], f"nx{ji}{g}"))
        return toks

    w_pump(1)
    gate = []
    for ji in range(3):
        gate += job_load(ji)
    for ji, (src_d, hf, dst, dkey) in enumerate(jobs):
        buf = NB[ji % 3]
        bi = ji % 3
        if ji == 3:
            gate += job_load(ji)
            wstate["gate"] = gate
            w_pump()
        bank = ji % 4
        for c in range(16):
            qi = sqi["i"] % 4
            sqi["i"] += 1
            sq = SQ[qi // 2][:, (qi % 2) * 512:(qi % 2 + 1) * 512]
            sk = ("SQ", qi // 2, qi % 2)
            ACT(sq, buf[:, c, :], AF.Square, [("NB", bi, c)], [sk])
            MM(PS[bank][:, :], ONESB, sq, c == 0, c == 15, [sk, "ONESB"], [("ps", bank)], signal=True)
        r = RSTD[:, (ji % 2) * 512:(ji % 2 + 1) * 512]
        rk = ("RSTD", ji % 2)
        ACT(r, PS[bank][:, :], AF.Ln, [("ps", bank), "EPSC"], [rk], scale=1.0 / D, bias=EPSC)
        ACT(r, r, AF.Exp, [rk], [rk], scale=-0.5)
        for c in range(16):
            DVE(lambda e, c=c, buf=buf, dst=dst, hf=hf, r=r: e.scalar_tensor_tensor(
                out=dst[:, c, hf * 512:(hf + 1) * 512], in0=buf[:, c, :], scalar=CV[:, C_GMIX + c:C_GMIX + c + 1], in1=r,
                op0=ALU.mult, op1=ALU.mult), [("NB", bi, c), rk, "CV"], [dkey(c, hf)])
    sqi["i"] = 0
    dump("ht", RH[:, :], [kRH(c, h) for c in range(16) for h in range(2)])
    fence(nb_keys, A1_KEYS + [kRQ(c, h) for c in range(16) for h in range(2)])

    units = [(h, s) for h in range(NH) for s in range(2)]
    uw = {}
    R4 = [0, 1, 2, 7]
    rot_i = {"i": 0}

    def rot():
        b = R4[rot_i["i"] % 4]
        rot_i["i"] += 1
        return b

    KVB = [3, 6]
    kvb_i = {"i": 0}
    for i in range(2):
        P.op("dve", lambda e, i=i: e.memset(KH0[i], 0.0), [], [("KH0", i)])
        P.op("dve", lambda e, i=i: e.memset(KH1[i], 0.0), [], [("KH1", i)])

    def proj_gen(ui):
        h, s = units[ui]
        hp, par2, set3 = h % 2, ui % 2, ui % 3
        src, skey = (RPv, kRP) if s == 0 else (HT, kRH)
        if s == 0:
            uw[h] = w_next()
        wa, ka = uw[h]
        wa3 = c3(wa, 256)

        def fm(w3, cs, hf, k, evac):
            b = rot()
            for c in range(16):
                MM(PS[b][:, :], w3[:, c, cs], src[:, c, hf * 512:(hf + 1) * 512], c == 0, c == 15,
                   [k, skey(c, hf)], [("ps", b)])
            evac(b)

        sgk = [("SG", 0), ("SG", 1)]
        bl = BB[:, 63:64]
        bl_bc = bass.AP(bl.tensor, bl.offset, [[bl.ap[0][0], 128], [64, 16], [0, 64]])
        b3 = BB.rearrange("p (c s) -> p c s", s=64)
        d3 = DD.rearrange("p (c s) -> p c s", s=64)

        def S1():
            DVE(lambda e: e.tensor_scalar(out=KK, in0=SG, scalar1=-1.0, scalar2=LBM1[:, h:h + 1], op0=ALU.add, op1=ALU.mult),
                sgk + ["LBM1"], ["KK"])
            DVE(lambda e: e.tensor_scalar(out=SG, in0=SG, scalar1=OML[:, h:h + 1], scalar2=LB[:, h:h + 1], op0=ALU.mult, op1=ALU.add),
                sgk + ["OML", "LB"], sgk)

        def S2():
            ACT(SG, SG, AF.Ln, sgk, sgk)

        def S3():
            DVE(lambda e: e.tensor_tensor_scan(out=BB, data0=RMASK, data1=SG, initial=0.0, op0=ALU.mult, op1=ALU.add),
                sgk + ["RMASK"], ["BB"])
            DVE(lambda e: e.tensor_tensor(out=d3, in0=bl_bc, in1=b3, op=ALU.subtract), ["BB"], ["DD"])

        def S4():
            ACT(DD, DD, AF.Exp, ["DD"], ["DD"])
            ACT(GAM[set3], BB[:, 63::64], AF.Exp, ["BB"], [("GAM", set3)])
            if s == 1:
                ACT(EP, BB, AF.Exp, ["BB"], ["EP"])

        def S5():
            DVE(lambda e: e.tensor_tensor(out=KHT[par2], in0=KK, in1=DD, op=ALU.mult), ["KK", "DD"], [("KHT", par2)])

        def S6():
            ACT(SG, BB, AF.Exp, ["BB"], sgk, scale=-1.0)

        def S7():
            DVE(lambda e: e.tensor_tensor(out=KT[hp], in0=KK, in1=SG, op=ALU.mult), ["KK"] + sgk, [("KT", hp)])

        def f_step(hf):
            fm(wa3, slice(0, 128), hf, ka,
               lambda b: ACT(SG[:, hf * 512:(hf + 1) * 512], PS[b][:, :], AF.Sigmoid, [("ps", b)], [("SG", hf)]))

        def inp_step(hf):
            fm(wa3, slice(128, 256), hf, ka,
               lambda b: ACT(INPT[:, hf * 512:(hf + 1) * 512], PS[b][:, :], AF.Copy, [("ps", b)], [("INPT", hf)]))

        def tv_step():
            b = rot()
            ptb = PS[b][:, :].bitcast(BF16)
            for i in range(8):
                P.op("pe", lambda e, i=i, ptb=ptb: e.transpose(ptb[:, i * 128:(i + 1) * 128], INPT[:, i * 128:(i + 1) * 128], IDB),
                     [("INPT", i // 4), "IDB"], [("ps", b)], signal=(i == 7))
            ACT(VTOK[set3], ptb, AF.Copy, [("ps", b)], [("VTOK", set3)])

        f_step(0)
        yield
        f_step(1)
        S1()
        yield
        inp_step(0)
        S2()
        yield
        inp_step(1)
        S3()
        yield
        if s == 0:
            tv_step()
            S4()
            S5()
            yield
            return
        w_done()
        wb, kb = w_next()
        wb3 = c3(wb, 256)

        def q_step(hf):
            fm(wb3, slice(0, 128), hf, kb,
               lambda b: ACT(QQ[:, hf * 512:(hf + 1) * 512], PS[b][:, :], AF.Silu, [("ps", b)], [("QQ", hf)]))

        def g_step(hf):
            fm(wb3, slice(128, 256), hf, kb,
               lambda b: ACT(ZG[hp][:, hf * 512:(hf + 1) * 512], PS[b][:, :], AF.Silu, [("ps", b)], [("ZG", hp, hf)]))

        tv_step()
        S4()
        S6()
        yield
        q_step(0)
        S5()
        S7()
        yield
        q_step(1)
        yield
        g_step(0)
        yield
        g_step(1)
        DVE(lambda e: e.tensor_tensor(out=QT[hp], in0=QQ, in1=EP, op=ALU.mult),
            [("QQ", 0), ("QQ", 1), "EP"], [("QT", hp)])
        w_done()
        yield

    def tk(ui):
        par2 = ui % 2
        b = rot()
        ptb = PS[b][:, :].bitcast(BF16)
        for i in range(8):
            P.op("pe", lambda e, i=i, ptb=ptb: e.transpose(ptb[:, i * 128:(i + 1) * 128], KHT[par2][:, i * 128:(i + 1) * 128], IDB),
                 [("KHT", par2), "IDB"], [("ps", b)], signal=(i == 7))
        ACT(KH0[par2][0:64, :], ptb[0:64, :], AF.Copy, [("ps", b)], [("KH0", par2)])
        ACT(KH1[par2][64:128, :], ptb[64:128, :], AF.Copy, [("ps", b)], [("KH1", par2)])

    def rec_gen(ui, pending):
        h, s = units[ui]
        hp, par2, set3 = h % 2, ui % 2, ui % 3
        kh = [KH0[par2].rearrange("p (i d) -> p i d", d=128), KH1[par2].rearrange("p (i d) -> p i d", d=128)]
        v3 = VTOK[set3].rearrange("p (i d) -> p i d", d=128)
        if s == 0:
            DVE(lambda e: e.memset(SF[0], 0.0), [], [("SF", 0)])

        def stageA(g):
            bk = KVB[kvb_i["i"] % 2]
            kvb_i["i"] += 1
            todo = [cc for cc in range(4) if not (s == 1 and 4 * g + cc == 15)]
            for cc in todo:
                c = 4 * g + cc
                MM(PS[bk][:, cc * 128:(cc + 1) * 128], kh[c % 2][:, c // 2, :], v3[:, c // 2, :], True, True,
                   [("KH0", par2), ("KH1", par2), ("VTOK", set3)], [("ps", bk)])
            for cc in todo:
                c = 4 * g + cc
                k = 16 * s + c
                kv = PS[bk][:, cc * 128:(cc + 1) * 128]
                g_ = GAM[set3][:, c:c + 1]
                DVE(lambda e, k=k, g_=g_, kv=kv: e.scalar_tensor_tensor(
                    out=SF[(k + 1) % 4], in0=SF[k % 4], scalar=g_, in1=kv, op0=ALU.mult, op1=ALU.add),
                    [("SF", k % 4), ("GAM", set3), ("ps", bk)], [("SF", (k + 1) % 4)])
                if 16 <= k + 1 <= 31:
                    P.op("pool", lambda e, k=k: e.tensor_copy(out=SBFR[(k + 1) % 16], in_=SF[(k + 1) % 4]),
                         [("SF", (k + 1) % 4)], [("SBFR", (k + 1) % 16)])

        def attn(i):
            tk_ = slice(i * 128, (i + 1) * 128)
            b = rot()
            MM(PS[b][:, 0:128], KT[hp][:, tk_], QT[hp][:, tk_], True, True, [("KT", hp), ("QT", hp)], [("ps", b)])
            DVE(lambda e, i=i, b=b: e.tensor_tensor(out=ATS[i % 6], in0=PS[b][:, 0:128], in1=MASKF, op=ALU.mult),
                [("ps", b), "CV"], [("ATS", i % 6)])

        def intra(i):
            ob = 4 + i // 4
            ocol = (i % 4) * 128
            MM(PS[ob][:, ocol:ocol + 128], v3[:, i, :], ATS[i % 6], i % 4 == 0, False,
               [("VTOK", set3), ("ATS", i % 6)], [("ps", ob)], signal=True)

        def stageC(g):
            for cc in range(4):
                c = 4 * g + cc
                k = 16 + c
                ob = 4 + c // 8
                ocol = (c % 8) * 64
                MM(PS[ob][:, ocol:ocol + 64], SBFR[k % 16], QT[hp][:, c * 64:(c + 1) * 64], False, c % 8 == 7,
                   [("SBFR", k % 16), ("QT", hp)], [("ps", ob)], signal=True)

        if s == 0:
            for g in range(4):
                stageA(g)
                yield
        else:
            seq = [("A", 0), ("T", (0, 1, 2, 3)), ("Y", 0), ("I", (0, 1)), ("A", 1), ("Y", 0), ("I", (2, 3)),
                   ("T", (4, 5)), ("C", 0), ("Y", 0), ("A", 2), ("T", (6, 7)), ("Y", 0), ("I", (4, 5)), ("C", 1), ("Y", 0),
                   ("A", 3), ("Y", 0), ("I", (6, 7)), ("C", 2), ("Y", 0), ("C", 3), ("Y", 0)]
            for kind, g in seq:
                if kind == "A":
                    stageA(g)
                elif kind == "T":
                    for i in g:
                        attn(i)
                elif kind == "I":
                    for i in g:
                        intra(i)
                elif kind == "C":
                    stageC(g)
                else:
                    yield
            for hf in range(2):
                ACT(SQ[0][:, hf * 512:(hf + 1) * 512], PS[4 + hf][:, :], AF.Square, [("ps", 4 + hf)], [("SQ", 0, hf)])
        if ui + 1 < len(units):
            tk(ui + 1)
        yield
        if s == 1:
            for hf in range(2):
                sl = slice(hf * 512, (hf + 1) * 512)
                b = rot()
                MM(PS[b][:, :], ONESB, SQ[0][:, sl], True, True, [("SQ", 0, hf), "ONESB"], [("ps", b)])
                r = RSTD[:, sl]
                ACT(r, PS[b][:, :], AF.Ln, [("ps", b), "EPSC"], [("RSTD", hf)], scale=1.0 / 128, bias=EPSC)
                ACT(r, r, AF.Exp, [("RSTD", hf)], [("RSTD", hf)], scale=-0.5)
                DVE(lambda e, r=r, hf=hf: e.scalar_tensor_tensor(
                    out=r, in0=PS[4 + hf][:, :], scalar=CV[:, C_HN:C_HN + 1], in1=r, op0=ALU.mult, op1=ALU.mult),
                    [("ps", 4 + hf), ("RSTD", hf), "CV"], [("RSTD", hf)])
                DVE(lambda e, r=r, sl=sl: e.tensor_tensor(out=OZ[:, h, sl], in0=r, in1=ZG[hp][:, sl], op=ALU.mult),
                    [("RSTD", hf), ("ZG", hp, hf)], [kRQ(h, hf)])
            yield

    def drain(g):
        for _ in g:
            pass

    def u_gen():
        for j in range(4):
            w, k = w_next()
            w3 = c3(w, 256)
            for jj in range(2):
                n = 2 * j + jj
                for hf in range(2):
                    b = rot()
                    for c in range(16):
                        MM(PS[b][:, :], w3[:, c, jj * 128:(jj + 1) * 128], HT[:, c, hf * 512:(hf + 1) * 512],
                           c == 0, c == 15, [k, kRH(c, hf)], [("ps", b)])
                    ACT(UU[:, n, hf * 512:(hf + 1) * 512], PS[b][:, :], AF.Gelu, [("ps", b)], [kRQ(8 + n, hf)])
                    if jj == 1 and hf == 1:
                        w_done()
                    yield

    ug = u_gen()
    NU = len(units)
    drain(proj_gen(0))
    drain(proj_gen(1))
    tk(0)
    for ui in range(NU):
        rg = rec_gen(ui, None)
        n_y = 5 if units[ui][1] == 0 else 10
        if ui + 2 < NU:
            pg = proj_gen(ui + 2)
            n_p = 5 if units[ui + 2][1] == 0 else 9
        else:
            pg, n_p = ug, (8 if units[ui][1] == 0 else 8)
        yd, pd = 0, 0
        for _ in rg:
            yd += 1
            while pd < n_p and pd * n_y < yd * n_p:
                next(pg, None)
                pd += 1
        if pg is not ug:
            drain(pg)
    dump("oz", RQ[:, 0:8192], [kRQ(h, hf) for h in range(8) for hf in range(2)])

    TMPA = [RX[:, 14336 + i * 512:14336 + (i + 1) * 512] for i in range(4)]
    TMPS = [RX[:, 10368 + i * 512:10368 + (i + 1) * 512] for i in range(2)]
    tmpa_keys = [("TMPA", i) for i in range(4)]
    A2K = A2_KEYS + [("TMPS", 0), ("TMPS", 1)]
    fence(A1_KEYS, A2K + tmpa_keys)
    bcast = lambda r: bass.AP(ln_d.tensor, r * 1024, [[0, 128], [1, 1024]])
    spdma(GLN, bcast(0), [], ["GLN"], "gln")
    spdma(BLN, bcast(1), [], ["BLN"], "bln")
    spdma(BSP, bcast(2), [], ["BSP"], "bsp")
    pooldma(WSP, wsp_d, [], ["WSP"], "wsp")
    wsp3 = WSP.rearrange("p (h t) -> p h t", t=128)
    P.op("dve", lambda e: e.memset(wsp3[64:128, :, 0:64], 0.0), [], ["WSP"])
    rr = {"i": 0}

    def pair4():
        b = (rr["i"] % 3) * 2
        rr["i"] += 1
        return b

    spi = {"i": 0}

    def sp_group(h, g4):
        b = pair4() + (spi["i"] % 2)
        ti = spi["i"] % 2
        spi["i"] += 1
        for ii in range(4):
            i = g4 * 4 + ii
            o = PS[b][:, ii * 128:(ii + 1) * 128]
            MM(o, VTG[:, i, h * 128:(h + 1) * 128], wsp3[:, h, :], True, True, [("VTG", i), "WSP"], [("ps", b)])
        sl = slice(g4 * 512, (g4 + 1) * 512)
        tmp = TMPS[ti]
        bs = BSP[:, h * 128:h * 128 + 1]
        bs_bc = bass.AP(bs.tensor, bs.offset, [[bs.ap[0][0], 128], [0, 4], [1, 128]])
        DVE(lambda e: e.tensor_tensor(
            out=tmp.rearrange("p (i t) -> p i t", t=128), in0=PS[b][:, :].rearrange("p (i t) -> p i t", t=128),
            in1=bs_bc, op=ALU.add), [("ps", b), "BSP"], [("TMPS", ti)])
        DVE(lambda e: e.tensor_tensor(out=UU[:, h, sl], in0=UU[:, h, sl], in1=tmp, op=ALU.mult),
            [kRQ(8 + h, g4), ("TMPS", ti)], [kRQ(8 + h, g4)])

    drain(ug)
    wv = [w_next() for _ in range(4)]
    for i in range(8):
        b = pair4()
        gv = GVT[i % 2]
        gk = ("GVT", i % 2)
        for j in range(4):
            w3 = c3(wv[j][0], 256)
            o = PS[b + j // 2][:, (j % 2) * 256:(j % 2 + 1) * 256]
            for c in range(16):
                MM(o, HT[:, c, i * 128:(i + 1) * 128], w3[:, c, :], c == 0, c == 15,
                   [wv[j][1], kRH(c, i // 4)], [("ps", b + j // 2)])
        for hf in range(2):
            ACT(gv[:, hf * 512:(hf + 1) * 512], PS[b + hf][:, :], AF.Gelu, [("ps", b + hf)], [gk])
        for hf in range(2):
            DVE(lambda e, hf=hf, gv=gv: e.bn_stats(out=STATS[:, hf * 6:(hf + 1) * 6], in_=gv[:, hf * 512:(hf + 1) * 512]),
                [gk], ["STATS"])
        DVE(lambda e: e.bn_aggr(out=STATS[:, 16:18], in_=STATS[:, 0:12]), ["STATS"], ["STATS"])
        ACT(STATS[:, 18:19], STATS[:, 17:18], AF.Ln, ["STATS", "EPSC"], ["STATS"], bias=EPSC)
        ACT(STATS[:, 18:19], STATS[:, 18:19], AF.Exp, ["STATS"], ["STATS"], scale=-0.5)
        DVE(lambda e, gv=gv: e.tensor_scalar(out=gv, in0=gv, scalar1=STATS[:, 16:17], scalar2=STATS[:, 18:19],
                                             op0=ALU.subtract, op1=ALU.mult), [gk, "STATS"], [gk])
        DVE(lambda e, gv=gv: e.tensor_tensor(out=gv, in0=gv, in1=GLN, op=ALU.mult), [gk, "GLN"], [gk])
        DVE(lambda e, gv=gv, i=i: e.tensor_tensor(out=VTG[:, i, :], in0=gv, in1=BLN, op=ALU.add), [gk, "BLN"], [("VTG", i)])
        if i >= 4:
            sp_group(2 * (i - 4), 0)
            sp_group(2 * (i - 4) + 1, 0)
    w_done(4)

    rx_lo = [kRX(c, h) for c in range(14) for h in range(2)]
    rx_hi = [kRX(c, h) for c in range(14, 16) for h in range(2)]
    xv_ = xT_d.rearrange("(c p) t -> p c t", p=128)
    bset = {"i": 0}
    mw = {}

    def merge_group(n, hf):
        if hf == 0:
            mw["g"] = w_next()
            mw["ab"] = w_next()
        wg, kg = mw["g"]
        wab, kab = mw["ab"]
        g3 = c3(wg, 256)
        wa3 = c3(wab[:, 0:1024], 128)
        wb3 = c3(wab[:, 1024:2048], 128)
        b0 = (bset["i"] % 2) * 4
        bset["i"] += 1
        ts = slice(hf * 512, (hf + 1) * 512)
        for c in range(16):
            MM(PS[b0][:, :], g3[:, c, 0:128], HT[:, c, ts], c == 0, c == 15, [kg, kRH(c, hf)], [("ps", b0)])
        for c in range(8):
            MM(PS[b0 + 1][:, :], wa3[:, c, :], OZ[:, c, ts], c == 0, c == 7, [kab, kRQ(c, hf)], [("ps", b0 + 1)])
        for c in range(16):
            MM(PS[b0 + 2][:, :], g3[:, c, 128:256], HT[:, c, ts], c == 0, c == 15, [kg, kRH(c, hf)], [("ps", b0 + 2)])
        for c in range(8):
            MM(PS[b0 + 3][:, :], wb3[:, c, :], UU[:, c, ts], c == 0, c == 7, [kab, kRQ(8 + c, hf)], [("ps", b0 + 3)])
        ta, tb = TMPA[(b0 // 4) * 2], TMPA[(b0 // 4) * 2 + 1]
        ka_, kb_ = ("TMPA", (b0 // 4) * 2), ("TMPA", (b0 // 4) * 2 + 1)
        ACT(ta, PS[b0][:, :], AF.Sigmoid, [("ps", b0)], [ka_])
        DVE(lambda e: e.tensor_tensor(out=ta, in0=ta, in1=PS[b0 + 1][:, :], op=ALU.mult), [ka_, ("ps", b0 + 1)], [ka_])
        ACT(tb, PS[b0 + 2][:, :], AF.Sigmoid, [("ps", b0 + 2)], [kb_])
        DVE(lambda e: e.tensor_tensor(out=tb, in0=tb, in1=PS[b0 + 3][:, :], op=ALU.mult), [kb_, ("ps", b0 + 3)], [kb_])
        DVE(lambda e: e.tensor_tensor(out=RPv[:, n, ts], in0=ta, in1=tb, op=ALU.add), [ka_, kb_], [kRP(n, hf)])
        if hf == 1:
            w_done(2)

    merge_group(0, 0)
    for h in range(8):
        sp_group(h, 1)
    dump("prod", RQ[:, 8192:16384], [kRQ(8 + h, hf) for h in range(8) for hf in range(2)])
    fence(A2K, rx_lo)
    for g, (c0, c1) in enumerate([(0, 4), (4, 8), (8, 11), (11, 14)]):
        spdma(XT[:, c0:c1, :], xv_[:, c0:c1, :], [], [kRX(c, h) for c in range(c0, c1) for h in range(2)], f"xr{g}")
    merge_group(0, 1)
    for n in range(1, 16):
        merge_group(n, 0)
        merge_group(n, 1)
    dump("merged", RP[:, :], [kRP(c, h) for c in range(16) for h in range(2)])

    fence([("TMPA", i) for i in range(4)], rx_hi)
    spdma(XT[:, 14:16, :], xv_[:, 14:16, :], [], rx_hi, "xr4")

    def sq_inline(n, hf):
        qi = sqi["i"] % 4
        sqi["i"] += 1
        sq = SQ[qi // 2][:, (qi % 2) * 512:(qi % 2 + 1) * 512]
        sk = ("SQ", qi // 2, qi % 2)
        ACT(sq, XT[:, n, hf * 512:(hf + 1) * 512], AF.Square, [kRX(n, hf)], [sk])
        deferred.append(lambda: MM(PS[6 + hf][:, :], ONESB, sq, n == 0, n == 15, [sk, "ONESB"], [("ps", 6 + hf)],
                                   signal=True))

    deferred = []

    def run_deferred(keep):
        while len(deferred) > keep:
            deferred.pop(0)()

    def rstd_from(bank0):
        for h in range(2):
            r = RSTD[:, h * 512:(h + 1) * 512]
            ACT(r, PS[bank0 + h][:, :], AF.Ln, [("ps", bank0 + h), "EPSC"], [("RSTD", h)], scale=1.0 / D, bias=EPSC)
            ACT(r, r, AF.Exp, [("RSTD", h)], [("RSTD", h)], scale=-0.5)

    def proj_add(src, skey, kc, stats=False, gnext=None):
        for j in range(8):
            w, k = w_next()
            w3 = c3(w, 256)
            for jj in range(2):
                n = 2 * j + jj
                b = pair4()
                for hf in range(2):
                    ts = slice(hf * 512, (hf + 1) * 512)
                    run_deferred(2)
                    for c in range(kc):
                        MM(PS[b + hf][:, :], w3[:, c, jj * 128:(jj + 1) * 128], src[:, c, ts], c == 0, c == kc - 1,
                           [k, skey(c, hf)], [("ps", b + hf)])
                    DVE(lambda e, n=n, ts=ts, b=b, hf=hf: e.tensor_tensor(out=XT[:, n, ts], in0=XT[:, n, ts], in1=PS[b + hf][:, :],
                                                                        op=ALU.add), [kRX(n, hf), ("ps", b + hf)], [kRX(n, hf)])
                    if stats:
                        sq_inline(n, hf)
                    if gnext is not None:
                        ACT(HT[:, n, ts], XT[:, n, ts], AF.Copy, [kRX(n, hf), "CV"], [kRH(n, hf)],
                            scale=CV[:, gnext + n:gnext + n + 1])
            w_done()
        run_deferred(0)

    proj_add(RPv, kRP, 16, stats=True, gnext=C_GFFN)
    dump("x1", RX[:, :], rx_keys)

    rstd_from(6)
    PTB = c3(RP[:, 0:2048])
    for q in range(4):
        H1, hkey = (RPv, kRP) if q % 2 == 0 else (RQv, kRQ)
        if q == 3:
            pooldma(PTB, pT_d.rearrange("(c p) t -> p c t", p=128), [], [kRP(0, 0), kRP(0, 1), kRP(1, 0), kRP(1, 1)], "ptb")
        for j in range(8):
            w, k = w_next()
            w3 = c3(w, 256)
            for jj in range(2):
                n = 2 * j + jj
                b = pair4()
                for hf in range(2):
                    ts = slice(hf * 512, (hf + 1) * 512)
                    for c in range(16):
                        MM(PS[b + hf][:, :], w3[:, c, jj * 128:(jj + 1) * 128], HT[:, c, ts], c == 0, c == 15,
                           [k, kRH(c, hf)], [("ps", b + hf)])
                    pb = PS[b + hf][:, :]
                    ACT(pb, pb, AF.Relu, [("ps", b + hf)], [("ps", b + hf)])
                    DVE(lambda e, pb=pb, ts=ts: e.tensor_tensor(out=pb, in0=pb, in1=RSTD[:, ts], op=ALU.mult),
                        [("ps", b + hf), ("RSTD", hf)], [("ps", b + hf)])
                    ACT(H1[:, n, ts], pb, AF.Square, [("ps", b + hf)], [hkey(n, hf)])
            w_done()
        proj_add(H1, hkey, 16, stats=(q == 3), gnext=(C_GPLE if q == 3 else None))
    dump("x2", RX[:, :], rx_keys)

    rstd_from(6)
    wpp, kpp = w_next()
    WPP = RP[:, 4096:8192]
    kwpp = [kRP(c, h) for c in range(4, 8) for h in range(2)]
    P.op("act", lambda e: e.activation(out=WPP, in_=wpp, func=AF.Copy), [kpp], kwpp)
    w_done()
    wpp3 = c3(WPP, 2048)
    GT = [RP[:, 2048 + i * 1024:2048 + (i + 1) * 1024].bitcast(F32) for i in range(2)]
    for j in range(8):
        w, k = w_next()
        w3 = c3(w, 256)
        for jj in range(2):
            n = 2 * j + jj
            for hf in range(2):
                b = pair4()
                ts = slice(hf * 512, (hf + 1) * 512)
                run_deferred(2)
                for c in range(16):
                    MM(PS[b][:, :], w3[:, c, jj * 128:(jj + 1) * 128], HT[:, c, ts], c == 0, c == 15,
                       [k, kRH(c, hf)], [("ps", b)])
                for c in range(2):
                    MM(PS[b + 1][:, :], wpp3[:, c, n * 128:(n + 1) * 128], PTB[:, c, ts], c == 0, c == 1,
                       kwpp + [kRP(c, hf)], [("ps", b + 1)])
                gt = GT[hf]
                gk = kRP(2, hf)
                DVE(lambda e, gt=gt, b=b, ts=ts: e.tensor_tensor(out=gt, in0=PS[b][:, :], in1=RSTD[:, ts], op=ALU.mult),
                    [("ps", b), ("RSTD", hf)], [gk])
                ACT(gt, gt, AF.Sigmoid, [gk], [gk])
                DVE(lambda e, gt=gt, b=b: e.tensor_tensor(out=gt, in0=gt, in1=PS[b + 1][:, :], op=ALU.mult),
                    [gk, ("ps", b + 1)], [gk])
                DVE(lambda e, gt=gt, n=n, ts=ts: e.tensor_tensor(out=XT[:, n, ts], in0=XT[:, n, ts], in1=gt, op=ALU.add),
                    [gk, kRX(n, hf)], [kRX(n, hf)])
                sq_inline(n, hf)
        w_done()
    run_deferred(0)

    rstd_from(6)
    ov = out_d.rearrange("(c p) t -> p c t", p=128)
    TMPF = [RP[:, 8192 + i * 1024:8192 + (i + 1) * 1024].bitcast(F32) for i in range(2)]
    for g in range(8):
        for c in range(2 * g, 2 * g + 2):
            for h in range(2):
                sl = slice(h * 512, (h + 1) * 512)
                it = 2 * c + h
                if False:
                    ti = (it // 3) % 2
                    ACT(TMPF[ti], XT[:, c, sl], AF.Copy, [kRX(c, h), "CV"], [kRP(8, ti)], scale=CV[:, C_GFIN + c:C_GFIN + c + 1])
                    P.op("pool", lambda e, c=c, sl=sl, ti=ti: e.tensor_tensor(out=XT[:, c, sl], in0=TMPF[ti], in1=RSTD[:, sl],
                                                                         op=ALU.mult),
                         [kRP(8, ti), ("RSTD", h)], [kRX(c, h)])
                    continue
                DVE(lambda e, c=c, sl=sl: e.scalar_tensor_tensor(
                    out=XT[:, c, sl], in0=XT[:, c, sl], scalar=CV[:, C_GFIN + c:C_GFIN + c + 1], in1=RSTD[:, sl],
                    op0=ALU.mult, op1=ALU.mult), [kRX(c, h), ("RSTD", h), "CV"], [kRX(c, h)])
        t = spdma(ov[:, g * 2:(g + 1) * 2, :], XT[:, g * 2:(g + 1) * 2, :],
                  [kRX(c, h) for c in range(g * 2, g * 2 + 2) for h in range(2)], [], f"out{g}")
        final_toks.append(t)
    for t in final_toks:
        P.wait("sp", t)
    assert wstate["cur"] == NBLK and wstate["done"] == NBLK, wstate
    assert P.simulate(), "deadlock in plan"
    print("plan ops", {e: len(P.ops[e]) for e in ENGS}, "sems", {e: P.echan[e].val for e in P.echan})
    P.emit(block)
    es.close()
    return nc


def make_in_maps(inputs):
    f = lambda a: np.ascontiguousarray(np.asarray(a, dtype=np.float32))
    x = f(inputs["x"])
    p = f(inputs["p"])[0]
    ws = {
        "w_in": f(inputs["w_in"])[0], "w_a_out": f(inputs["w_a_out"])[0], "w_b_out": f(inputs["w_b_out"])[0],
        "w_o": f(inputs["w_o"])[0], "w_ff1": f(inputs["w_ff1"])[0], "w_ff2": f(inputs["w_ff2"])[0],
        "w_ple_gate": f(inputs["w_ple_gate"])[0], "w_ple_proj": f(inputs["w_ple_proj"])[0],
    }
    wb = pack_weights(ws)
    cv = np.zeros((128, NCV), np.float32)
    fm = lambda v: np.ascontiguousarray(v.reshape(-1, 128).T)
    cv[:, C_GMIX:C_GMIX + 16] = fm(f(inputs["norm_mix"])[0])
    cv[:, C_GFFN:C_GFFN + 16] = fm(f(inputs["norm_ffn"])[0])
    cv[:, C_GPLE:C_GPLE + 16] = fm(f(inputs["norm_ple"])[0])
    cv[:, C_GFIN:C_GFIN + 16] = fm(f(inputs["norm_final"]))
    lbl = f(inputs["lb_logits"])
    cv[:, C_L0:C_L0 + 8] = fm(lbl[0])
    cv[:, C_L1:C_L1 + 8] = fm(lbl[1])
    cv[:, C_HN] = f(inputs["hgrn_norm"])[0]
    cv[:, C_ID:C_ID + 128] = np.eye(128, dtype=np.float32)
    s_ = np.arange(128)[:, None]
    t_ = np.arange(128)[None, :]
    cv[:, C_MASK:C_MASK + 128] = ((s_ <= t_) & (s_ // 64 == t_ // 64)).astype(np.float32)
    lnrow = np.stack([f(inputs["gmlp_ln_g"])[0], f(inputs["gmlp_ln_b"])[0], f(inputs["b_spatial"])[0].reshape(-1)])
    wspT = np.ascontiguousarray(f(inputs["w_spatial"])[0].transpose(2, 0, 1).reshape(128, 1024))
    maps = []
    for core in range(8):
        b, half = core // 2, core % 2
        xs = x[b, half * T:(half + 1) * T, :]
        xp = x[b, 0:T, :] if half == 1 else np.zeros((T, D), np.float32)
        maps.append({
            "xT": np.ascontiguousarray(xs.T), "xpT": np.ascontiguousarray(xp.T),
            "pT": np.ascontiguousarray(p[b, half * T:(half + 1) * T, :].T),
            "wb": wb, "cvec": cv, "lnrow": np.ascontiguousarray(lnrow), "wspT": wspT,
        })
    return maps


_NC_CACHE = {}


def kernel(**inputs):
    maps = make_in_maps(inputs)
    if "nc" not in _NC_CACHE:
        _NC_CACHE["nc"] = build_program()
    res = run_bass_kernel_spmd(_NC_CACHE["nc"], maps, core_ids=list(range(8)))
    out = np.empty((4, 2 * T, D), np.float32)
    for core in range(8):
        b, half = core // 2, core % 2
        out[b, half * T:(half + 1) * T, :] = res.results[core]["outT"].T
    return out
```
